# Optimizing a Trainium2 kernel written in Bass

```python
import math
import jax, jax.numpy as jnp
from jax import lax
import numpy as np

D_MODEL = 1024
BATCH = 4
SEQ = 8192
DEPTH = 4

N_MIXERS = 2
N_A = (DEPTH + 1) // 2
N_B = DEPTH // 2
NORM_EPS = 1e-6
D_RNN = D_MODEL
RG_HEADS = 8
RG_BW = D_RNN // RG_HEADS
RG_CONV_W = 4
RG_C = 8.0
RG_RAD_MIN = 0.9
RG_RAD_MAX = 0.999
D_S5 = D_MODEL
S5_GC = 16
S5_G = D_S5 // S5_GC
S5_P = 64
S5_DT_MIN = 0.001
S5_DT_MAX = 0.1
D_FF = 3 * D_MODEL
FFN_CONV_W = 3

kernel_name = "hybrid_rglru_s5_convffn_trunk"


def _rmsnorm(x, g):
    x32 = x.astype(jnp.float32)
    var = jnp.mean(x32 * x32, axis=-1, keepdims=True)
    return (x32 * lax.rsqrt(var + NORM_EPS) * g.astype(jnp.float32)).astype(x.dtype)


def _causal_dwconv(x, w, b):
    k_w = w.shape[0]
    s = x.shape[1]
    xp = jnp.pad(x, ((0, 0), (k_w - 1, 0), (0, 0)))
    out = b
    for k in range(k_w):
        out = out + xp[:, k:k + s, :] * w[k]
    return out


def _real_scan_combine(e1, e2):
    a1, b1 = e1
    a2, b2 = e2
    return a1 * a2, a2 * b1 + b2


def _complex_scan_combine(e1, e2):
    a1r, a1i, b1r, b1i = e1
    a2r, a2i, b2r, b2i = e2
    ar = a2r * a1r - a2i * a1i
    ai = a2r * a1i + a2i * a1r
    br = a2r * b1r - a2i * b1i + b2r
    bi = a2r * b1i + a2i * b1r + b2i
    return ar, ai, br, bi


def _rglru_mixer(h, w_in, conv_w, conv_b, w_a, b_a, w_x, b_x, lam, w_out):
    bsz, s, _ = h.shape
    xg = h @ w_in
    xr, gate = xg[..., :D_RNN], xg[..., D_RNN:]
    xr = _causal_dwconv(xr, conv_w, conv_b)
    xh = xr.reshape(bsz, s, RG_HEADS, RG_BW)
    r = jax.nn.sigmoid(jnp.einsum('bshi,hij->bshj', xh, w_a) + b_a).reshape(bsz, s, D_RNN)
    ig = jax.nn.sigmoid(jnp.einsum('bshi,hij->bshj', xh, w_x) + b_x).reshape(bsz, s, D_RNN)
    log_a = -RG_C * r.astype(jnp.float32) * jax.nn.softplus(-lam.astype(jnp.float32))
    a = jnp.exp(log_a)
    mult = jnp.sqrt(-jnp.expm1(2.0 * log_a))
    bterm = mult * (ig * xr).astype(jnp.float32)
    _, hs = lax.associative_scan(_real_scan_combine, (a, bterm), axis=1)
    y = hs.astype(h.dtype) * jax.nn.gelu(gate)
    return y @ w_out


def _s5_mixer(h, w_in, a_re, a_im, log_dt, b_re, b_im, c_re, c_im, d, w_glu, w_out):
    bsz, s, _ = h.shape
    u = h @ w_in
    ug = u.reshape(bsz, s, S5_G, S5_GC).astype(jnp.float32)
    ar = a_re.astype(jnp.float32)
    ai = a_im.astype(jnp.float32)
    dt = jnp.exp(log_dt.astype(jnp.float32))[:, None]
    mag = jnp.exp(ar * dt)
    abr = mag * jnp.cos(ai * dt)
    abi = mag * jnp.sin(ai * dt)
    ur, ui = abr - 1.0, abi
    den = ar * ar + ai * ai
    wr = (ur * ar + ui * ai) / den
    wi = (ui * ar - ur * ai) / den
    br32, bi32 = b_re.astype(jnp.float32), b_im.astype(jnp.float32)
    bbr = wr[..., None] * br32 - wi[..., None] * bi32
    bbi = wr[..., None] * bi32 + wi[..., None] * br32
    bu_r = jnp.einsum('bsgc,gpc->bsgp', ug, bbr)
    bu_i = jnp.einsum('bsgc,gpc->bsgp', ug, bbi)
    a_r = jnp.broadcast_to(abr, (1, s, S5_G, S5_P))
    a_i = jnp.broadcast_to(abi, (1, s, S5_G, S5_P))
    _, _, hr, hi = lax.associative_scan(_complex_scan_combine, (a_r, a_i, bu_r, bu_i), axis=1)
    y = (jnp.einsum('bsgp,gcp->bsgc', hr, c_re.astype(jnp.float32))
         - jnp.einsum('bsgp,gcp->bsgc', hi, c_im.astype(jnp.float32)))
    y = y.reshape(bsz, s, D_S5).astype(h.dtype) + d * u
    g = jax.nn.gelu(y)
    gl = g @ w_glu
    out = gl[..., :D_S5] * jax.nn.sigmoid(gl[..., D_S5:])
    return out @ w_out


def _conv_ffn(h, w_up, conv_w, conv_b, w_down):
    up = _causal_dwconv(h @ w_up, conv_w, conv_b)
    return (jax.nn.gelu(up[..., :D_FF]) * up[..., D_FF:]) @ w_down


def setup_inputs(seed: int = 0) -> dict:
    key = jax.random.key(seed)
    ks = jax.random.split(key, 32)
    f32 = jnp.float32
    nrm = lambda k, shp, sc: jax.random.normal(k, shp, f32) * sc
    x = jax.random.normal(ks[0], (BATCH, SEQ, D_MODEL), f32)
    norm_mix_g = 1.0 + nrm(ks[1], (DEPTH, D_MODEL), 0.02)
    norm_ffn_g = 1.0 + nrm(ks[2], (DEPTH, D_MODEL), 0.02)
    norm_final_g = 1.0 + nrm(ks[3], (D_MODEL,), 0.02)
    rg_w_in = nrm(ks[4], (N_A, D_MODEL, 2 * D_RNN), D_MODEL ** -0.5)
    rg_conv_w = nrm(ks[5], (N_A, RG_CONV_W, D_RNN), RG_CONV_W ** -0.5)
    rg_conv_b = nrm(ks[6], (N_A, D_RNN), 0.01)
    rg_w_a = nrm(ks[7], (N_A, RG_HEADS, RG_BW, RG_BW), RG_BW ** -0.5)
    rg_b_a = nrm(ks[8], (N_A, RG_HEADS, RG_BW), 0.01)
    rg_w_x = nrm(ks[9], (N_A, RG_HEADS, RG_BW, RG_BW), RG_BW ** -0.5)
    rg_b_x = nrm(ks[10], (N_A, RG_HEADS, RG_BW), 0.01)
    a0 = jnp.sqrt(jax.random.uniform(ks[11], (N_A, D_RNN), f32,
                                     RG_RAD_MIN ** 2, RG_RAD_MAX ** 2))
    rg_lambda = jnp.log(a0) - jnp.log1p(-a0)
    rg_w_out = nrm(ks[12], (N_A, D_RNN, D_MODEL), D_RNN ** -0.5)
    s5_w_in = nrm(ks[13], (N_B, D_MODEL, D_S5), D_MODEL ** -0.5)
    s5_a_re = -0.5 + nrm(ks[14], (N_B, S5_G, S5_P), 0.01)
    s5_a_im = (math.pi * jnp.arange(S5_P, dtype=f32))[None, None, :] + nrm(ks[15], (N_B, S5_G, S5_P), 0.01)
    s5_log_dt = jax.random.uniform(ks[16], (N_B, S5_G), f32,
                                   math.log(S5_DT_MIN), math.log(S5_DT_MAX))
    s5_b_re = nrm(ks[17], (N_B, S5_G, S5_P, S5_GC), (2 * S5_GC) ** -0.5)
    s5_b_im = nrm(ks[18], (N_B, S5_G, S5_P, S5_GC), (2 * S5_GC) ** -0.5)
    s5_c_re = nrm(ks[19], (N_B, S5_G, S5_GC, S5_P), (0.5 * S5_P) ** -0.5)
    s5_c_im = nrm(ks[20], (N_B, S5_G, S5_GC, S5_P), (0.5 * S5_P) ** -0.5)
    s5_d = nrm(ks[21], (N_B, D_S5), 1.0)
    s5_w_glu = nrm(ks[22], (N_B, D_S5, 2 * D_S5), D_S5 ** -0.5)
    s5_w_out = nrm(ks[23], (N_B, D_S5, D_MODEL), D_S5 ** -0.5)
    ffn_w_up = nrm(ks[24], (DEPTH, D_MODEL, 2 * D_FF), D_MODEL ** -0.5)
    ffn_conv_w = nrm(ks[25], (DEPTH, FFN_CONV_W, 2 * D_FF), FFN_CONV_W ** -0.5)
    ffn_conv_b = nrm(ks[26], (DEPTH, 2 * D_FF), 0.01)
    ffn_w_down = nrm(ks[27], (DEPTH, D_FF, D_MODEL), D_FF ** -0.5)
    return {"x": x, "norm_mix_g": norm_mix_g, "norm_ffn_g": norm_ffn_g, "norm_final_g": norm_final_g,
            "rg_w_in": rg_w_in, "rg_conv_w": rg_conv_w, "rg_conv_b": rg_conv_b,
            "rg_w_a": rg_w_a, "rg_b_a": rg_b_a, "rg_w_x": rg_w_x, "rg_b_x": rg_b_x,
            "rg_lambda": rg_lambda, "rg_w_out": rg_w_out,
            "s5_w_in": s5_w_in, "s5_a_re": s5_a_re, "s5_a_im": s5_a_im, "s5_log_dt": s5_log_dt,
            "s5_b_re": s5_b_re, "s5_b_im": s5_b_im, "s5_c_re": s5_c_re, "s5_c_im": s5_c_im,
            "s5_d": s5_d, "s5_w_glu": s5_w_glu, "s5_w_out": s5_w_out,
            "ffn_w_up": ffn_w_up, "ffn_conv_w": ffn_conv_w, "ffn_conv_b": ffn_conv_b,
            "ffn_w_down": ffn_w_down}


def reference(x, norm_mix_g, norm_ffn_g, norm_final_g,
              rg_w_in, rg_conv_w, rg_conv_b, rg_w_a, rg_b_a, rg_w_x, rg_b_x, rg_lambda, rg_w_out,
              s5_w_in, s5_a_re, s5_a_im, s5_log_dt, s5_b_re, s5_b_im, s5_c_re, s5_c_im,
              s5_d, s5_w_glu, s5_w_out,
              ffn_w_up, ffn_conv_w, ffn_conv_b, ffn_w_down):
    h = x
    for i in range(DEPTH):
        hn = _rmsnorm(h, norm_mix_g[i])
        j = i // N_MIXERS
        if i % N_MIXERS == 0:
            mix = _rglru_mixer(hn, rg_w_in[j], rg_conv_w[j], rg_conv_b[j], rg_w_a[j], rg_b_a[j],
                               rg_w_x[j], rg_b_x[j], rg_lambda[j], rg_w_out[j])
        else:
            mix = _s5_mixer(hn, s5_w_in[j], s5_a_re[j], s5_a_im[j], s5_log_dt[j], s5_b_re[j],
                            s5_b_im[j], s5_c_re[j], s5_c_im[j], s5_d[j], s5_w_glu[j], s5_w_out[j])
        h = h + mix.astype(h.dtype)
        hn = _rmsnorm(h, norm_ffn_g[i])
        h = h + _conv_ffn(hn, ffn_w_up[i], ffn_conv_w[i], ffn_conv_b[i], ffn_w_down[i]).astype(h.dtype)
    return _rmsnorm(h, norm_final_g)
```

```python
import math
import numpy as np
import concourse.bass as bass
import concourse.mybir as mybir
from concourse.bass_utils import run_bass_kernel_spmd

F32 = mybir.dt.float32
BF16 = mybir.dt.bfloat16
AF = mybir.ActivationFunctionType
ALU = mybir.AluOpType

D = 1024
KC = 8
T = 512
SEQ = 8192
BATCH = 4
DEPTH = 4
DFF = 3072
NJ = 24
EPS = 1e-6
LSUB = 64
NQ = 32
ENGS = ("pe", "act", "dve", "pool", "sp")
SLAB = 8192
import os
S5_PIPE = os.environ.get('K_S5PIPE', '1') == '1'
FFN_LOOK = int(os.environ.get('K_FFNLOOK', '2'))
RG_PIPE = os.environ.get('K_RGPIPE', '1') == '1'
USE_WSC = True
NSLOT = 3


class Prog:
    def __init__(self):
        self.prog = {e: [] for e in ENGS}
        self.count = {e: 0 for e in ENGS}
        self.dcount = {}
        self.seen = {e: {} for e in ENGS}
        self.last_write = {}
        self.readers = {}
        self.n_ops = 0

    def _need(self, eng, waits, tok):
        if tok is None:
            return
        sk, v = tok
        if sk == eng and (eng in ("pe", "sp") or v > self.count[eng]):
            return
        if self.seen[eng].get(sk, 0) >= v:
            return
        if waits.get(sk, 0) < v:
            waits[sk] = v

    def op(self, eng, fn, reads=(), writes=(), track=True, dma=None, ninc=1):
        waits = {}
        for r in reads:
            self._need(eng, waits, self.last_write.get(r))
        for w in writes:
            self._need(eng, waits, self.last_write.get(w))
            for sk, v in self.readers.get(w, {}).items():
                self._need(eng, waits, (sk, v))
        for sk, v in waits.items():
            self.prog[eng].append(("wait", sk, v))
            self.seen[eng][sk] = v
        if dma is not None:
            prev = self.dcount.get(dma, 0)
            if prev > self.seen[eng].get(dma, 0) and (dma.startswith("cld") or dma.startswith("wcs")):
                self.prog[eng].append(("wait", dma, prev))
                self.seen[eng][dma] = prev
            self.dcount[dma] = self.dcount.get(dma, 0) + 16 * ninc
            tok = (dma, self.dcount[dma])
            self.prog[eng].append(("op", fn, dma, 16))
        elif track:
            self.count[eng] += 1
            tok = (eng, self.count[eng])
            self.prog[eng].append(("op", fn, eng, 1))
        else:
            tok = (eng, self.count[eng] + 1)
            self.prog[eng].append(("op", fn, None, 0))
        for w in writes:
            self.last_write[w] = tok
            self.readers[w] = {}
        for r in reads:
            d = self.readers.setdefault(r, {})
            if d.get(tok[0], 0) < tok[1]:
                d[tok[0]] = tok[1]
        self.n_ops += 1

    def barrier(self):
        snap = dict(self.count)
        dsnap = dict(self.dcount)
        for e in ENGS:
            for o, v in list(snap.items()) + list(dsnap.items()):
                if o.startswith("wcs"):
                    continue
                if o != e and v > 0 and self.seen[e].get(o, 0) < v:
                    self.prog[e].append(("wait", o, v))
                    self.seen[e][o] = v

    def final_wait(self, eng, keys):
        waits = {}
        for k in keys:
            self._need(eng, waits, self.last_write.get(k))
        for sk, v in waits.items():
            self.prog[eng].append(("wait", sk, v))
            self.seen[eng][sk] = v

    def replay(self, eng, handle, sems):
        for item in self.prog[eng]:
            if item[0] == "wait":
                handle.wait_ge(sems[item[1]], item[2])
            else:
                _, fn, sk, inc = item
                r = fn(handle)
                if sk is not None:
                    if isinstance(r, (list, tuple)):
                        for ins in r:
                            ins.then_inc(sems[sk], inc)
                    else:
                        r.then_inc(sems[sk], inc)


def build_program(n_tiles=SEQ // T, layers=(0, 1, 2, 3), seq=SEQ, mix="rsf", debug=0, s5_stage=4):
    nc = bass.Bass("TRN2", target_bir_lowering=False)
    P = Prog()

    def dram(name, shape, kind="ExternalInput", dt=F32):
        return nc.dram_tensor(name, list(shape), dt, kind=kind).ap()

    x_d = dram("x", [seq, D])
    out_d = dram("out", [seq, D], kind="ExternalOutput")
    ident_d = dram("ident", [128, 128])
    tau_d = dram("tau", [128, T])
    s5tab_d = [dram(f"s5tab{j}", [NQ, 128, 2 * T], kind="Internal") for j in range(2)]
    W = {}
    for name, shape in [
        ("norm_mix_g", [4, D]), ("norm_ffn_g", [4, D]), ("norm_final_g", [1, D]),
        ("rg_w_in", [2, D, 2 * D]), ("rg_conv_w", [2, 4, D]), ("rg_conv_b", [2, D]),
        ("rg_w_a", [2, 8, 128, 128]), ("rg_b_a", [2, D]), ("rg_w_x", [2, 8, 128, 128]),
        ("rg_b_x", [2, D]), ("rg_lambda", [2, D]), ("rg_w_out", [2, D, D]),
        ("s5_w_in", [2, D, D]), ("s5_a_re", [2, 4096]), ("s5_a_im", [2, 4096]),
        ("s5_log_dt", [2, 64]), ("s5_b_re", [2, 4096, 16]), ("s5_b_im", [2, 4096, 16]),
        ("s5_c_re", [2, 1024, 64]), ("s5_c_im", [2, 1024, 64]), ("s5_d", [2, D]),
        ("s5_w_glu", [2, D, 2 * D]), ("s5_w_out", [2, D, D]),
        ("ffn_w_up", [4, D, 2 * DFF]), ("ffn_conv_w", [4, 3, 2 * DFF]),
        ("ffn_conv_b", [4, 2 * DFF]), ("ffn_w_down", [4, DFF, D]),
    ]:
        W[name] = dram(name, shape)
    s5bc_d = [dram(f"s5bc{j}", [4, 128, NQ * 128], kind="Internal", dt=BF16) for j in range(2)]

    dbg_d = dram("dbg", [max(debug, 1), 128, T], kind="ExternalOutput") if debug else None
    dbg_n = [0]
    cur = [0]
    dbg_names = []

    import contextlib
    es = contextlib.ExitStack()
    with es:
        def sb(name, shape, dt=F32):
            return es.enter_context(nc.sbuf_tensor(name, list(shape), dt))

        big16 = sb("big16", [128, 4096])
        xin = big16[:, :].rearrange("p (j d) -> p j d", j=4)
        u32 = big16[:, :].rearrange("p (k t) -> p k t", k=KC)
        h = sb("h", [128, KC, T])
        sq = sb("sq", [128, KC, T], BF16)
        hn = sb("hn", [128, KC, T], BF16)
        rstd = sb("rstd", [128, T])
        ring = sb("ring", [128, NSLOT, SLAB], BF16)
        act = sb("act", [128, NJ, T], BF16)
        NTMP = 16
        TW = T + 4
        tmp = sb("tmp", [128, NTMP * TW])
        xcb = sb("xcb", [128, 2, T], BF16)
        hrb = sb("hrb", [128, 2, T], BF16)
        hib = sb("hib", [128, 2, T], BF16)
        ident = sb("ident_sb", [128, 128])
        identb = sb("identb", [128, 128], BF16)
        onesb = sb("onesb", [128, 128], BF16)
        tau = sb("tau_sb", [128, T])
        tabr = sb("tabr", [128, 3, 2, T])
        NCONST = 1152
        cst = sb("cst", [128, NCONST])
        stage = tmp[:, 0:1152].rearrange("p (b n) -> p b n", b=9)
        rgc = sb("rgc", [128, 2, 2, KC])
        halo_rg = sb("halo_rg", [128, 2, KC, 3])
        carry_rg = sb("carry_rg", [128, 2, KC])
        halo_ffn = sb("halo_ffn", [128, 4, 2 * NJ, 2], BF16)
        NUPB = 6
        upb = sb("upb", [128, NUPB, T + 2], BF16)
        dgs = sb("dgs", [128, NUPB, 3, 128], BF16)
        upb_i = [0]
        s5k = sb("s5k", [128, 2, 16, NQ])
        s5carry = sb("s5carry", [128, 2, 2, NQ])
        s5nat = tmp[:, 1152:1152 + 2048].rearrange("p (a q c) -> p a q c", a=4, q=NQ)
        s5cn = tmp[:, 3200:3200 + 1024].rearrange("p (a k n) -> p a k n", a=2, k=KC)
        s5full = sb("s5full", [128, 2, 128])
        s5bcs = sb("s5bcs", [128, 4, 128], BF16)
        masks = sb("masks", [128, 20])
        onesf = sb("onesf", [128, 128])
        ldrow = sb("ldrow", [1, 128])

        ps = [es.enter_context(nc.psum_tensor(f"ps{i}", [128, T], F32)) for i in range(8)]
        ps_i = [0]
        ps_mod = [8]

        def psn():
            i = ps_i[0] % ps_mod[0]
            ps_i[0] = (i + 1) % ps_mod[0]
            return i

        NCLD = 8
        sem_names = list(ENGS) + [f"w{s}" for s in range(NSLOT)] + [f"cld{i}" for i in range(NCLD)] + ["xld", "ost", "scr", "tb0", "tb1", "tb2", "tbw0", "tbw1"]
        cld_i = [0]
        NWCS = 8
        wcs_i = [0]
        sem_names += [f"wcs{i_}" for i_ in range(NWCS)]
        sems = {n: es.enter_context(nc.semaphore(n)) for n in sem_names}

        def mm(pi, lhsT, rhs, start, stop, reads, cols=slice(0, T)):
            P.op("pe", lambda e: e.matmul(ps[pi][:, cols], lhsT=lhsT, rhs=rhs, start=start, stop=stop),
                 reads=reads, writes=[("ps", pi)], track=stop)

        def tr(pi, cols, in_, reads, last=True):
            P.op("pe", lambda e: e.transpose(ps[pi][:, cols], in_, ident[:, :]),
                 reads=list(reads) + ["ident"], writes=[("ps", pi)], track=last)

        def actf(out, in_, func, reads, writes, bias=None, scale=None):
            kw = {}
            if bias is not None:
                kw["bias"] = bias
            if scale is not None:
                kw["scale"] = scale
            P.op("act", lambda e: e.activation(out=out, in_=in_, func=func, **kw), reads=reads, writes=writes)

        def tt(eng, out, in0, in1, op, reads, writes):
            P.op(eng, lambda e: e.tensor_tensor(out=out, in0=in0, in1=in1, op=op), reads=reads, writes=writes)

        def ts(eng, out, in0, s1, s2, op0, op1, reads, writes):
            if op1 is None:
                P.op(eng, lambda e: e.tensor_scalar(out=out, in0=in0, scalar1=s1, scalar2=None, op0=op0),
                     reads=reads, writes=writes)
            else:
                P.op(eng, lambda e: e.tensor_scalar(out=out, in0=in0, scalar1=s1, scalar2=s2, op0=op0, op1=op1),
                     reads=reads, writes=writes)

        def stt(out, in0, scalar, in1, op0, op1, reads, writes):
            P.op("dve", lambda e: e.scalar_tensor_tensor(out=out, in0=in0, scalar=scalar, in1=in1, op0=op0, op1=op1),
                 reads=reads, writes=writes)

        def cp(eng, out, in_, reads, writes):
            if eng == "act":
                P.op("act", lambda e: e.copy(out=out, in_=in_), reads=reads, writes=writes)
            else:
                P.op(eng, lambda e: e.tensor_copy(out=out, in_=in_), reads=reads, writes=writes)

        def memset(eng, ap, val, writes):
            P.op(eng, lambda e: e.memset(ap, val), writes=writes)

        def dma(eng, out, in_, sem, reads, writes, slow=False):
            if sem == "cld":
                sem = f"cld{cld_i[0]}"
                cld_i[0] = (cld_i[0] + 1) % NCLD
            if sem == "wcs":
                sem = f"wcs{wcs_i[0]}"
                wcs_i[0] = (wcs_i[0] + 1) % NWCS
            kw = {"allow_slow_non_contiguous": True} if slow else {}
            P.op(eng, lambda e: e.dma_start(out=out, in_=in_, **kw), reads=reads, writes=writes, dma=sem)

        def dbg(name, ap, key):
            if not debug or dbg_n[0] >= debug or cur[0] != n_tiles - 1:
                return
            i = dbg_n[0]
            dbg_n[0] += 1
            dbg_names.append(name)
            dma("sp", dbg_d[i], ap, "scr", [key], [("dbg", i)])

        tmp_i = [0]

        def tslot():
            i = tmp_i[0]
            tmp_i[0] = (i + 1) % NTMP
            return i

        def TM(i, c0=0, n=T):
            return tmp[:, i * TW + c0:i * TW + c0 + n]

        slab_list = []
        slab_state = {"issued": 0, "used": 0}
        PF = NSLOT - 1

        wsc_box = [None]

        def issue_slab(gidx):
            idx = gidx % len(slab_list)
            slot = gidx % NSLOT
            sd = slab_list[idx]
            if sd.direct or not USE_WSC:
                pairs, q = sd(ring[:, slot, :])
                if USE_WSC:
                    q = "sp"

                def fn(e, pairs=pairs):
                    return [e.dma_start(out=o, in_=i) for (o, i) in pairs]
                P.op(q, fn, reads=[r for r in sd.reads], writes=[("ring", slot)],
                     dma=f"w{slot}", ninc=len(pairs))
            else:
                n = sd.size
                src = wsc_box[0][idx, :, 0:n]
                P.op("sp", lambda e, slot=slot, n=n, src=src: e.dma_start(out=ring[:, slot, 0:n], in_=src),
                     reads=[("wsc", idx)], writes=[("ring", slot)], dma=f"w{slot}", ninc=1)

        def cast_all_slabs():
            wsc_box[0] = dram("wsc", [len(slab_list), 128, SLAB], kind="Internal", dt=BF16)
            for idx, sd in enumerate(slab_list):
                if sd.direct:
                    continue
                pairs, _ = sd(wsc_box[0][idx])
                for (o, i) in pairs:
                    dma("pool", o, i, "wcs", [], [("wsc", idx)])

        released = set()

        def pump():
            total = len(slab_list) * n_tiles
            while slab_state["issued"] < min(slab_state["used"] + PF + 1, total):
                n = slab_state["issued"]
                if n >= NSLOT and (n - NSLOT) not in released:
                    break
                issue_slab(n)
                slab_state["issued"] += 1

        def next_slab():
            g = slab_state["used"]
            slab_state["used"] += 1
            pump()
            assert slab_state["issued"] > g, "slab ring deadlock: too many live slabs"
            return g % NSLOT, g

        def release(g):
            released.add(g)
            pump()

        class SlabDef:
            def __init__(self, fn, reads=(), direct=False, size=SLAB):
                self.fn = fn
                self.reads = reads
                self.direct = direct
                self.size = size

            def __call__(self, dst2d):
                return self.fn(dst2d)

        def slab_cols(wname, l, kc, col_sets, q="pool"):
            ntot = sum(n for _, n in col_sets)

            def fn(dst2d):
                dst = dst2d[:, 0:kc * ntot].rearrange("p (k n) -> p k n", k=kc)
                src = W[wname][l].rearrange("(k p) n -> p k n", p=128)
                pairs = []
                o = 0
                for c0, n in col_sets:
                    pairs.append((dst[:, :, o:o + n], src[:, :, c0:c0 + n]))
                    o += n
                return pairs, q
            return SlabDef(fn, size=kc * ntot)

        def ring_view(slot, kc, n):
            return ring[:, slot, 0:kc * n].rearrange("p (k n) -> p k n", k=kc)

        crow = {}
        vec_list = []

        def addvec(name, ap2d):
            crow[name] = sum(v[1] for v in vec_list)
            vec_list.append((name, ap2d.shape[0], ap2d))

        addvec("g_mix", W["norm_mix_g"].rearrange("l (k p) -> (l k) p", p=128))
        addvec("g_ffn", W["norm_ffn_g"].rearrange("l (k p) -> (l k) p", p=128))
        addvec("g_fin", W["norm_final_g"].rearrange("l (k p) -> (l k) p", p=128))
        addvec("rg_cw", W["rg_conv_w"].rearrange("l t (k p) -> (l t k) p", p=128))
        addvec("rg_cb", W["rg_conv_b"].rearrange("l (k p) -> (l k) p", p=128))
        addvec("rg_ba", W["rg_b_a"].rearrange("l (k p) -> (l k) p", p=128))
        addvec("rg_bx", W["rg_b_x"].rearrange("l (k p) -> (l k) p", p=128))
        addvec("rg_lam", W["rg_lambda"].rearrange("l (k p) -> (l k) p", p=128))
        addvec("s5_d", W["s5_d"].rearrange("l (k p) -> (l k) p", p=128))
        addvec("ffn_cw", W["ffn_conv_w"].rearrange("l t (k p) -> (l t k) p", p=128))
        addvec("ffn_cb", W["ffn_conv_b"].rearrange("l (k p) -> (l k) p", p=128))
        addvec("s5_are", W["s5_a_re"].rearrange("l (q p) -> (l q) p", p=128))
        addvec("s5_aim", W["s5_a_im"].rearrange("l (q p) -> (l q) p", p=128))
        nrows = sum(v[1] for v in vec_list)
        assert nrows <= NCONST, nrows

        def C1(name, idx):
            c = crow[name] + idx
            return cst[:, c:c + 1]

        dma("sp", ident[:, :], ident_d[:, :], "cld", [], ["ident"])
        dma("sp", tau[:, :], tau_d[:, :], "cld", [], ["tau"])
        cp("dve", identb[:, :], ident[:, :], ["ident"], ["identb"])
        memset("dve", onesb[:, :], 1.0, ["onesb"])
        memset("dve", halo_rg[:, :, :, :], 0.0, [("halo_rg", l_, c_) for l_ in range(2) for c_ in range(KC)])
        memset("dve", carry_rg[:, :, :], 0.0, [("carry_rg", l_, c_) for l_ in range(2) for c_ in range(KC)])
        memset("dve", halo_ffn[:, :, :, :], 0.0, [("halo_ffn", i_, c_) for i_ in range(4) for c_ in range(2 * NJ)])
        memset("dve", s5carry[:, :, :, :], 0.0, ["s5carry"])
        nblk = (nrows + 127) // 128
        memset("dve", stage[:, :, :], 0.0, [("stage", b) for b in range(nblk)])
        r0 = 0
        for name, n, ap2d in vec_list:
            done = 0
            while done < n:
                blk, off = divmod(r0 + done, 128)
                take = min(n - done, 128 - off)
                dma("sp", stage[off:off + take, blk, :], ap2d[done:done + take, :], "cld", [], [("stage", blk)])
                done += take
            r0 += n
        for blk in range(nblk):
            pi = psn()
            tr(pi, slice(0, 128), stage[:, blk, :], [("stage", blk)])
            cp("dve", cst[:, blk * 128:(blk + 1) * 128], ps[pi][:, 0:128], [("ps", pi)], ["cst"])
        for l in range(2):
            lam = cst[:, crow["rg_lam"] + l * KC: crow["rg_lam"] + (l + 1) * KC]
            actf(rgc[:, l, 0, :], lam, AF.Exp, ["cst"], ["rgc"], scale=-1.0)
            actf(rgc[:, l, 0, :], rgc[:, l, 0, :], AF.Ln, ["rgc"], ["rgc"], bias=1.0)
            ts("dve", rgc[:, l, 1, :], rgc[:, l, 0, :], -16.0, None, ALU.mult, None, ["rgc"], ["rgc"])
            ts("dve", rgc[:, l, 0, :], rgc[:, l, 0, :], -8.0, None, ALU.mult, None, ["rgc"], ["rgc"])

        def phase_load(t):
            dma("sp", xin, x_d[t * T:(t + 1) * T, :].rearrange("(j p) d -> p j d", p=128), "xld",
                [], [("big16", i) for i in range(8)])
            for k in range(KC):
                pi = psn()
                for j in range(4):
                    tr(pi, slice(128 * j, 128 * j + 128), xin[:, j, 128 * k:128 * k + 128],
                       [("big16", 2 * j), ("big16", 2 * j + 1)], last=(j == 3))
                cp("act" if k % 2 else "dve", h[:, k, :], ps[pi][:, :], [("ps", pi)], [("h", k)])

        def phase_norm(gname, gidx0, out_f32=None):
            for k in range(KC):
                actf(sq[:, k, :], h[:, k, :], AF.Square, [("h", k)], [("sq", k)])
            pi = psn()
            for k in range(KC):
                mm(pi, onesb[:, :], sq[:, k, :], k == 0, k == KC - 1, ["onesb", ("sq", k)])
            actf(rstd[:, :], ps[pi][:, :], AF.Sqrt, [("ps", pi), "eps"], ["rstd"], bias=EPS_AP[:, :], scale=1.0 / D)
            P.op("dve", lambda e: e.reciprocal(out=rstd[:, :], in_=rstd[:, :]), reads=["rstd"], writes=["rstd"])
            for k in range(KC):
                if out_f32 is None:
                    stt(hn[:, k, :], h[:, k, :], C1(gname, gidx0 + k), rstd[:, :], ALU.mult, ALU.mult,
                        [("h", k), "rstd", "cst"], [("hn", k)])
                else:
                    o, key = out_f32(k)
                    stt(o, h[:, k, :], C1(gname, gidx0 + k), rstd[:, :], ALU.mult, ALU.mult,
                        [("h", k), "rstd", "cst"], [key])

        def phase_ffn(i):
            cw0 = crow["ffn_cw"] + i * 3 * 48
            cb0 = crow["ffn_cb"] + i * 48
            slabs = {}
            st = {}

            def get_slab(si):
                if si not in slabs:
                    slabs[si] = next_slab()
                return slabs[si]

            def stage_a(hx):
                jg, half = divmod(hx, 2)
                si, cc = divmod(jg, 4)
                slot, gsl = get_slab(si)
                wv = ring_view(slot, KC, 1024)
                ch = jg + NJ * half
                pi = psn()
                for k in range(KC):
                    mm(pi, wv[:, k, 512 * half + 128 * cc: 512 * half + 128 * cc + 128], hn[:, k, :],
                       k == 0, k == KC - 1, [("ring", slot), ("hn", k)])
                if hx % 8 == 7:
                    release(gsl)
                ub_i = upb_i[0]
                upb_i[0] = (ub_i + 1) % NUPB
                cp("pool", upb[:, ub_i, 0:2], halo_ffn[:, i, ch, :], [("halo_ffn", i, ch)], [("upbh", ub_i)])
                cp("act", upb[:, ub_i, 2:2 + T], ps[pi][:, :], [("ps", pi)], [("upb", ub_i)])
                cp("pool", halo_ffn[:, i, ch, :], upb[:, ub_i, T:T + 2], [("upb", ub_i)], [("halo_ffn", i, ch)])
                for t3 in range(3):
                    actf(dgs[:, ub_i, t3, :], identb[:, :], AF.Copy, ["identb", "cst"], [("dgs", ub_i, t3)],
                         scale=cst[:, cw0 + 48 * t3 + ch:cw0 + 48 * t3 + ch + 1])
                st[hx] = ub_i

            def stage_b(hx):
                jg, half = divmod(hx, 2)
                ub_i = st.pop(hx)
                pc = psn()
                for t3 in range(3):
                    mm(pc, dgs[:, ub_i, t3, :], upb[:, ub_i, t3:t3 + T], t3 == 0, t3 == 2,
                       [("dgs", ub_i, t3), ("upb", ub_i), ("upbh", ub_i)])
                if half == 0:
                    g = tslot()
                    actf(TM(g), ps[pc][:, :], AF.Gelu_apprx_tanh, [("ps", pc), "cst"], [("tmp", g)],
                         bias=cst[:, cb0 + jg:cb0 + jg + 1])
                    st[("g", jg)] = g
                else:
                    g = st.pop(("g", jg))
                    stt(act[:, jg, :], ps[pc][:, :], cst[:, cb0 + NJ + jg:cb0 + NJ + jg + 1], TM(g), ALU.add, ALU.mult,
                        [("ps", pc), ("tmp", g), "cst"], [("act", jg)])

            NH = 2 * NJ
            LOOK = FFN_LOOK
            for hx in range(min(LOOK, NH)):
                stage_a(hx)
            for hx in range(NH):
                if hx + LOOK < NH:
                    stage_a(hx + LOOK)
                stage_b(hx)
            for s in range(4):
                slot, gsl = next_slab()
                wv = ring_view(slot, NJ, 256)
                for mh in range(2):
                    m = 2 * s + mh
                    pi = psn()
                    for j in range(NJ):
                        mm(pi, wv[:, j, 128 * mh:128 * mh + 128], act[:, j, :], j == 0, j == NJ - 1,
                           [("ring", slot), ("act", j)])
                    tt("dve", h[:, m, :], h[:, m, :], ps[pi][:, :], ALU.add, [("h", m), ("ps", pi)], [("h", m)])
                release(gsl)

        def ffn_slabs(i):
            for s in range(6):
                slab_list.append(slab_cols("ffn_w_up", i, KC, [(512 * s, 512), (DFF + 512 * s, 512)]))
            for s in range(4):
                slab_list.append(slab_cols("ffn_w_down", i, NJ, [(256 * s, 256)]))

        def phase_rg(l):
            slot_g, g_g = next_slab()
            wg = ring_view(slot_g, KC, 1024)
            for c in range(KC):
                pg = psn()
                for k in range(KC):
                    mm(pg, wg[:, k, 128 * c:128 * c + 128], hn[:, k, :], k == 0, k == KC - 1, [("ring", slot_g), ("hn", k)])
                actf(act[:, 8 + c, :], ps[pg][:, :], AF.Gelu_apprx_tanh, [("ps", pg)], [("act", 8 + c)])
            release(g_g)
            slot_x, g_x = next_slab()
            wx = ring_view(slot_x, KC, 1024)
            slot_a, g_a = next_slab()
            wa = ring_view(slot_a, 16, 128)
            ust = {}

            def rg_a(c):
                pi = psn()
                for k in range(KC):
                    mm(pi, wx[:, k, 128 * c:128 * c + 128], hn[:, k, :], k == 0, k == KC - 1, [("ring", slot_x), ("hn", k)])
                u = tslot()
                cp("pool", TM(u, 0, 3), halo_rg[:, l, c, :], [("halo_rg", l, c)], [("tmp", u)])
                cp("act", TM(u, 3, T), ps[pi][:, :], [("ps", pi)], [("tmp", u)])
                cp("pool", halo_rg[:, l, c, :], TM(u, T, 3), [("tmp", u)], [("halo_rg", l, c)])
                ust[c] = u

            def rg_b(c):
                u = ust.pop(c)
                xc = tslot()
                ts("dve", TM(xc), TM(u, 0, T), C1("rg_cw", l * 32 + 0 * KC + c), C1("rg_cb", l * KC + c),
                   ALU.mult, ALU.add, [("tmp", u), "cst"], [("tmp", xc)])
                for k in range(1, 4):
                    stt(TM(xc), TM(u, k, T), C1("rg_cw", l * 32 + k * KC + c), TM(xc), ALU.mult, ALU.add,
                        [("tmp", u), ("tmp", xc), "cst"], [("tmp", xc)])
                xb = c % 2
                xbv = xcb[:, xb, :]
                cp("pool", xbv, TM(xc), [("tmp", xc)], [("xcb", xb)])
                pa = psn()
                mm(pa, wa[:, c, :], xbv, True, True, [("ring", slot_a), ("xcb", xb)])
                px = psn()
                mm(px, wa[:, 8 + c, :], xbv, True, True, [("ring", slot_a), ("xcb", xb)])
                r = tslot()
                actf(TM(r), ps[pa][:, :], AF.Sigmoid, [("ps", pa), "cst"], [("tmp", r)], bias=C1("rg_ba", l * KC + c))
                ig = tslot()
                actf(TM(ig), ps[px][:, :], AF.Sigmoid, [("ps", px), "cst"], [("tmp", ig)], bias=C1("rg_bx", l * KC + c))
                a = tslot()
                actf(TM(a), TM(r), AF.Exp, [("tmp", r), "rgc"], [("tmp", a)], scale=rgc[:, l, 0, c:c + 1])
                actf(TM(r), TM(r), AF.Exp, [("tmp", r), "rgc"], [("tmp", r)], scale=rgc[:, l, 1, c:c + 1])
                actf(TM(r), TM(r), AF.Sqrt, [("tmp", r), "one"], [("tmp", r)], bias=ONE_AP[:, :], scale=-1.0)
                tt("pool", TM(ig), TM(ig), TM(xc), ALU.mult, [("tmp", ig), ("tmp", xc)], [("tmp", ig)])
                tt("dve", TM(ig), TM(ig), TM(r), ALU.mult, [("tmp", ig), ("tmp", r)], [("tmp", ig)])
                hs = tslot()
                P.op("dve", lambda e, hs=hs, a=a, ig=ig, c=c: e.tensor_tensor_scan(
                    out=TM(hs), data0=TM(a), data1=TM(ig), initial=carry_rg[:, l, c:c + 1],
                    op0=ALU.mult, op1=ALU.add),
                    reads=[("tmp", a), ("tmp", ig), ("carry_rg", l, c)], writes=[("tmp", hs)])
                cp("dve", carry_rg[:, l, c:c + 1], TM(hs, T - 1, 1), [("tmp", hs)], [("carry_rg", l, c)])
                tt("dve", act[:, c, :], TM(hs), act[:, 8 + c, :], ALU.mult, [("tmp", hs), ("act", 8 + c)], [("act", c)])

            if RG_PIPE:
                rg_a(0)
                for c in range(KC):
                    if c + 1 < KC:
                        rg_a(c + 1)
                    rg_b(c)
            else:
                for c in range(KC):
                    rg_a(c)
                    rg_b(c)
            release(g_x)
            release(g_a)
            slot_o, g_o = next_slab()
            wo = ring_view(slot_o, KC, 1024)
            for m in range(KC):
                pi = psn()
                for c in range(KC):
                    mm(pi, wo[:, c, 128 * m:128 * m + 128], act[:, c, :], c == 0, c == KC - 1, [("ring", slot_o), ("act", c)])
                tt("dve", h[:, m, :], h[:, m, :], ps[pi][:, :], ALU.add, [("h", m), ("ps", pi)], [("h", m)])
            release(g_o)

        def rg_slabs(l):
            slab_list.append(slab_cols("rg_w_in", l, KC, [(1024, 1024)]))
            slab_list.append(slab_cols("rg_w_in", l, KC, [(0, 1024)]))

            def fn(dst2d, l=l):
                dst = dst2d[:, 0:2048].rearrange("p (k n) -> p k n", k=16)
                return [(dst[:, 0:8, :], W["rg_w_a"][l].rearrange("h i j -> i h j")),
                        (dst[:, 8:16, :], W["rg_w_x"][l].rearrange("h i j -> i h j"))], "pool"
            slab_list.append(SlabDef(fn, size=2048))
            slab_list.append(slab_cols("rg_w_out", l, KC, [(0, 1024)]))


        I32 = mybir.dt.int32
        TWO_PI = 2.0 * math.pi

        def K5(j, idx, q0=0, n=NQ):
            return s5k[:, j, idx, q0:q0 + n]

        def range_reduce(eng_out_ap, z_ap, zi_ap, kf_ap, width_keys):
            C1_ = 6.28125
            C2_ = TWO_PI - C1_
            PI_LO = 3.1415925
            ts("dve", zi_ap, z_ap, 1.0 / TWO_PI, None, ALU.mult, None, width_keys, width_keys)
            cp("dve", kf_ap, zi_ap, width_keys, width_keys)
            stt(eng_out_ap, kf_ap, -C1_, z_ap, ALU.mult, ALU.add, width_keys, width_keys)
            stt(eng_out_ap, kf_ap, -C2_, eng_out_ap, ALU.mult, ALU.add, width_keys, width_keys)
            ts("dve", eng_out_ap, eng_out_ap, -PI_LO, PI_LO, ALU.max, ALU.min, width_keys, width_keys)

        def s5_prologue(j):
            PK = ["s5p"]
            cp("dve", K5(j, 0), cst[:, crow["s5_are"] + NQ * j:crow["s5_are"] + NQ * (j + 1)], ["cst"], PK)
            cp("dve", K5(j, 1), cst[:, crow["s5_aim"] + NQ * j:crow["s5_aim"] + NQ * (j + 1)], ["cst"], PK)
            dma("sp", ldrow[0:1, 0:64], W["s5_log_dt"][j:j + 1, :], "cld", [], PK)
            for q0 in range(0, NQ, 4):
                dma("sp", s5nat[:, 0, q0:q0 + 4, :], W["s5_b_re"][j].rearrange("(q p) c -> p q c", p=128)[:, q0:q0 + 4, :], "cld", [], PK)
                dma("sp", s5nat[:, 1, q0:q0 + 4, :], W["s5_b_im"][j].rearrange("(q p) c -> p q c", p=128)[:, q0:q0 + 4, :], "cld", [], PK)
            dma("sp", s5cn[:, 0, :, :], W["s5_c_re"][j].rearrange("(k r) n -> r k n", r=128), "cld", [], PK)
            dma("sp", s5cn[:, 1, :, :], W["s5_c_im"][j].rearrange("(k r) n -> r k n", r=128), "cld", [], PK)
            pi = psn()
            P.op("pe", lambda e: e.matmul(ps[pi][:, 0:64], lhsT=onesf[0:1, :], rhs=ldrow[0:1, 0:64], start=True, stop=True),
                 reads=PK + ["onesf"], writes=[("ps", pi)])
            pv = ps[pi][:, 0:64].rearrange("p (q g) -> p q g", g=2)
            cp("dve", s5k[0:64, j, 2, :], pv[0:64, :, 0], [("ps", pi)], PK)
            cp("dve", s5k[64:128, j, 2, :], pv[64:128, :, 1], [("ps", pi)], PK)
            actf(K5(j, 2), K5(j, 2), AF.Exp, PK, PK)
            tt("dve", K5(j, 12), K5(j, 0), K5(j, 2), ALU.mult, PK, PK)
            actf(K5(j, 3), K5(j, 12), AF.Exp, PK, PK)
            tt("dve", K5(j, 4), K5(j, 1), K5(j, 2), ALU.mult, PK, PK)
            zi = s5k[:, j, 13, :].bitcast(I32)
            range_reduce(K5(j, 12), K5(j, 4), zi, K5(j, 14), PK)
            actf(K5(j, 6), K5(j, 12), AF.Sin, PK, PK)
            ts("dve", K5(j, 15), K5(j, 4), math.pi / 2, None, ALU.add, None, PK, PK)
            range_reduce(K5(j, 12), K5(j, 15), zi, K5(j, 14), PK)
            actf(K5(j, 5), K5(j, 12), AF.Sin, PK, PK)
            tt("dve", K5(j, 5), K5(j, 5), K5(j, 3), ALU.mult, PK, PK)
            tt("dve", K5(j, 6), K5(j, 6), K5(j, 3), ALU.mult, PK, PK)
            tt("dve", K5(j, 12), K5(j, 0), K5(j, 0), ALU.mult, PK, PK)
            tt("dve", K5(j, 13), K5(j, 1), K5(j, 1), ALU.mult, PK, PK)
            tt("dve", K5(j, 9), K5(j, 12), K5(j, 13), ALU.add, PK, PK)
            P.op("dve", lambda e: e.reciprocal(out=K5(j, 9), in_=K5(j, 9)), reads=PK, writes=PK)
            ts("dve", K5(j, 10), K5(j, 5), -1.0, None, ALU.add, None, PK, PK)
            tt("dve", K5(j, 12), K5(j, 10), K5(j, 0), ALU.mult, PK, PK)
            tt("dve", K5(j, 13), K5(j, 6), K5(j, 1), ALU.mult, PK, PK)
            tt("dve", K5(j, 12), K5(j, 12), K5(j, 13), ALU.add, PK, PK)
            tt("dve", K5(j, 7), K5(j, 12), K5(j, 9), ALU.mult, PK, PK)
            tt("dve", K5(j, 12), K5(j, 6), K5(j, 0), ALU.mult, PK, PK)
            tt("dve", K5(j, 13), K5(j, 10), K5(j, 1), ALU.mult, PK, PK)
            tt("dve", K5(j, 12), K5(j, 12), K5(j, 13), ALU.subtract, PK, PK)
            tt("dve", K5(j, 8), K5(j, 12), K5(j, 9), ALU.mult, PK, PK)
            ts("dve", K5(j, 10), K5(j, 8), -1.0, None, ALU.mult, None, PK, PK)
            hflat = h[:, :, :].rearrange("p k t -> p (k t)")
            za = big16[:, 0:T]
            zb = big16[:, T:2 * T]
            zc = hflat[:, 0:T]
            zd = hflat[:, T:2 * T]
            ZK = ["zq"]
            for q in range(NQ):
                b_ = q % 2
                stg = tmp[:, 4256 + b_ * 2 * T:4256 + (b_ + 1) * 2 * T].rearrange("p (a t) -> p a t", a=2)
                SK = [("tabstg", b_)]
                ts("dve", za, tau[:, :], s5k[:, j, 4, q:q + 1], None, ALU.mult, None, PK + ZK + ["tau"], ZK)
                range_reduce(zd, za, zb.bitcast(I32), zc, ZK)
                actf(stg[:, 1, :], zd, AF.Sin, ZK, SK)
                ts("dve", za, za, math.pi / 2, None, ALU.add, None, ZK, ZK)
                range_reduce(zd, za, zb.bitcast(I32), zc, ZK)
                actf(stg[:, 0, :], zd, AF.Sin, ZK, SK)
                dma("sp", s5tab_d[j][q].rearrange("p (a t) -> p a t", a=2), stg, f"tbw{b_}", SK, [("s5tab", j)])
            if s5_stage < 1:
                return
            ta = tmp[:, 4224:4240]
            tb = tmp[:, 4240:4256]
            for q in range(NQ):
                k, r4 = divmod(q, 4)
                br = s5nat[:, 0, q, :]
                bi = s5nat[:, 1, q, :]
                ts("dve", ta, br, s5k[:, j, 7, q:q + 1], None, ALU.mult, None, PK, PK)
                stt(ta, bi, s5k[:, j, 10, q:q + 1], ta, ALU.mult, ALU.add, PK, PK)
                ts("dve", tb, bi, s5k[:, j, 7, q:q + 1], None, ALU.mult, None, PK, PK)
                stt(tb, br, s5k[:, j, 8, q:q + 1], tb, ALU.mult, ALU.add, PK, PK)
                FK = ["s5full"]
                memset("dve", s5full[:, :, :], 0.0, FK)
                for a_, src in ((0, ta), (1, tb)):
                    for gl in range(2):
                        ts("dve", s5full[:, a_, 32 * r4 + 16 * gl:32 * r4 + 16 * gl + 16], src, masks[:, 16 + gl:17 + gl], None,
                           ALU.mult, None, PK + FK + ["masks"], FK)
                pi = psn()
                tr(pi, slice(0, 128), s5full[:, 0, :], FK, last=False)
                tr(pi, slice(128, 256), s5full[:, 1, :], FK, last=True)
                cp("act", s5bcs[:, 0:2, :], ps[pi][:, 0:256].rearrange("p (a n) -> p a n", a=2), [("ps", pi)], ["s5bcs"])
                for a_ in range(2):
                    for gl in range(2):
                        mcol = (8 if a_ else 0) + 2 * r4 + gl
                        ts("dve", s5full[:, a_, 64 * gl:64 * gl + 64], s5cn[:, a_, k, :], masks[:, mcol:mcol + 1], None,
                           ALU.mult, None, PK + FK + ["masks"], FK)
                pi = psn()
                tr(pi, slice(0, 128), s5full[:, 0, :], FK, last=False)
                tr(pi, slice(128, 256), s5full[:, 1, :], FK, last=True)
                cp("act", s5bcs[:, 2:4, :], ps[pi][:, 0:256].rearrange("p (a n) -> p a n", a=2), [("ps", pi)], ["s5bcs"])
                dma("sp", s5bc_d[j][:, :, q * 128:(q + 1) * 128].rearrange("a p n -> p a n"), s5bcs[:, :, :], "scr",
                    ["s5bcs"], [("s5bc", j)])

        def s5_slabs(j):
            slab_list.append(slab_cols("s5_w_in", j, KC, [(0, 1024)]))
            for half in range(2):
                def fn(dst2d, j=j, half=half):
                    dst = dst2d[:, 0:8192].rearrange("p (k n) -> p k n", k=64)
                    src = s5bc_d[j][2 * half:2 * half + 2, :, :].rearrange("a p (q n) -> p a q n", q=NQ)
                    return [(dst[:, 0:32, :], src[:, 0, :, :]), (dst[:, 32:64, :], src[:, 1, :, :])], "pool"
                slab_list.append(SlabDef(fn, reads=[("s5bc", j)], direct=True))
            for i2 in range(2):
                slab_list.append(slab_cols("s5_w_glu", j, KC, [(512 * i2, 512), (1024 + 512 * i2, 512)]))
            slab_list.append(slab_cols("s5_w_out", j, KC, [(0, 1024)]))

        def phase_s5(j):
            slot_i, g_i = next_slab()
            wi_ = ring_view(slot_i, KC, 1024)
            for c in range(KC):
                pi = psn()
                for k in range(KC):
                    mm(pi, wi_[:, k, 128 * c:128 * c + 128], hn[:, k, :], k == 0, k == KC - 1, [("ring", slot_i), ("hn", k)])
                cp("act", u32[:, c, :], ps[pi][:, :], [("ps", pi)], [("big16", c)])
                cp("dve", sq[:, c, :], u32[:, c, :], [("big16", c)], [("sq", c)])
            release(g_i)
            slot_b, g_b = next_slab()
            wB = ring_view(slot_b, 64, 128)
            slot_c, g_c = next_slab()
            wC = ring_view(slot_c, 64, 128)
            if s5_stage in (2, 20, 21):
                pi = psn()
                mm(pi, wB[:, 0, :], sq[:, 0, :], True, True, [("ring", slot_b), ("sq", 0)])
                pi = psn()
                mm(pi, wC[:, 0, :], sq[:, 0, :], True, True, [("ring", slot_c), ("sq", 0)])
                release(g_b)
                release(g_c)
                for _ in range(3):
                    sl_, g_ = next_slab()
                    pi = psn()
                    mm(pi, ring[:, sl_, 0:128], sq[:, 0, :], True, True, [("ring", sl_), ("sq", 0)])
                    release(g_)
                return
            ps_mod[0] = 6
            ps_i[0] = 0
            qst = {}
            free = list(range(NTMP))

            def talloc():
                assert free, "S5 tmp slots exhausted"
                return free.pop(0)

            def tfree(*ids):
                free.extend(ids)

            def load_tab(q):
                slot = q % 3
                dma("sp", tabr[:, slot, :, :], s5tab_d[j][q].rearrange("p (a t) -> p a t", a=2), f"tb{slot}",
                    [("s5tab", j)], [("tabr", slot)])

            def s5_a(q):
                c = q // 4
                sl = q % 3
                Cq = tabr[:, sl, 0, :]
                Sq = tabr[:, sl, 1, :]
                TK = [("tabr", sl)]
                pbr = psn()
                mm(pbr, wB[:, q, :], sq[:, c, :], True, True, [("ring", slot_b), ("sq", c)])
                pbi = psn()
                mm(pbi, wB[:, 32 + q, :], sq[:, c, :], True, True, [("ring", slot_b), ("sq", c)])
                br_, bi_, t_, mr, u_, v_ = [talloc() for _ in range(6)]
                cp("act", TM(br_), ps[pbr][:, :], [("ps", pbr)], [("tmp", br_)])
                cp("act", TM(bi_), ps[pbi][:, :], [("ps", pbi)], [("tmp", bi_)])
                tt("dve", TM(t_), TM(br_), Cq, ALU.mult, [("tmp", br_)] + TK, [("tmp", t_)])
                tt("dve", TM(mr), TM(bi_), Sq, ALU.mult, [("tmp", bi_)] + TK, [("tmp", mr)])
                tt("dve", TM(mr), TM(mr), TM(t_), ALU.add, [("tmp", mr), ("tmp", t_)], [("tmp", mr)])
                tt("pool", TM(u_), TM(bi_), Cq, ALU.mult, [("tmp", bi_)] + TK, [("tmp", u_)])
                tt("pool", TM(v_), TM(br_), Sq, ALU.mult, [("tmp", br_)] + TK, [("tmp", v_)])
                tt("pool", TM(u_), TM(u_), TM(v_), ALU.subtract, [("tmp", u_), ("tmp", v_)], [("tmp", u_)])
                qst[q] = (br_, bi_, t_, mr, u_, v_)

            def s5_b(q):
                c, qq = divmod(q, 4)
                py = 6 + (c % 2)
                sl = q % 3
                Cq = tabr[:, sl, 0, :]
                Sq = tabr[:, sl, 1, :]
                Cend = tabr[:, sl, 0, T - 1:T]
                Send = tabr[:, sl, 1, T - 1:T]
                TK = [("tabr", sl)]
                br_, bi_, t_, mr, mi, v_ = qst.pop(q)
                rho = s5k[:, j, 3, q:q + 1].to_broadcast([128, T])
                gr, gi = talloc(), talloc()
                cr = s5carry[:, j, 0, q:q + 1]
                ci = s5carry[:, j, 1, q:q + 1]
                CK = [("s5carry", q)]
                P.op("dve", lambda e, gr=gr, mr=mr, cr=cr, rho=rho: e.tensor_tensor_scan(
                    out=TM(gr), data0=rho, data1=TM(mr), initial=cr, op0=ALU.mult, op1=ALU.add),
                    reads=[("tmp", mr), "s5p"] + CK, writes=[("tmp", gr)])
                P.op("dve", lambda e, gi=gi, mi=mi, ci=ci, rho=rho: e.tensor_tensor_scan(
                    out=TM(gi), data0=rho, data1=TM(mi), initial=ci, op0=ALU.mult, op1=ALU.add),
                    reads=[("tmp", mi), "s5p"] + CK, writes=[("tmp", gi)])
                gre = TM(gr, T - 1, 1)
                gie = TM(gi, T - 1, 1)
                tA = s5k[:, j, 14, q:q + 1]
                tB = s5k[:, j, 15, q:q + 1]
                ts("dve", tA, gie, Send, None, ALU.mult, None, [("tmp", gi)] + TK, [("s5t", q)])
                ts("dve", tB, gre, Send, None, ALU.mult, None, [("tmp", gr)] + TK, [("s5t", q)])
                stt(cr, gre, Cend, tA, ALU.mult, ALU.subtract, [("tmp", gr), ("s5t", q)] + TK, CK)
                stt(ci, gie, Cend, tB, ALU.mult, ALU.add, [("tmp", gi), ("s5t", q)] + TK, CK)
                hb = q % 2
                tt("dve", TM(t_), TM(gr), Cq, ALU.mult, [("tmp", gr)] + TK, [("tmp", t_)])
                tt("dve", TM(mr), TM(gi), Sq, ALU.mult, [("tmp", gi)] + TK, [("tmp", mr)])
                tt("dve", hrb[:, hb, :], TM(t_), TM(mr), ALU.subtract, [("tmp", t_), ("tmp", mr)], [("hrb", hb)])
                tt("pool", TM(v_), TM(gr), Sq, ALU.mult, [("tmp", gr)] + TK, [("tmp", v_)])
                tt("pool", TM(br_), TM(gi), Cq, ALU.mult, [("tmp", gi)] + TK, [("tmp", br_)])
                tt("pool", hib[:, hb, :], TM(v_), TM(br_), ALU.add, [("tmp", v_), ("tmp", br_)], [("hib", hb)])
                mm(py, wC[:, q, :], hrb[:, hb, :], qq == 0, False, [("ring", slot_c), ("hrb", hb)])
                mm(py, wC[:, 32 + q, :], hib[:, hb, :], False, qq == 3, [("ring", slot_c), ("hib", hb)])
                tfree(br_, bi_, t_, mr, mi, v_, gr, gi)
                if qq == 3:
                    yy = talloc()
                    tfree(yy)
                    cp("act", TM(yy), ps[py][:, :], [("ps", py)], [("tmp", yy)])
                    stt(TM(yy), u32[:, c, :], C1("s5_d", j * KC + c), TM(yy), ALU.mult, ALU.add,
                        [("big16", c), ("tmp", yy), "cst"], [("tmp", yy)])
                    actf(act[:, 8 + c, :], TM(yy), AF.Gelu_apprx_tanh, [("tmp", yy)], [("act", 8 + c)])

            load_tab(0)
            load_tab(1)
            if S5_PIPE:
                s5_a(0)
                for q in range(NQ):
                    if q + 2 < NQ:
                        load_tab(q + 2)
                    if q + 1 < NQ:
                        s5_a(q + 1)
                    s5_b(q)
            else:
                for q in range(NQ):
                    if q + 2 < NQ:
                        load_tab(q + 2)
                    s5_a(q)
                    s5_b(q)
            ps_mod[0] = 8
            ps_i[0] = 0
            tmp_i[0] = 0
            release(g_b)
            release(g_c)
            for i2 in range(2):
                slot_g, g_g = next_slab()
                wg = ring_view(slot_g, KC, 1024)
                for m4 in range(4):
                    m = 4 * i2 + m4
                    pv = psn()
                    for k in range(KC):
                        mm(pv, wg[:, k, 128 * m4:128 * m4 + 128], act[:, 8 + k, :], k == 0, k == KC - 1,
                           [("ring", slot_g), ("act", 8 + k)])
                    pg = psn()
                    for k in range(KC):
                        mm(pg, wg[:, k, 512 + 128 * m4:512 + 128 * m4 + 128], act[:, 8 + k, :], k == 0, k == KC - 1,
                           [("ring", slot_g), ("act", 8 + k)])
                    sg = tslot()
                    actf(TM(sg), ps[pg][:, :], AF.Sigmoid, [("ps", pg)], [("tmp", sg)])
                    tt("dve", act[:, 16 + m, :], ps[pv][:, :], TM(sg), ALU.mult, [("ps", pv), ("tmp", sg)], [("act", 16 + m)])
                release(g_g)
            slot_o, g_o = next_slab()
            wo = ring_view(slot_o, KC, 1024)
            for m in range(KC):
                pi = psn()
                for k in range(KC):
                    mm(pi, wo[:, k, 128 * m:128 * m + 128], act[:, 16 + k, :], k == 0, k == KC - 1,
                       [("ring", slot_o), ("act", 16 + k)])
                tt("dve", h[:, m, :], h[:, m, :], ps[pi][:, :], ALU.add, [("h", m), ("ps", pi)], [("h", m)])
            release(g_o)

        def phase_store(t):
            fs = [tslot() for _ in range(KC)]

            def o(k):
                return TM(fs[k]), ("tmp", fs[k])
            phase_norm("g_fin", 0, out_f32=o)
            for j in range(4):
                for hf in range(2):
                    pi = psn()
                    for kk in range(4):
                        k = 4 * hf + kk
                        tr(pi, slice(128 * kk, 128 * kk + 128), TM(fs[k], 128 * j, 128), [("tmp", fs[k])], last=(kk == 3))
                    cp("act" if hf else "dve", xin[:, j, 512 * hf:512 * hf + 512], ps[pi][:, :], [("ps", pi)],
                       [("big16", 2 * j + hf)])
            dma("sp", out_d[t * T:(t + 1) * T, :].rearrange("(j p) d -> p j d", p=128), xin, "ost",
                [("big16", i) for i in range(8)], [("outd", t)])

        EPS_AP = sb("eps_ap", [128, 1])
        ONE_AP = sb("one_ap", [128, 1])
        memset("dve", EPS_AP[:, :], EPS, ["eps"])
        memset("dve", ONE_AP[:, :], 1.0, ["one"])

        memset("dve", onesf[:, :], 1.0, ["onesf"])
        for gi_ in range(8):
            P.op("dve", lambda e, gi_=gi_: e.reduce_sum(out=masks[:, gi_:gi_ + 1], in_=ident[:, 16 * gi_:16 * gi_ + 16],
                                                        axis=mybir.AxisListType.X), reads=["ident"], writes=["masks"])
        for hf_ in range(2):
            P.op("dve", lambda e, hf_=hf_: e.reduce_sum(out=masks[:, 16 + hf_:17 + hf_], in_=ident[:, 64 * hf_:64 * hf_ + 64],
                                                        axis=mybir.AxisListType.X), reads=["ident"], writes=["masks"])
        ts("dve", masks[:, 8:16], masks[:, 0:8], -1.0, None, ALU.mult, None, ["masks"], ["masks"])
        if "s" in mix:
            for j_ in sorted({i // 2 for i in layers if i % 2 == 1}):
                if s5_stage != 21:
                    s5_prologue(j_)

        for i in layers:
            if i % 2 == 0 and "r" in mix:
                rg_slabs(i // 2)
            if i % 2 == 1 and "s" in mix and s5_stage >= 2:
                s5_slabs(i // 2)
            if "f" in mix:
                ffn_slabs(i)

        if USE_WSC:
            cast_all_slabs()
        P.barrier()
        for t in range(n_tiles):
            cur[0] = t
            phase_load(t)
            for i in layers:
                if i % 2 == 0 and "r" in mix:
                    phase_norm("g_mix", i * KC)
                    phase_rg(i // 2)
                if i % 2 == 1 and "s" in mix and s5_stage >= 2:
                    phase_norm("g_mix", i * KC)
                    phase_s5(i // 2)
                if "f" in mix:
                    phase_norm("g_ffn", i * KC)
                    phase_ffn(i)
            phase_store(t)
        P.final_wait("sp", [("outd", t) for t in range(n_tiles)] + [("dbg", i) for i in range(dbg_n[0])])
        P.dbg_names = dbg_names

        with nc.Block() as block:
            @block.sync
            def _(e):
                P.replay("sp", e, sems)

            @block.gpsimd
            def _(e):
                P.replay("pool", e, sems)

            @block.scalar
            def _(e):
                P.replay("act", e, sems)

            @block.vector
            def _(e):
                P.replay("dve", e, sems)

            @block.tensor
            def _(e):
                P.replay("pe", e, sems)
    return nc, P


def make_consts():
    return {"ident": np.eye(128, dtype=np.float32),
            "tau": np.tile(np.arange(1, T + 1, dtype=np.float32)[None, :], (128, 1))}


def shape_inputs(inputs):
    r = {}
    for k, v in inputs.items():
        if k == "x":
            continue
        v = np.ascontiguousarray(v, dtype=np.float32)
        if k == "norm_final_g":
            v = v.reshape(1, D)
        elif k in ("rg_b_a", "rg_b_x"):
            v = v.reshape(2, D)
        elif k in ("s5_a_re", "s5_a_im"):
            v = v.reshape(2, 4096)
        elif k in ("s5_b_re", "s5_b_im"):
            v = v.reshape(2, 4096, 16)
        elif k in ("s5_c_re", "s5_c_im"):
            v = v.reshape(2, 1024, 64)
        r[k] = v
    return r


def kernel(**inputs):
    x = np.ascontiguousarray(inputs["x"], dtype=np.float32)
    w = shape_inputs(inputs)
    w.update(make_consts())
    nc, _ = build_program()
    in_maps = [dict(w, x=x[b]) for b in range(BATCH)]
    res = run_bass_kernel_spmd(nc, in_maps, core_ids=list(range(BATCH)))
    return np.stack([res.results[b]["out"] for b in range(BATCH)], axis=0)
```

```python
import math
import numpy as np
import concourse.bass as bass
import concourse.mybir as mybir
from concourse.bass_utils import run_bass_kernel_spmd

F32 = mybir.dt.float32
BF16 = mybir.dt.bfloat16
AF = mybir.ActivationFunctionType
ALU = mybir.AluOpType

D = 1024
KC = 8
T = 512
SEQ = 8192
BATCH = 4
DEPTH = 4
DFF = 3072
NJ = 24
EPS = 1e-6
LSUB = 64
NQ = 32
ENGS = ("pe", "act", "dve", "pool", "sp")
SLAB = 8192
import os
S5_PIPE = os.environ.get('K_S5PIPE', '1') == '1'
S5_ENG2 = os.environ.get('K_S5ENG2', 'dve')
FFN_LOOK = int(os.environ.get('K_FFNLOOK', '2'))
RG_PIPE = os.environ.get('K_RGPIPE', '1') == '1'
USE_WSC = True
NSLOT = 3


class Prog:
    def __init__(self):
        self.prog = {e: [] for e in ENGS}
        self.count = {e: 0 for e in ENGS}
        self.dcount = {}
        self.seen = {e: {} for e in ENGS}
        self.last_write = {}
        self.readers = {}
        self.n_ops = 0

    def _need(self, eng, waits, tok):
        if tok is None:
            return
        sk, v = tok
        if sk == eng and (eng in ("pe", "sp") or v > self.count[eng]):
            return
        if self.seen[eng].get(sk, 0) >= v:
            return
        if waits.get(sk, 0) < v:
            waits[sk] = v

    def op(self, eng, fn, reads=(), writes=(), track=True, dma=None, ninc=1):
        waits = {}
        for r in reads:
            self._need(eng, waits, self.last_write.get(r))
        for w in writes:
            self._need(eng, waits, self.last_write.get(w))
            for sk, v in self.readers.get(w, {}).items():
                self._need(eng, waits, (sk, v))
        for sk, v in waits.items():
            self.prog[eng].append(("wait", sk, v))
            self.seen[eng][sk] = v
        if dma is not None:
            prev = self.dcount.get(dma, 0)
            if prev > self.seen[eng].get(dma, 0) and (dma.startswith("cld") or dma.startswith("wcs")):
                self.prog[eng].append(("wait", dma, prev))
                self.seen[eng][dma] = prev
            self.dcount[dma] = self.dcount.get(dma, 0) + 16 * ninc
            tok = (dma, self.dcount[dma])
            self.prog[eng].append(("op", fn, dma, 16))
        elif track:
            self.count[eng] += 1
            tok = (eng, self.count[eng])
            self.prog[eng].append(("op", fn, eng, 1))
        else:
            tok = (eng, self.count[eng] + 1)
            self.prog[eng].append(("op", fn, None, 0))
        for w in writes:
            self.last_write[w] = tok
            self.readers[w] = {}
        for r in reads:
            d = self.readers.setdefault(r, {})
            if d.get(tok[0], 0) < tok[1]:
                d[tok[0]] = tok[1]
        self.n_ops += 1

    def barrier(self):
        snap = dict(self.count)
        dsnap = dict(self.dcount)
        for e in ENGS:
            for o, v in list(snap.items()) + list(dsnap.items()):
                if o.startswith("wcs"):
                    continue
                if o != e and v > 0 and self.seen[e].get(o, 0) < v:
                    self.prog[e].append(("wait", o, v))
                    self.seen[e][o] = v

    def final_wait(self, eng, keys):
        waits = {}
        for k in keys:
            self._need(eng, waits, self.last_write.get(k))
        for sk, v in waits.items():
            self.prog[eng].append(("wait", sk, v))
            self.seen[eng][sk] = v

    def replay(self, eng, handle, sems):
        for item in self.prog[eng]:
            if item[0] == "wait":
                handle.wait_ge(sems[item[1]], item[2])
            else:
                _, fn, sk, inc = item
                r = fn(handle)
                if sk is not None:
                    if isinstance(r, (list, tuple)):
                        for ins in r:
                            ins.then_inc(sems[sk], inc)
                    else:
                        r.then_inc(sems[sk], inc)


def build_program(n_tiles=SEQ // T, layers=(0, 1, 2, 3), seq=SEQ, mix="rsf", debug=0, s5_stage=4):
    nc = bass.Bass("TRN2", target_bir_lowering=False)
    P = Prog()

    def dram(name, shape, kind="ExternalInput", dt=F32):
        return nc.dram_tensor(name, list(shape), dt, kind=kind).ap()

    x_d = dram("x", [seq, D])
    out_d = dram("out", [seq, D], kind="ExternalOutput")
    ident_d = dram("ident", [128, 128])
    tau_d = dram("tau", [128, T])
    s5tab_d = [dram(f"s5tab{j}", [NQ, 128, 2 * T], kind="Internal") for j in range(2)]
    W = {}
    for name, shape in [
        ("norm_mix_g", [4, D]), ("norm_ffn_g", [4, D]), ("norm_final_g", [1, D]),
        ("rg_w_in", [2, D, 2 * D]), ("rg_conv_w", [2, 4, D]), ("rg_conv_b", [2, D]),
        ("rg_w_a", [2, 8, 128, 128]), ("rg_b_a", [2, D]), ("rg_w_x", [2, 8, 128, 128]),
        ("rg_b_x", [2, D]), ("rg_lambda", [2, D]), ("rg_w_out", [2, D, D]),
        ("s5_w_in", [2, D, D]), ("s5_a_re", [2, 4096]), ("s5_a_im", [2, 4096]),
        ("s5_log_dt", [2, 64]), ("s5_b_re", [2, 4096, 16]), ("s5_b_im", [2, 4096, 16]),
        ("s5_c_re", [2, 1024, 64]), ("s5_c_im", [2, 1024, 64]), ("s5_d", [2, D]),
        ("s5_w_glu", [2, D, 2 * D]), ("s5_w_out", [2, D, D]),
        ("ffn_w_up", [4, D, 2 * DFF]), ("ffn_conv_w", [4, 3, 2 * DFF]),
        ("ffn_conv_b", [4, 2 * DFF]), ("ffn_w_down", [4, DFF, D]),
    ]:
        W[name] = dram(name, shape)
    s5bc_d = [dram(f"s5bc{j}", [4, 128, NQ * 128], kind="Internal", dt=BF16) for j in range(2)]

    dbg_d = dram("dbg", [max(debug, 1), 128, T], kind="ExternalOutput") if debug else None
    dbg_n = [0]
    cur = [0]
    dbg_names = []

    import contextlib
    es = contextlib.ExitStack()
    with es:
        def sb(name, shape, dt=F32):
            return es.enter_context(nc.sbuf_tensor(name, list(shape), dt))

        big16 = sb("big16", [128, 4096])
        xin = big16[:, :].rearrange("p (j d) -> p j d", j=4)
        u32 = big16[:, :].rearrange("p (k t) -> p k t", k=KC)
        h = sb("h", [128, KC, T])
        sq = sb("sq", [128, KC, T], BF16)
        hn = sb("hn", [128, KC, T], BF16)
        rstd = sb("rstd", [128, T])
        ring = sb("ring", [128, NSLOT, SLAB], BF16)
        act = sb("act", [128, NJ, T], BF16)
        NTMP = 16
        TW = T + 4
        tmp = sb("tmp", [128, NTMP * TW])
        xcb = sb("xcb", [128, 2, T], BF16)
        hrb = sb("hrb", [128, 2, T], BF16)
        hib = sb("hib", [128, 2, T], BF16)
        ident = sb("ident_sb", [128, 128])
        identb = sb("identb", [128, 128], BF16)
        onesb = sb("onesb", [128, 128], BF16)
        tau = sb("tau_sb", [128, T])
        tabr = sb("tabr", [128, 3, 2, T])
        NCONST = 1152
        cst = sb("cst", [128, NCONST])
        stage = tmp[:, 0:1152].rearrange("p (b n) -> p b n", b=9)
        rgc = sb("rgc", [128, 2, 2, KC])
        halo_rg = sb("halo_rg", [128, 2, KC, 3])
        carry_rg = sb("carry_rg", [128, 2, KC])
        halo_ffn = sb("halo_ffn", [128, 4, 2 * NJ, 2], BF16)
        NUPB = 6
        upb = sb("upb", [128, NUPB, T + 2], BF16)
        dgs = sb("dgs", [128, NUPB, 3, 128], BF16)
        upb_i = [0]
        s5k = sb("s5k", [128, 2, 16, NQ])
        s5carry = sb("s5carry", [128, 2, 2, NQ])
        s5nat = tmp[:, 1152:1152 + 2048].rearrange("p (a q c) -> p a q c", a=4, q=NQ)
        s5cn = tmp[:, 3200:3200 + 1024].rearrange("p (a k n) -> p a k n", a=2, k=KC)
        s5full = sb("s5full", [128, 2, 128])
        s5bcs = sb("s5bcs", [128, 4, 128], BF16)
        masks = sb("masks", [128, 20])
        onesf = sb("onesf", [128, 128])
        ldrow = sb("ldrow", [1, 128])

        ps = [es.enter_context(nc.psum_tensor(f"ps{i}", [128, T], F32)) for i in range(8)]
        ps_i = [0]
        ps_mod = [8]

        def psn():
            i = ps_i[0] % ps_mod[0]
            ps_i[0] = (i + 1) % ps_mod[0]
            return i

        NCLD = 8
        sem_names = list(ENGS) + [f"w{s}" for s in range(NSLOT)] + [f"cld{i}" for i in range(NCLD)] + ["xld", "ost", "scr", "tb0", "tb1", "tb2", "tbw0", "tbw1"]
        cld_i = [0]
        NWCS = 8
        wcs_i = [0]
        sem_names += [f"wcs{i_}" for i_ in range(NWCS)]
        sems = {n: es.enter_context(nc.semaphore(n)) for n in sem_names}

        def mm(pi, lhsT, rhs, start, stop, reads, cols=slice(0, T)):
            P.op("pe", lambda e: e.matmul(ps[pi][:, cols], lhsT=lhsT, rhs=rhs, start=start, stop=stop),
                 reads=reads, writes=[("ps", pi)], track=stop)

        def tr(pi, cols, in_, reads, last=True):
            P.op("pe", lambda e: e.transpose(ps[pi][:, cols], in_, ident[:, :]),
                 reads=list(reads) + ["ident"], writes=[("ps", pi)], track=last)

        def actf(out, in_, func, reads, writes, bias=None, scale=None):
            kw = {}
            if bias is not None:
                kw["bias"] = bias
            if scale is not None:
                kw["scale"] = scale
            P.op("act", lambda e: e.activation(out=out, in_=in_, func=func, **kw), reads=reads, writes=writes)

        def tt(eng, out, in0, in1, op, reads, writes):
            P.op(eng, lambda e: e.tensor_tensor(out=out, in0=in0, in1=in1, op=op), reads=reads, writes=writes)

        def ts(eng, out, in0, s1, s2, op0, op1, reads, writes):
            if op1 is None:
                P.op(eng, lambda e: e.tensor_scalar(out=out, in0=in0, scalar1=s1, scalar2=None, op0=op0),
                     reads=reads, writes=writes)
            else:
                P.op(eng, lambda e: e.tensor_scalar(out=out, in0=in0, scalar1=s1, scalar2=s2, op0=op0, op1=op1),
                     reads=reads, writes=writes)

        def stt(out, in0, scalar, in1, op0, op1, reads, writes):
            P.op("dve", lambda e: e.scalar_tensor_tensor(out=out, in0=in0, scalar=scalar, in1=in1, op0=op0, op1=op1),
                 reads=reads, writes=writes)

        def cp(eng, out, in_, reads, writes):
            if eng == "act":
                P.op("act", lambda e: e.copy(out=out, in_=in_), reads=reads, writes=writes)
            else:
                P.op(eng, lambda e: e.tensor_copy(out=out, in_=in_), reads=reads, writes=writes)

        def memset(eng, ap, val, writes):
            P.op(eng, lambda e: e.memset(ap, val), writes=writes)

        def dma(eng, out, in_, sem, reads, writes, slow=False):
            if sem == "cld":
                sem = f"cld{cld_i[0]}"
                cld_i[0] = (cld_i[0] + 1) % NCLD
            if sem == "wcs":
                sem = f"wcs{wcs_i[0]}"
                wcs_i[0] = (wcs_i[0] + 1) % NWCS
            kw = {"allow_slow_non_contiguous": True} if slow else {}
            P.op(eng, lambda e: e.dma_start(out=out, in_=in_, **kw), reads=reads, writes=writes, dma=sem)

        def dbg(name, ap, key):
            if not debug or dbg_n[0] >= debug or cur[0] != n_tiles - 1:
                return
            i = dbg_n[0]
            dbg_n[0] += 1
            dbg_names.append(name)
            dma("sp", dbg_d[i], ap, "scr", [key], [("dbg", i)])

        tmp_i = [0]

        def tslot():
            i = tmp_i[0]
            tmp_i[0] = (i + 1) % NTMP
            return i

        def TM(i, c0=0, n=T):
            return tmp[:, i * TW + c0:i * TW + c0 + n]

        slab_list = []
        slab_state = {"issued": 0, "used": 0}
        PF = NSLOT - 1

        wsc_box = [None]

        def issue_slab(gidx):
            idx = gidx % len(slab_list)
            slot = gidx % NSLOT
            sd = slab_list[idx]
            if sd.direct or not USE_WSC:
                pairs, q = sd(ring[:, slot, :])
                if USE_WSC:
                    q = "sp"

                def fn(e, pairs=pairs):
                    return [e.dma_start(out=o, in_=i) for (o, i) in pairs]
                P.op(q, fn, reads=[r for r in sd.reads], writes=[("ring", slot)],
                     dma=f"w{slot}", ninc=len(pairs))
            else:
                n = sd.size
                src = wsc_box[0][idx, :, 0:n]
                P.op("sp", lambda e, slot=slot, n=n, src=src: e.dma_start(out=ring[:, slot, 0:n], in_=src),
                     reads=[("wsc", idx)], writes=[("ring", slot)], dma=f"w{slot}", ninc=1)

        def cast_all_slabs():
            wsc_box[0] = dram("wsc", [len(slab_list), 128, SLAB], kind="Internal", dt=BF16)
            for idx, sd in enumerate(slab_list):
                if sd.direct:
                    continue
                pairs, _ = sd(wsc_box[0][idx])
                for (o, i) in pairs:
                    dma("pool", o, i, "wcs", [], [("wsc", idx)])

        released = set()

        def pump():
            total = len(slab_list) * n_tiles
            while slab_state["issued"] < min(slab_state["used"] + PF + 1, total):
                n = slab_state["issued"]
                if n >= NSLOT and (n - NSLOT) not in released:
                    break
                issue_slab(n)
                slab_state["issued"] += 1

        def next_slab():
            g = slab_state["used"]
            slab_state["used"] += 1
            pump()
            assert slab_state["issued"] > g, "slab ring deadlock: too many live slabs"
            return g % NSLOT, g

        def release(g):
            released.add(g)
            pump()

        class SlabDef:
            def __init__(self, fn, reads=(), direct=False, size=SLAB):
                self.fn = fn
                self.reads = reads
                self.direct = direct
                self.size = size

            def __call__(self, dst2d):
                return self.fn(dst2d)

        def slab_cols(wname, l, kc, col_sets, q="pool"):
            ntot = sum(n for _, n in col_sets)

            def fn(dst2d):
                dst = dst2d[:, 0:kc * ntot].rearrange("p (k n) -> p k n", k=kc)
                src = W[wname][l].rearrange("(k p) n -> p k n", p=128)
                pairs = []
                o = 0
                for c0, n in col_sets:
                    pairs.append((dst[:, :, o:o + n], src[:, :, c0:c0 + n]))
                    o += n
                return pairs, q
            return SlabDef(fn, size=kc * ntot)

        def ring_view(slot, kc, n):
            return ring[:, slot, 0:kc * n].rearrange("p (k n) -> p k n", k=kc)

        crow = {}
        vec_list = []

        def addvec(name, ap2d):
            crow[name] = sum(v[1] for v in vec_list)
            vec_list.append((name, ap2d.shape[0], ap2d))

        addvec("g_mix", W["norm_mix_g"].rearrange("l (k p) -> (l k) p", p=128))
        addvec("g_ffn", W["norm_ffn_g"].rearrange("l (k p) -> (l k) p", p=128))
        addvec("g_fin", W["norm_final_g"].rearrange("l (k p) -> (l k) p", p=128))
        addvec("rg_cw", W["rg_conv_w"].rearrange("l t (k p) -> (l t k) p", p=128))
        addvec("rg_cb", W["rg_conv_b"].rearrange("l (k p) -> (l k) p", p=128))
        addvec("rg_ba", W["rg_b_a"].rearrange("l (k p) -> (l k) p", p=128))
        addvec("rg_bx", W["rg_b_x"].rearrange("l (k p) -> (l k) p", p=128))
        addvec("rg_lam", W["rg_lambda"].rearrange("l (k p) -> (l k) p", p=128))
        addvec("s5_d", W["s5_d"].rearrange("l (k p) -> (l k) p", p=128))
        addvec("ffn_cw", W["ffn_conv_w"].rearrange("l t (k p) -> (l t k) p", p=128))
        addvec("ffn_cb", W["ffn_conv_b"].rearrange("l (k p) -> (l k) p", p=128))
        addvec("s5_are", W["s5_a_re"].rearrange("l (q p) -> (l q) p", p=128))
        addvec("s5_aim", W["s5_a_im"].rearrange("l (q p) -> (l q) p", p=128))
        nrows = sum(v[1] for v in vec_list)
        assert nrows <= NCONST, nrows

        def C1(name, idx):
            c = crow[name] + idx
            return cst[:, c:c + 1]

        dma("sp", ident[:, :], ident_d[:, :], "cld", [], ["ident"])
        dma("sp", tau[:, :], tau_d[:, :], "cld", [], ["tau"])
        cp("dve", identb[:, :], ident[:, :], ["ident"], ["identb"])
        memset("dve", onesb[:, :], 1.0, ["onesb"])
        memset("dve", halo_rg[:, :, :, :], 0.0, [("halo_rg", l_, c_) for l_ in range(2) for c_ in range(KC)])
        memset("dve", carry_rg[:, :, :], 0.0, [("carry_rg", l_, c_) for l_ in range(2) for c_ in range(KC)])
        memset("dve", halo_ffn[:, :, :, :], 0.0, [("halo_ffn", i_, c_) for i_ in range(4) for c_ in range(2 * NJ)])
        memset("dve", s5carry[:, :, :, :], 0.0, ["s5carry"])
        nblk = (nrows + 127) // 128
        memset("dve", stage[:, :, :], 0.0, [("stage", b) for b in range(nblk)])
        r0 = 0
        for name, n, ap2d in vec_list:
            done = 0
            while done < n:
                blk, off = divmod(r0 + done, 128)
                take = min(n - done, 128 - off)
                dma("sp", stage[off:off + take, blk, :], ap2d[done:done + take, :], "cld", [], [("stage", blk)])
                done += take
            r0 += n
        for blk in range(nblk):
            pi = psn()
            tr(pi, slice(0, 128), stage[:, blk, :], [("stage", blk)])
            cp("dve", cst[:, blk * 128:(blk + 1) * 128], ps[pi][:, 0:128], [("ps", pi)], ["cst"])
        for l in range(2):
            lam = cst[:, crow["rg_lam"] + l * KC: crow["rg_lam"] + (l + 1) * KC]
            actf(rgc[:, l, 0, :], lam, AF.Exp, ["cst"], ["rgc"], scale=-1.0)
            actf(rgc[:, l, 0, :], rgc[:, l, 0, :], AF.Ln, ["rgc"], ["rgc"], bias=1.0)
            ts("dve", rgc[:, l, 1, :], rgc[:, l, 0, :], -16.0, None, ALU.mult, None, ["rgc"], ["rgc"])
            ts("dve", rgc[:, l, 0, :], rgc[:, l, 0, :], -8.0, None, ALU.mult, None, ["rgc"], ["rgc"])

        def phase_load(t):
            dma("sp", xin, x_d[t * T:(t + 1) * T, :].rearrange("(j p) d -> p j d", p=128), "xld",
                [], [("big16", i) for i in range(8)])
            for k in range(KC):
                pi = psn()
                for j in range(4):
                    tr(pi, slice(128 * j, 128 * j + 128), xin[:, j, 128 * k:128 * k + 128],
                       [("big16", 2 * j), ("big16", 2 * j + 1)], last=(j == 3))
                cp("act" if k % 2 else "dve", h[:, k, :], ps[pi][:, :], [("ps", pi)], [("h", k)])

        def phase_norm(gname, gidx0, out_f32=None):
            for k in range(KC):
                actf(sq[:, k, :], h[:, k, :], AF.Square, [("h", k)], [("sq", k)])
            pi = psn()
            for k in range(KC):
                mm(pi, onesb[:, :], sq[:, k, :], k == 0, k == KC - 1, ["onesb", ("sq", k)])
            actf(rstd[:, :], ps[pi][:, :], AF.Sqrt, [("ps", pi), "eps"], ["rstd"], bias=EPS_AP[:, :], scale=1.0 / D)
            P.op("dve", lambda e: e.reciprocal(out=rstd[:, :], in_=rstd[:, :]), reads=["rstd"], writes=["rstd"])
            for k in range(KC):
                if out_f32 is None:
                    stt(hn[:, k, :], h[:, k, :], C1(gname, gidx0 + k), rstd[:, :], ALU.mult, ALU.mult,
                        [("h", k), "rstd", "cst"], [("hn", k)])
                else:
                    o, key = out_f32(k)
                    stt(o, h[:, k, :], C1(gname, gidx0 + k), rstd[:, :], ALU.mult, ALU.mult,
                        [("h", k), "rstd", "cst"], [key])

        def phase_ffn(i):
            cw0 = crow["ffn_cw"] + i * 3 * 48
            cb0 = crow["ffn_cb"] + i * 48
            slabs = {}
            st = {}

            def get_slab(si):
                if si not in slabs:
                    slabs[si] = next_slab()
                return slabs[si]

            def stage_a(hx):
                jg, half = divmod(hx, 2)
                si, cc = divmod(jg, 4)
                slot, gsl = get_slab(si)
                wv = ring_view(slot, KC, 1024)
                ch = jg + NJ * half
                pi = psn()
                for k in range(KC):
                    mm(pi, wv[:, k, 512 * half + 128 * cc: 512 * half + 128 * cc + 128], hn[:, k, :],
                       k == 0, k == KC - 1, [("ring", slot), ("hn", k)])
                if hx % 8 == 7:
                    release(gsl)
                ub_i = upb_i[0]
                upb_i[0] = (ub_i + 1) % NUPB
                cp("pool", upb[:, ub_i, 0:2], halo_ffn[:, i, ch, :], [("halo_ffn", i, ch)], [("upbh", ub_i)])
                cp("act", upb[:, ub_i, 2:2 + T], ps[pi][:, :], [("ps", pi)], [("upb", ub_i)])
                cp("pool", halo_ffn[:, i, ch, :], upb[:, ub_i, T:T + 2], [("upb", ub_i)], [("halo_ffn", i, ch)])
                for t3 in range(3):
                    actf(dgs[:, ub_i, t3, :], identb[:, :], AF.Copy, ["identb", "cst"], [("dgs", ub_i, t3)],
                         scale=cst[:, cw0 + 48 * t3 + ch:cw0 + 48 * t3 + ch + 1])
                st[hx] = ub_i

            def stage_b(hx):
                jg, half = divmod(hx, 2)
                ub_i = st.pop(hx)
                pc = psn()
                for t3 in range(3):
                    mm(pc, dgs[:, ub_i, t3, :], upb[:, ub_i, t3:t3 + T], t3 == 0, t3 == 2,
                       [("dgs", ub_i, t3), ("upb", ub_i), ("upbh", ub_i)])
                if half == 0:
                    g = tslot()
                    actf(TM(g), ps[pc][:, :], AF.Gelu_apprx_tanh, [("ps", pc), "cst"], [("tmp", g)],
                         bias=cst[:, cb0 + jg:cb0 + jg + 1])
                    st[("g", jg)] = g
                else:
                    g = st.pop(("g", jg))
                    stt(act[:, jg, :], ps[pc][:, :], cst[:, cb0 + NJ + jg:cb0 + NJ + jg + 1], TM(g), ALU.add, ALU.mult,
                        [("ps", pc), ("tmp", g), "cst"], [("act", jg)])

            NH = 2 * NJ
            LOOK = FFN_LOOK
            for hx in range(min(LOOK, NH)):
                stage_a(hx)
            for hx in range(NH):
                if hx + LOOK < NH:
                    stage_a(hx + LOOK)
                stage_b(hx)
            for s in range(4):
                slot, gsl = next_slab()
                wv = ring_view(slot, NJ, 256)
                for mh in range(2):
                    m = 2 * s + mh
                    pi = psn()
                    for j in range(NJ):
                        mm(pi, wv[:, j, 128 * mh:128 * mh + 128], act[:, j, :], j == 0, j == NJ - 1,
                           [("ring", slot), ("act", j)])
                    tt("dve", h[:, m, :], h[:, m, :], ps[pi][:, :], ALU.add, [("h", m), ("ps", pi)], [("h", m)])
                release(gsl)

        def ffn_slabs(i):
            for s in range(6):
                slab_list.append(slab_cols("ffn_w_up", i, KC, [(512 * s, 512), (DFF + 512 * s, 512)]))
            for s in range(4):
                slab_list.append(slab_cols("ffn_w_down", i, NJ, [(256 * s, 256)]))

        def phase_rg(l):
            slot_g, g_g = next_slab()
            wg = ring_view(slot_g, KC, 1024)
            for c in range(KC):
                pg = psn()
                for k in range(KC):
                    mm(pg, wg[:, k, 128 * c:128 * c + 128], hn[:, k, :], k == 0, k == KC - 1, [("ring", slot_g), ("hn", k)])
                actf(act[:, 8 + c, :], ps[pg][:, :], AF.Gelu_apprx_tanh, [("ps", pg)], [("act", 8 + c)])
            release(g_g)
            slot_x, g_x = next_slab()
            wx = ring_view(slot_x, KC, 1024)
            slot_a, g_a = next_slab()
            wa = ring_view(slot_a, 16, 128)
            ust = {}

            def rg_a(c):
                pi = psn()
                for k in range(KC):
                    mm(pi, wx[:, k, 128 * c:128 * c + 128], hn[:, k, :], k == 0, k == KC - 1, [("ring", slot_x), ("hn", k)])
                u = tslot()
                cp("pool", TM(u, 0, 3), halo_rg[:, l, c, :], [("halo_rg", l, c)], [("tmp", u)])
                cp("act", TM(u, 3, T), ps[pi][:, :], [("ps", pi)], [("tmp", u)])
                cp("pool", halo_rg[:, l, c, :], TM(u, T, 3), [("tmp", u)], [("halo_rg", l, c)])
                ust[c] = u

            def rg_b(c):
                u = ust.pop(c)
                xc = tslot()
                ts("dve", TM(xc), TM(u, 0, T), C1("rg_cw", l * 32 + 0 * KC + c), C1("rg_cb", l * KC + c),
                   ALU.mult, ALU.add, [("tmp", u), "cst"], [("tmp", xc)])
                for k in range(1, 4):
                    stt(TM(xc), TM(u, k, T), C1("rg_cw", l * 32 + k * KC + c), TM(xc), ALU.mult, ALU.add,
                        [("tmp", u), ("tmp", xc), "cst"], [("tmp", xc)])
                xb = c % 2
                xbv = xcb[:, xb, :]
                cp("pool", xbv, TM(xc), [("tmp", xc)], [("xcb", xb)])
                pa = psn()
                mm(pa, wa[:, c, :], xbv, True, True, [("ring", slot_a), ("xcb", xb)])
                px = psn()
                mm(px, wa[:, 8 + c, :], xbv, True, True, [("ring", slot_a), ("xcb", xb)])
                r = tslot()
                actf(TM(r), ps[pa][:, :], AF.Sigmoid, [("ps", pa), "cst"], [("tmp", r)], bias=C1("rg_ba", l * KC + c))
                ig = tslot()
                actf(TM(ig), ps[px][:, :], AF.Sigmoid, [("ps", px), "cst"], [("tmp", ig)], bias=C1("rg_bx", l * KC + c))
                a = tslot()
                actf(TM(a), TM(r), AF.Exp, [("tmp", r), "rgc"], [("tmp", a)], scale=rgc[:, l, 0, c:c + 1])
                actf(TM(r), TM(r), AF.Exp, [("tmp", r), "rgc"], [("tmp", r)], scale=rgc[:, l, 1, c:c + 1])
                actf(TM(r), TM(r), AF.Sqrt, [("tmp", r), "one"], [("tmp", r)], bias=ONE_AP[:, :], scale=-1.0)
                tt("pool", TM(ig), TM(ig), TM(xc), ALU.mult, [("tmp", ig), ("tmp", xc)], [("tmp", ig)])
                tt("dve", TM(ig), TM(ig), TM(r), ALU.mult, [("tmp", ig), ("tmp", r)], [("tmp", ig)])
                hs = tslot()
                P.op("dve", lambda e, hs=hs, a=a, ig=ig, c=c: e.tensor_tensor_scan(
                    out=TM(hs), data0=TM(a), data1=TM(ig), initial=carry_rg[:, l, c:c + 1],
                    op0=ALU.mult, op1=ALU.add),
                    reads=[("tmp", a), ("tmp", ig), ("carry_rg", l, c)], writes=[("tmp", hs)])
                cp("dve", carry_rg[:, l, c:c + 1], TM(hs, T - 1, 1), [("tmp", hs)], [("carry_rg", l, c)])
                tt("dve", act[:, c, :], TM(hs), act[:, 8 + c, :], ALU.mult, [("tmp", hs), ("act", 8 + c)], [("act", c)])

            if RG_PIPE:
                rg_a(0)
                for c in range(KC):
                    if c + 1 < KC:
                        rg_a(c + 1)
                    rg_b(c)
            else:
                for c in range(KC):
                    rg_a(c)
                    rg_b(c)
            release(g_x)
            release(g_a)
            slot_o, g_o = next_slab()
            wo = ring_view(slot_o, KC, 1024)
            for m in range(KC):
                pi = psn()
                for c in range(KC):
                    mm(pi, wo[:, c, 128 * m:128 * m + 128], act[:, c, :], c == 0, c == KC - 1, [("ring", slot_o), ("act", c)])
                tt("dve", h[:, m, :], h[:, m, :], ps[pi][:, :], ALU.add, [("h", m), ("ps", pi)], [("h", m)])
            release(g_o)

        def rg_slabs(l):
            slab_list.append(slab_cols("rg_w_in", l, KC, [(1024, 1024)]))
            slab_list.append(slab_cols("rg_w_in", l, KC, [(0, 1024)]))

            def fn(dst2d, l=l):
                dst = dst2d[:, 0:2048].rearrange("p (k n) -> p k n", k=16)
                return [(dst[:, 0:8, :], W["rg_w_a"][l].rearrange("h i j -> i h j")),
                        (dst[:, 8:16, :], W["rg_w_x"][l].rearrange("h i j -> i h j"))], "pool"
            slab_list.append(SlabDef(fn, size=2048))
            slab_list.append(slab_cols("rg_w_out", l, KC, [(0, 1024)]))


        I32 = mybir.dt.int32
        TWO_PI = 2.0 * math.pi

        def K5(j, idx, q0=0, n=NQ):
            return s5k[:, j, idx, q0:q0 + n]

        def range_reduce(eng_out_ap, z_ap, zi_ap, kf_ap, width_keys):
            C1_ = 6.28125
            C2_ = TWO_PI - C1_
            PI_LO = 3.1415925
            ts("dve", zi_ap, z_ap, 1.0 / TWO_PI, None, ALU.mult, None, width_keys, width_keys)
            cp("dve", kf_ap, zi_ap, width_keys, width_keys)
            stt(eng_out_ap, kf_ap, -C1_, z_ap, ALU.mult, ALU.add, width_keys, width_keys)
            stt(eng_out_ap, kf_ap, -C2_, eng_out_ap, ALU.mult, ALU.add, width_keys, width_keys)
            ts("dve", eng_out_ap, eng_out_ap, -PI_LO, PI_LO, ALU.max, ALU.min, width_keys, width_keys)

        def s5_prologue(j):
            PK = ["s5p"]
            cp("dve", K5(j, 0), cst[:, crow["s5_are"] + NQ * j:crow["s5_are"] + NQ * (j + 1)], ["cst"], PK)
            cp("dve", K5(j, 1), cst[:, crow["s5_aim"] + NQ * j:crow["s5_aim"] + NQ * (j + 1)], ["cst"], PK)
            dma("sp", ldrow[0:1, 0:64], W["s5_log_dt"][j:j + 1, :], "cld", [], PK)
            for q0 in range(0, NQ, 4):
                dma("sp", s5nat[:, 0, q0:q0 + 4, :], W["s5_b_re"][j].rearrange("(q p) c -> p q c", p=128)[:, q0:q0 + 4, :], "cld", [], PK)
                dma("sp", s5nat[:, 1, q0:q0 + 4, :], W["s5_b_im"][j].rearrange("(q p) c -> p q c", p=128)[:, q0:q0 + 4, :], "cld", [], PK)
            dma("sp", s5cn[:, 0, :, :], W["s5_c_re"][j].rearrange("(k r) n -> r k n", r=128), "cld", [], PK)
            dma("sp", s5cn[:, 1, :, :], W["s5_c_im"][j].rearrange("(k r) n -> r k n", r=128), "cld", [], PK)
            pi = psn()
            P.op("pe", lambda e: e.matmul(ps[pi][:, 0:64], lhsT=onesf[0:1, :], rhs=ldrow[0:1, 0:64], start=True, stop=True),
                 reads=PK + ["onesf"], writes=[("ps", pi)])
            pv = ps[pi][:, 0:64].rearrange("p (q g) -> p q g", g=2)
            cp("dve", s5k[0:64, j, 2, :], pv[0:64, :, 0], [("ps", pi)], PK)
            cp("dve", s5k[64:128, j, 2, :], pv[64:128, :, 1], [("ps", pi)], PK)
            actf(K5(j, 2), K5(j, 2), AF.Exp, PK, PK)
            tt("dve", K5(j, 12), K5(j, 0), K5(j, 2), ALU.mult, PK, PK)
            actf(K5(j, 3), K5(j, 12), AF.Exp, PK, PK)
            tt("dve", K5(j, 4), K5(j, 1), K5(j, 2), ALU.mult, PK, PK)
            zi = s5k[:, j, 13, :].bitcast(I32)
            range_reduce(K5(j, 12), K5(j, 4), zi, K5(j, 14), PK)
            actf(K5(j, 6), K5(j, 12), AF.Sin, PK, PK)
            ts("dve", K5(j, 15), K5(j, 4), math.pi / 2, None, ALU.add, None, PK, PK)
            range_reduce(K5(j, 12), K5(j, 15), zi, K5(j, 14), PK)
            actf(K5(j, 5), K5(j, 12), AF.Sin, PK, PK)
            tt("dve", K5(j, 5), K5(j, 5), K5(j, 3), ALU.mult, PK, PK)
            tt("dve", K5(j, 6), K5(j, 6), K5(j, 3), ALU.mult, PK, PK)
            tt("dve", K5(j, 12), K5(j, 0), K5(j, 0), ALU.mult, PK, PK)
            tt("dve", K5(j, 13), K5(j, 1), K5(j, 1), ALU.mult, PK, PK)
            tt("dve", K5(j, 9), K5(j, 12), K5(j, 13), ALU.add, PK, PK)
            P.op("dve", lambda e: e.reciprocal(out=K5(j, 9), in_=K5(j, 9)), reads=PK, writes=PK)
            ts("dve", K5(j, 10), K5(j, 5), -1.0, None, ALU.add, None, PK, PK)
            tt("dve", K5(j, 12), K5(j, 10), K5(j, 0), ALU.mult, PK, PK)
            tt("dve", K5(j, 13), K5(j, 6), K5(j, 1), ALU.mult, PK, PK)
            tt("dve", K5(j, 12), K5(j, 12), K5(j, 13), ALU.add, PK, PK)
            tt("dve", K5(j, 7), K5(j, 12), K5(j, 9), ALU.mult, PK, PK)
            tt("dve", K5(j, 12), K5(j, 6), K5(j, 0), ALU.mult, PK, PK)
            tt("dve", K5(j, 13), K5(j, 10), K5(j, 1), ALU.mult, PK, PK)
            tt("dve", K5(j, 12), K5(j, 12), K5(j, 13), ALU.subtract, PK, PK)
            tt("dve", K5(j, 8), K5(j, 12), K5(j, 9), ALU.mult, PK, PK)
            ts("dve", K5(j, 10), K5(j, 8), -1.0, None, ALU.mult, None, PK, PK)
            hflat = h[:, :, :].rearrange("p k t -> p (k t)")
            za = big16[:, 0:T]
            zb = big16[:, T:2 * T]
            zc = hflat[:, 0:T]
            zd = hflat[:, T:2 * T]
            ZK = ["zq"]
            for q in range(NQ):
                b_ = q % 2
                stg = tmp[:, 4256 + b_ * 2 * T:4256 + (b_ + 1) * 2 * T].rearrange("p (a t) -> p a t", a=2)
                SK = [("tabstg", b_)]
                ts("dve", za, tau[:, :], s5k[:, j, 4, q:q + 1], None, ALU.mult, None, PK + ZK + ["tau"], ZK)
                range_reduce(zd, za, zb.bitcast(I32), zc, ZK)
                actf(stg[:, 1, :], zd, AF.Sin, ZK, SK)
                ts("dve", za, za, math.pi / 2, None, ALU.add, None, ZK, ZK)
                range_reduce(zd, za, zb.bitcast(I32), zc, ZK)
                actf(stg[:, 0, :], zd, AF.Sin, ZK, SK)
                dma("sp", s5tab_d[j][q].rearrange("p (a t) -> p a t", a=2), stg, f"tbw{b_}", SK, [("s5tab", j)])
            if s5_stage < 1:
                return
            ta = tmp[:, 4224:4240]
            tb = tmp[:, 4240:4256]
            for q in range(NQ):
                k, r4 = divmod(q, 4)
                br = s5nat[:, 0, q, :]
                bi = s5nat[:, 1, q, :]
                ts("dve", ta, br, s5k[:, j, 7, q:q + 1], None, ALU.mult, None, PK, PK)
                stt(ta, bi, s5k[:, j, 10, q:q + 1], ta, ALU.mult, ALU.add, PK, PK)
                ts("dve", tb, bi, s5k[:, j, 7, q:q + 1], None, ALU.mult, None, PK, PK)
                stt(tb, br, s5k[:, j, 8, q:q + 1], tb, ALU.mult, ALU.add, PK, PK)
                FK = ["s5full"]
                memset("dve", s5full[:, :, :], 0.0, FK)
                for a_, src in ((0, ta), (1, tb)):
                    for gl in range(2):
                        ts("dve", s5full[:, a_, 32 * r4 + 16 * gl:32 * r4 + 16 * gl + 16], src, masks[:, 16 + gl:17 + gl], None,
                           ALU.mult, None, PK + FK + ["masks"], FK)
                pi = psn()
                tr(pi, slice(0, 128), s5full[:, 0, :], FK, last=False)
                tr(pi, slice(128, 256), s5full[:, 1, :], FK, last=True)
                cp("act", s5bcs[:, 0:2, :], ps[pi][:, 0:256].rearrange("p (a n) -> p a n", a=2), [("ps", pi)], ["s5bcs"])
                for a_ in range(2):
                    for gl in range(2):
                        mcol = (8 if a_ else 0) + 2 * r4 + gl
                        ts("dve", s5full[:, a_, 64 * gl:64 * gl + 64], s5cn[:, a_, k, :], masks[:, mcol:mcol + 1], None,
                           ALU.mult, None, PK + FK + ["masks"], FK)
                pi = psn()
                tr(pi, slice(0, 128), s5full[:, 0, :], FK, last=False)
                tr(pi, slice(128, 256), s5full[:, 1, :], FK, last=True)
                cp("act", s5bcs[:, 2:4, :], ps[pi][:, 0:256].rearrange("p (a n) -> p a n", a=2), [("ps", pi)], ["s5bcs"])
                dma("sp", s5bc_d[j][:, :, q * 128:(q + 1) * 128].rearrange("a p n -> p a n"), s5bcs[:, :, :], "scr",
                    ["s5bcs"], [("s5bc", j)])

        def s5_slabs(j):
            slab_list.append(slab_cols("s5_w_in", j, KC, [(0, 1024)]))
            for half in range(2):
                def fn(dst2d, j=j, half=half):
                    dst = dst2d[:, 0:8192].rearrange("p (k n) -> p k n", k=64)
                    src = s5bc_d[j][2 * half:2 * half + 2, :, :].rearrange("a p (q n) -> p a q n", q=NQ)
                    return [(dst[:, 0:32, :], src[:, 0, :, :]), (dst[:, 32:64, :], src[:, 1, :, :])], "pool"
                slab_list.append(SlabDef(fn, reads=[("s5bc", j)], direct=True))
            for i2 in range(2):
                slab_list.append(slab_cols("s5_w_glu", j, KC, [(512 * i2, 512), (1024 + 512 * i2, 512)]))
            slab_list.append(slab_cols("s5_w_out", j, KC, [(0, 1024)]))

        def phase_s5(j):
            slot_i, g_i = next_slab()
            wi_ = ring_view(slot_i, KC, 1024)
            for c in range(KC):
                pi = psn()
                for k in range(KC):
                    mm(pi, wi_[:, k, 128 * c:128 * c + 128], hn[:, k, :], k == 0, k == KC - 1, [("ring", slot_i), ("hn", k)])
                cp("act", u32[:, c, :], ps[pi][:, :], [("ps", pi)], [("big16", c)])
                cp("dve", sq[:, c, :], u32[:, c, :], [("big16", c)], [("sq", c)])
            release(g_i)
            slot_b, g_b = next_slab()
            wB = ring_view(slot_b, 64, 128)
            slot_c, g_c = next_slab()
            wC = ring_view(slot_c, 64, 128)
            if s5_stage in (2, 20, 21):
                pi = psn()
                mm(pi, wB[:, 0, :], sq[:, 0, :], True, True, [("ring", slot_b), ("sq", 0)])
                pi = psn()
                mm(pi, wC[:, 0, :], sq[:, 0, :], True, True, [("ring", slot_c), ("sq", 0)])
                release(g_b)
                release(g_c)
                for _ in range(3):
                    sl_, g_ = next_slab()
                    pi = psn()
                    mm(pi, ring[:, sl_, 0:128], sq[:, 0, :], True, True, [("ring", sl_), ("sq", 0)])
                    release(g_)
                return
            ps_mod[0] = 6
            ps_i[0] = 0
            qst = {}
            free = list(range(NTMP))

            def talloc():
                assert free, "S5 tmp slots exhausted"
                return free.pop(0)

            def tfree(*ids):
                free.extend(ids)

            def load_tab(q):
                slot = q % 3
                dma("sp", tabr[:, slot, :, :], s5tab_d[j][q].rearrange("p (a t) -> p a t", a=2), f"tb{slot}",
                    [("s5tab", j)], [("tabr", slot)])

            def s5_a(q):
                c = q // 4
                sl = q % 3
                Cq = tabr[:, sl, 0, :]
                Sq = tabr[:, sl, 1, :]
                TK = [("tabr", sl)]
                pbr = psn()
                mm(pbr, wB[:, q, :], sq[:, c, :], True, True, [("ring", slot_b), ("sq", c)])
                pbi = psn()
                mm(pbi, wB[:, 32 + q, :], sq[:, c, :], True, True, [("ring", slot_b), ("sq", c)])
                br_, bi_, t_, mr, u_, v_ = [talloc() for _ in range(6)]
                cp("act", TM(br_), ps[pbr][:, :], [("ps", pbr)], [("tmp", br_)])
                cp("act", TM(bi_), ps[pbi][:, :], [("ps", pbi)], [("tmp", bi_)])
                tt("dve", TM(t_), TM(br_), Cq, ALU.mult, [("tmp", br_)] + TK, [("tmp", t_)])
                tt("dve", TM(mr), TM(bi_), Sq, ALU.mult, [("tmp", bi_)] + TK, [("tmp", mr)])
                tt("dve", TM(mr), TM(mr), TM(t_), ALU.add, [("tmp", mr), ("tmp", t_)], [("tmp", mr)])
                tt(S5_ENG2, TM(u_), TM(bi_), Cq, ALU.mult, [("tmp", bi_)] + TK, [("tmp", u_)])
                tt(S5_ENG2, TM(v_), TM(br_), Sq, ALU.mult, [("tmp", br_)] + TK, [("tmp", v_)])
                tt(S5_ENG2, TM(u_), TM(u_), TM(v_), ALU.subtract, [("tmp", u_), ("tmp", v_)], [("tmp", u_)])
                qst[q] = (br_, bi_, t_, mr, u_, v_)

            def s5_b(q):
                c, qq = divmod(q, 4)
                py = 6 + (c % 2)
                sl = q % 3
                Cq = tabr[:, sl, 0, :]
                Sq = tabr[:, sl, 1, :]
                Cend = tabr[:, sl, 0, T - 1:T]
                Send = tabr[:, sl, 1, T - 1:T]
                TK = [("tabr", sl)]
                br_, bi_, t_, mr, mi, v_ = qst.pop(q)
                rho = s5k[:, j, 3, q:q + 1].to_broadcast([128, T])
                gr, gi = talloc(), talloc()
                cr = s5carry[:, j, 0, q:q + 1]
                ci = s5carry[:, j, 1, q:q + 1]
                CK = [("s5carry", q)]
                P.op("dve", lambda e, gr=gr, mr=mr, cr=cr, rho=rho: e.tensor_tensor_scan(
                    out=TM(gr), data0=rho, data1=TM(mr), initial=cr, op0=ALU.mult, op1=ALU.add),
                    reads=[("tmp", mr), "s5p"] + CK, writes=[("tmp", gr)])
                P.op("dve", lambda e, gi=gi, mi=mi, ci=ci, rho=rho: e.tensor_tensor_scan(
                    out=TM(gi), data0=rho, data1=TM(mi), initial=ci, op0=ALU.mult, op1=ALU.add),
                    reads=[("tmp", mi), "s5p"] + CK, writes=[("tmp", gi)])
                gre = TM(gr, T - 1, 1)
                gie = TM(gi, T - 1, 1)
                tA = s5k[:, j, 14, q:q + 1]
                tB = s5k[:, j, 15, q:q + 1]
                ts("dve", tA, gie, Send, None, ALU.mult, None, [("tmp", gi)] + TK, [("s5t", q)])
                ts("dve", tB, gre, Send, None, ALU.mult, None, [("tmp", gr)] + TK, [("s5t", q)])
                stt(cr, gre, Cend, tA, ALU.mult, ALU.subtract, [("tmp", gr), ("s5t", q)] + TK, CK)
                stt(ci, gie, Cend, tB, ALU.mult, ALU.add, [("tmp", gi), ("s5t", q)] + TK, CK)
                hb = q % 2
                tt("dve", TM(t_), TM(gr), Cq, ALU.mult, [("tmp", gr)] + TK, [("tmp", t_)])
                tt("dve", TM(mr), TM(gi), Sq, ALU.mult, [("tmp", gi)] + TK, [("tmp", mr)])
                tt("dve", hrb[:, hb, :], TM(t_), TM(mr), ALU.subtract, [("tmp", t_), ("tmp", mr)], [("hrb", hb)])
                tt(S5_ENG2, TM(v_), TM(gr), Sq, ALU.mult, [("tmp", gr)] + TK, [("tmp", v_)])
                tt(S5_ENG2, TM(br_), TM(gi), Cq, ALU.mult, [("tmp", gi)] + TK, [("tmp", br_)])
                tt(S5_ENG2, hib[:, hb, :], TM(v_), TM(br_), ALU.add, [("tmp", v_), ("tmp", br_)], [("hib", hb)])
                mm(py, wC[:, q, :], hrb[:, hb, :], qq == 0, False, [("ring", slot_c), ("hrb", hb)])
                mm(py, wC[:, 32 + q, :], hib[:, hb, :], False, qq == 3, [("ring", slot_c), ("hib", hb)])
                tfree(br_, bi_, t_, mr, mi, v_, gr, gi)
                if qq == 3:
                    yy = talloc()
                    tfree(yy)
                    cp("act", TM(yy), ps[py][:, :], [("ps", py)], [("tmp", yy)])
                    stt(TM(yy), u32[:, c, :], C1("s5_d", j * KC + c), TM(yy), ALU.mult, ALU.add,
                        [("big16", c), ("tmp", yy), "cst"], [("tmp", yy)])
                    actf(act[:, 8 + c, :], TM(yy), AF.Gelu_apprx_tanh, [("tmp", yy)], [("act", 8 + c)])

            load_tab(0)
            load_tab(1)
            if S5_PIPE:
                s5_a(0)
                for q in range(NQ):
                    if q + 2 < NQ:
                        load_tab(q + 2)
                    if q + 1 < NQ:
                        s5_a(q + 1)
                    s5_b(q)
            else:
                for q in range(NQ):
                    if q + 2 < NQ:
                        load_tab(q + 2)
                    s5_a(q)
                    s5_b(q)
            ps_mod[0] = 8
            ps_i[0] = 0
            tmp_i[0] = 0
            release(g_b)
            release(g_c)
            for i2 in range(2):
                slot_g, g_g = next_slab()
                wg = ring_view(slot_g, KC, 1024)
                for m4 in range(4):
                    m = 4 * i2 + m4
                    pv = psn()
                    for k in range(KC):
                        mm(pv, wg[:, k, 128 * m4:128 * m4 + 128], act[:, 8 + k, :], k == 0, k == KC - 1,
                           [("ring", slot_g), ("act", 8 + k)])
                    pg = psn()
                    for k in range(KC):
                        mm(pg, wg[:, k, 512 + 128 * m4:512 + 128 * m4 + 128], act[:, 8 + k, :], k == 0, k == KC - 1,
                           [("ring", slot_g), ("act", 8 + k)])
                    sg = tslot()
                    actf(TM(sg), ps[pg][:, :], AF.Sigmoid, [("ps", pg)], [("tmp", sg)])
                    tt("dve", act[:, 16 + m, :], ps[pv][:, :], TM(sg), ALU.mult, [("ps", pv), ("tmp", sg)], [("act", 16 + m)])
                release(g_g)
            slot_o, g_o = next_slab()
            wo = ring_view(slot_o, KC, 1024)
            for m in range(KC):
                pi = psn()
                for k in range(KC):
                    mm(pi, wo[:, k, 128 * m:128 * m + 128], act[:, 16 + k, :], k == 0, k == KC - 1,
                       [("ring", slot_o), ("act", 16 + k)])
                tt("dve", h[:, m, :], h[:, m, :], ps[pi][:, :], ALU.add, [("h", m), ("ps", pi)], [("h", m)])
            release(g_o)

        def phase_store(t):
            fs = [tslot() for _ in range(KC)]

            def o(k):
                return TM(fs[k]), ("tmp", fs[k])
            phase_norm("g_fin", 0, out_f32=o)
            for j in range(4):
                for hf in range(2):
                    pi = psn()
                    for kk in range(4):
                        k = 4 * hf + kk
                        tr(pi, slice(128 * kk, 128 * kk + 128), TM(fs[k], 128 * j, 128), [("tmp", fs[k])], last=(kk == 3))
                    cp("act" if hf else "dve", xin[:, j, 512 * hf:512 * hf + 512], ps[pi][:, :], [("ps", pi)],
                       [("big16", 2 * j + hf)])
            dma("sp", out_d[t * T:(t + 1) * T, :].rearrange("(j p) d -> p j d", p=128), xin, "ost",
                [("big16", i) for i in range(8)], [("outd", t)])

        EPS_AP = sb("eps_ap", [128, 1])
        ONE_AP = sb("one_ap", [128, 1])
        memset("dve", EPS_AP[:, :], EPS, ["eps"])
        memset("dve", ONE_AP[:, :], 1.0, ["one"])

        memset("dve", onesf[:, :], 1.0, ["onesf"])
        for gi_ in range(8):
            P.op("dve", lambda e, gi_=gi_: e.reduce_sum(out=masks[:, gi_:gi_ + 1], in_=ident[:, 16 * gi_:16 * gi_ + 16],
                                                        axis=mybir.AxisListType.X), reads=["ident"], writes=["masks"])
        for hf_ in range(2):
            P.op("dve", lambda e, hf_=hf_: e.reduce_sum(out=masks[:, 16 + hf_:17 + hf_], in_=ident[:, 64 * hf_:64 * hf_ + 64],
                                                        axis=mybir.AxisListType.X), reads=["ident"], writes=["masks"])
        ts("dve", masks[:, 8:16], masks[:, 0:8], -1.0, None, ALU.mult, None, ["masks"], ["masks"])
        if "s" in mix:
            for j_ in sorted({i // 2 for i in layers if i % 2 == 1}):
                if s5_stage != 21:
                    s5_prologue(j_)

        for i in layers:
            if i % 2 == 0 and "r" in mix:
                rg_slabs(i // 2)
            if i % 2 == 1 and "s" in mix and s5_stage >= 2:
                s5_slabs(i // 2)
            if "f" in mix:
                ffn_slabs(i)

        if USE_WSC:
            cast_all_slabs()
        P.barrier()
        for t in range(n_tiles):
            cur[0] = t
            phase_load(t)
            for i in layers:
                if i % 2 == 0 and "r" in mix:
                    phase_norm("g_mix", i * KC)
                    phase_rg(i // 2)
                if i % 2 == 1 and "s" in mix and s5_stage >= 2:
                    phase_norm("g_mix", i * KC)
                    phase_s5(i // 2)
                if "f" in mix:
                    phase_norm("g_ffn", i * KC)
                    phase_ffn(i)
            phase_store(t)
        P.final_wait("sp", [("outd", t) for t in range(n_tiles)] + [("dbg", i) for i in range(dbg_n[0])])
        P.dbg_names = dbg_names

        with nc.Block() as block:
            @block.sync
            def _(e):
                P.replay("sp", e, sems)

            @block.gpsimd
            def _(e):
                P.replay("pool", e, sems)

            @block.scalar
            def _(e):
                P.replay("act", e, sems)

            @block.vector
            def _(e):
                P.replay("dve", e, sems)

            @block.tensor
            def _(e):
                P.replay("pe", e, sems)
    return nc, P


def make_consts():
    return {"ident": np.eye(128, dtype=np.float32),
            "tau": np.tile(np.arange(1, T + 1, dtype=np.float32)[None, :], (128, 1))}


def shape_inputs(inputs):
    r = {}
    for k, v in inputs.items():
        if k == "x":
            continue
        v = np.ascontiguousarray(v, dtype=np.float32)
        if k == "norm_final_g":
            v = v.reshape(1, D)
        elif k in ("rg_b_a", "rg_b_x"):
            v = v.reshape(2, D)
        elif k in ("s5_a_re", "s5_a_im"):
            v = v.reshape(2, 4096)
        elif k in ("s5_b_re", "s5_b_im"):
            v = v.reshape(2, 4096, 16)
        elif k in ("s5_c_re", "s5_c_im"):
            v = v.reshape(2, 1024, 64)
        r[k] = v
    return r


def kernel(**inputs):
    x = np.ascontiguousarray(inputs["x"], dtype=np.float32)
    w = shape_inputs(inputs)
    w.update(make_consts())
    nc, _ = build_program()
    in_maps = [dict(w, x=x[b]) for b in range(BATCH)]
    res = run_bass_kernel_spmd(nc, in_maps, core_ids=list(range(BATCH)))
    return np.stack([res.results[b]["out"] for b in range(BATCH)], axis=0)
```

```python
import math
import numpy as np
import concourse.bass as bass
import concourse.mybir as mybir
from concourse.bass_utils import run_bass_kernel_spmd

F32 = mybir.dt.float32
BF16 = mybir.dt.bfloat16
AF = mybir.ActivationFunctionType
ALU = mybir.AluOpType

D = 1024
KC = 8
T = 512
SEQ = 8192
BATCH = 4
DEPTH = 4
DFF = 3072
NJ = 24
EPS = 1e-6
LSUB = 64
NQ = 32
ENGS = ("pe", "act", "dve", "pool", "sp")
SLAB = 8192
import os
S5_PIPE = os.environ.get('K_S5PIPE', '1') == '1'
S5_ENG2 = os.environ.get('K_S5ENG2', 'dve')
FFN_LOOK = int(os.environ.get('K_FFNLOOK', '2'))
RG_PIPE = os.environ.get('K_RGPIPE', '1') == '1'
USE_WSC = True
NSLOT = 3


class Prog:
    def __init__(self):
        self.prog = {e: [] for e in ENGS}
        self.count = {e: 0 for e in ENGS}
        self.dcount = {}
        self.seen = {e: {} for e in ENGS}
        self.last_write = {}
        self.readers = {}
        self.n_ops = 0
        self._rec = None

    def record(self, f):
        assert self._rec is None
        self._rec = []
        try:
            f()
            return self._rec
        finally:
            self._rec = None

    def emit(self, ops):
        for o in ops:
            self.op(*o)

    def _need(self, eng, waits, tok):
        if tok is None:
            return
        sk, v = tok
        if sk == eng and (eng in ("pe", "sp") or v > self.count[eng]):
            return
        if self.seen[eng].get(sk, 0) >= v:
            return
        if waits.get(sk, 0) < v:
            waits[sk] = v

    def op(self, eng, fn, reads=(), writes=(), track=True, dma=None, ninc=1):
        if self._rec is not None:
            self._rec.append((eng, fn, tuple(reads), tuple(writes), track, dma, ninc))
            return
        waits = {}
        for r in reads:
            self._need(eng, waits, self.last_write.get(r))
        for w in writes:
            self._need(eng, waits, self.last_write.get(w))
            for sk, v in self.readers.get(w, {}).items():
                self._need(eng, waits, (sk, v))
        for sk, v in waits.items():
            self.prog[eng].append(("wait", sk, v))
            self.seen[eng][sk] = v
        if dma is not None:
            prev = self.dcount.get(dma, 0)
            if prev > self.seen[eng].get(dma, 0) and (dma.startswith("cld") or dma.startswith("wcs")):
                self.prog[eng].append(("wait", dma, prev))
                self.seen[eng][dma] = prev
            self.dcount[dma] = self.dcount.get(dma, 0) + 16 * ninc
            tok = (dma, self.dcount[dma])
            self.prog[eng].append(("op", fn, dma, 16))
        elif track:
            self.count[eng] += 1
            tok = (eng, self.count[eng])
            self.prog[eng].append(("op", fn, eng, 1))
        else:
            tok = (eng, self.count[eng] + 1)
            self.prog[eng].append(("op", fn, None, 0))
        for w in writes:
            self.last_write[w] = tok
            self.readers[w] = {}
        for r in reads:
            d = self.readers.setdefault(r, {})
            if d.get(tok[0], 0) < tok[1]:
                d[tok[0]] = tok[1]
        self.n_ops += 1

    def barrier(self):
        snap = dict(self.count)
        dsnap = dict(self.dcount)
        for e in ENGS:
            for o, v in list(snap.items()) + list(dsnap.items()):
                if o.startswith("wcs"):
                    continue
                if o != e and v > 0 and self.seen[e].get(o, 0) < v:
                    self.prog[e].append(("wait", o, v))
                    self.seen[e][o] = v

    def final_wait(self, eng, keys):
        waits = {}
        for k in keys:
            self._need(eng, waits, self.last_write.get(k))
        for sk, v in waits.items():
            self.prog[eng].append(("wait", sk, v))
            self.seen[eng][sk] = v

    def replay(self, eng, handle, sems):
        for item in self.prog[eng]:
            if item[0] == "wait":
                handle.wait_ge(sems[item[1]], item[2])
            else:
                _, fn, sk, inc = item
                r = fn(handle)
                if sk is not None:
                    if isinstance(r, (list, tuple)):
                        for ins in r:
                            ins.then_inc(sems[sk], inc)
                    else:
                        r.then_inc(sems[sk], inc)


def build_program(n_tiles=SEQ // T, layers=(0, 1, 2, 3), seq=SEQ, mix="rsf", debug=0, s5_stage=4):
    nc = bass.Bass("TRN2", target_bir_lowering=False)
    P = Prog()

    def dram(name, shape, kind="ExternalInput", dt=F32):
        return nc.dram_tensor(name, list(shape), dt, kind=kind).ap()

    x_d = dram("x", [seq, D])
    out_d = dram("out", [seq, D], kind="ExternalOutput")
    ident_d = dram("ident", [128, 128])
    tau_d = dram("tau", [128, T])
    s5tab_d = [dram(f"s5tab{j}", [NQ, 128, 2 * T], kind="Internal") for j in range(2)]
    W = {}
    for name, shape in [
        ("norm_mix_g", [4, D]), ("norm_ffn_g", [4, D]), ("norm_final_g", [1, D]),
        ("rg_w_in", [2, D, 2 * D]), ("rg_conv_w", [2, 4, D]), ("rg_conv_b", [2, D]),
        ("rg_w_a", [2, 8, 128, 128]), ("rg_b_a", [2, D]), ("rg_w_x", [2, 8, 128, 128]),
        ("rg_b_x", [2, D]), ("rg_lambda", [2, D]), ("rg_w_out", [2, D, D]),
        ("s5_w_in", [2, D, D]), ("s5_a_re", [2, 4096]), ("s5_a_im", [2, 4096]),
        ("s5_log_dt", [2, 64]), ("s5_b_re", [2, 4096, 16]), ("s5_b_im", [2, 4096, 16]),
        ("s5_c_re", [2, 1024, 64]), ("s5_c_im", [2, 1024, 64]), ("s5_d", [2, D]),
        ("s5_w_glu", [2, D, 2 * D]), ("s5_w_out", [2, D, D]),
        ("ffn_w_up", [4, D, 2 * DFF]), ("ffn_conv_w", [4, 3, 2 * DFF]),
        ("ffn_conv_b", [4, 2 * DFF]), ("ffn_w_down", [4, DFF, D]),
    ]:
        W[name] = dram(name, shape)
    s5bc_d = [dram(f"s5bc{j}", [4, 128, NQ * 128], kind="Internal", dt=BF16) for j in range(2)]

    dbg_d = dram("dbg", [max(debug, 1), 128, T], kind="ExternalOutput") if debug else None
    dbg_n = [0]
    cur = [0]
    dbg_names = []

    import contextlib
    es = contextlib.ExitStack()
    with es:
        def sb(name, shape, dt=F32):
            return es.enter_context(nc.sbuf_tensor(name, list(shape), dt))

        big16 = sb("big16", [128, 4096])
        xin = big16[:, :].rearrange("p (j d) -> p j d", j=4)
        u32 = big16[:, :].rearrange("p (k t) -> p k t", k=KC)
        h = sb("h", [128, KC, T])
        sq = sb("sq", [128, KC, T], BF16)
        hn = sb("hn", [128, KC, T], BF16)
        rstd = sb("rstd", [128, T])
        ring = sb("ring", [128, NSLOT, SLAB], BF16)
        act = sb("act", [128, NJ, T], BF16)
        NTMP = 16
        TW = T + 4
        tmp = sb("tmp", [128, NTMP * TW])
        xcb = sb("xcb", [128, 2, T], BF16)
        hrb = sb("hrb", [128, 2, T], BF16)
        hib = sb("hib", [128, 2, T], BF16)
        ident = sb("ident_sb", [128, 128])
        identb = sb("identb", [128, 128], BF16)
        onesb = sb("onesb", [128, 128], BF16)
        tau = sb("tau_sb", [128, T])
        tabr = sb("tabr", [128, 3, 2, T])
        NCONST = 1152
        cst = sb("cst", [128, NCONST])
        stage = tmp[:, 0:1152].rearrange("p (b n) -> p b n", b=9)
        rgc = sb("rgc", [128, 2, 2, KC])
        halo_rg = sb("halo_rg", [128, 2, KC, 3])
        carry_rg = sb("carry_rg", [128, 2, KC])
        halo_ffn = sb("halo_ffn", [128, 4, 2 * NJ, 2], BF16)
        NUPB = 6
        upb = sb("upb", [128, NUPB, T + 2], BF16)
        dgs = sb("dgs", [128, NUPB, 3, 128], BF16)
        upb_i = [0]
        s5k = sb("s5k", [128, 2, 16, NQ])
        s5carry = sb("s5carry", [128, 2, 2, NQ])
        s5nat = tmp[:, 1152:1152 + 2048].rearrange("p (a q c) -> p a q c", a=4, q=NQ)
        s5cn = tmp[:, 3200:3200 + 1024].rearrange("p (a k n) -> p a k n", a=2, k=KC)
        s5full = sb("s5full", [128, 2, 128])
        s5bcs = sb("s5bcs", [128, 4, 128], BF16)
        masks = sb("masks", [128, 20])
        onesf = sb("onesf", [128, 128])
        ldrow = sb("ldrow", [1, 128])

        ps = [es.enter_context(nc.psum_tensor(f"ps{i}", [128, T], F32)) for i in range(8)]
        ps_i = [0]
        ps_mod = [8]

        def psn():
            i = ps_i[0] % ps_mod[0]
            ps_i[0] = (i + 1) % ps_mod[0]
            return i

        NCLD = 8
        sem_names = list(ENGS) + [f"w{s}" for s in range(NSLOT)] + [f"cld{i}" for i in range(NCLD)] + ["xld", "ost", "scr", "tb0", "tb1", "tb2", "tbw0", "tbw1"]
        cld_i = [0]
        NWCS = 8
        wcs_i = [0]
        sem_names += [f"wcs{i_}" for i_ in range(NWCS)]
        sems = {n: es.enter_context(nc.semaphore(n)) for n in sem_names}

        def mm(pi, lhsT, rhs, start, stop, reads, cols=slice(0, T)):
            P.op("pe", lambda e: e.matmul(ps[pi][:, cols], lhsT=lhsT, rhs=rhs, start=start, stop=stop),
                 reads=reads, writes=[("ps", pi)], track=stop)

        def tr(pi, cols, in_, reads, last=True):
            P.op("pe", lambda e: e.transpose(ps[pi][:, cols], in_, ident[:, :]),
                 reads=list(reads) + ["ident"], writes=[("ps", pi)], track=last)

        def actf(out, in_, func, reads, writes, bias=None, scale=None):
            kw = {}
            if bias is not None:
                kw["bias"] = bias
            if scale is not None:
                kw["scale"] = scale
            P.op("act", lambda e: e.activation(out=out, in_=in_, func=func, **kw), reads=reads, writes=writes)

        def tt(eng, out, in0, in1, op, reads, writes):
            P.op(eng, lambda e: e.tensor_tensor(out=out, in0=in0, in1=in1, op=op), reads=reads, writes=writes)

        def ts(eng, out, in0, s1, s2, op0, op1, reads, writes):
            if op1 is None:
                P.op(eng, lambda e: e.tensor_scalar(out=out, in0=in0, scalar1=s1, scalar2=None, op0=op0),
                     reads=reads, writes=writes)
            else:
                P.op(eng, lambda e: e.tensor_scalar(out=out, in0=in0, scalar1=s1, scalar2=s2, op0=op0, op1=op1),
                     reads=reads, writes=writes)

        def stt(out, in0, scalar, in1, op0, op1, reads, writes):
            P.op("dve", lambda e: e.scalar_tensor_tensor(out=out, in0=in0, scalar=scalar, in1=in1, op0=op0, op1=op1),
                 reads=reads, writes=writes)

        def cp(eng, out, in_, reads, writes):
            if eng == "act":
                P.op("act", lambda e: e.copy(out=out, in_=in_), reads=reads, writes=writes)
            else:
                P.op(eng, lambda e: e.tensor_copy(out=out, in_=in_), reads=reads, writes=writes)

        def memset(eng, ap, val, writes):
            P.op(eng, lambda e: e.memset(ap, val), writes=writes)

        def dma(eng, out, in_, sem, reads, writes, slow=False):
            if sem == "cld":
                sem = f"cld{cld_i[0]}"
                cld_i[0] = (cld_i[0] + 1) % NCLD
            if sem == "wcs":
                sem = f"wcs{wcs_i[0]}"
                wcs_i[0] = (wcs_i[0] + 1) % NWCS
            kw = {"allow_slow_non_contiguous": True} if slow else {}
            P.op(eng, lambda e: e.dma_start(out=out, in_=in_, **kw), reads=reads, writes=writes, dma=sem)

        def dbg(name, ap, key):
            if not debug or dbg_n[0] >= debug or cur[0] != n_tiles - 1:
                return
            i = dbg_n[0]
            dbg_n[0] += 1
            dbg_names.append(name)
            dma("sp", dbg_d[i], ap, "scr", [key], [("dbg", i)])

        tmp_i = [0]

        def tslot():
            i = tmp_i[0]
            tmp_i[0] = (i + 1) % NTMP
            return i

        def TM(i, c0=0, n=T):
            return tmp[:, i * TW + c0:i * TW + c0 + n]

        slab_list = []
        slab_state = {"issued": 0, "used": 0}
        PF = NSLOT - 1

        wsc_box = [None]

        def issue_slab(gidx):
            idx = gidx % len(slab_list)
            slot = gidx % NSLOT
            sd = slab_list[idx]
            if sd.direct or not USE_WSC:
                pairs, q = sd(ring[:, slot, :])
                if USE_WSC:
                    q = "sp"

                def fn(e, pairs=pairs):
                    return [e.dma_start(out=o, in_=i) for (o, i) in pairs]
                P.op(q, fn, reads=[r for r in sd.reads], writes=[("ring", slot)],
                     dma=f"w{slot}", ninc=len(pairs))
            else:
                n = sd.size
                src = wsc_box[0][idx, :, 0:n]
                P.op("sp", lambda e, slot=slot, n=n, src=src: e.dma_start(out=ring[:, slot, 0:n], in_=src),
                     reads=[("wsc", idx)], writes=[("ring", slot)], dma=f"w{slot}", ninc=1)

        def cast_all_slabs():
            wsc_box[0] = dram("wsc", [len(slab_list), 128, SLAB], kind="Internal", dt=BF16)
            for idx, sd in enumerate(slab_list):
                if sd.direct:
                    continue
                pairs, _ = sd(wsc_box[0][idx])
                for (o, i) in pairs:
                    dma("pool", o, i, "wcs", [], [("wsc", idx)])

        released = set()

        def pump():
            total = len(slab_list) * n_tiles
            while slab_state["issued"] < min(slab_state["used"] + PF + 1, total):
                n = slab_state["issued"]
                if n >= NSLOT and (n - NSLOT) not in released:
                    break
                issue_slab(n)
                slab_state["issued"] += 1

        def next_slab():
            g = slab_state["used"]
            slab_state["used"] += 1
            pump()
            assert slab_state["issued"] > g, "slab ring deadlock: too many live slabs"
            return g % NSLOT, g

        def release(g):
            released.add(g)
            pump()

        class SlabDef:
            def __init__(self, fn, reads=(), direct=False, size=SLAB):
                self.fn = fn
                self.reads = reads
                self.direct = direct
                self.size = size

            def __call__(self, dst2d):
                return self.fn(dst2d)

        def slab_cols(wname, l, kc, col_sets, q="pool"):
            ntot = sum(n for _, n in col_sets)

            def fn(dst2d):
                dst = dst2d[:, 0:kc * ntot].rearrange("p (k n) -> p k n", k=kc)
                src = W[wname][l].rearrange("(k p) n -> p k n", p=128)
                pairs = []
                o = 0
                for c0, n in col_sets:
                    pairs.append((dst[:, :, o:o + n], src[:, :, c0:c0 + n]))
                    o += n
                return pairs, q
            return SlabDef(fn, size=kc * ntot)

        def ring_view(slot, kc, n):
            return ring[:, slot, 0:kc * n].rearrange("p (k n) -> p k n", k=kc)

        crow = {}
        vec_list = []

        def addvec(name, ap2d):
            crow[name] = sum(v[1] for v in vec_list)
            vec_list.append((name, ap2d.shape[0], ap2d))

        addvec("g_mix", W["norm_mix_g"].rearrange("l (k p) -> (l k) p", p=128))
        addvec("g_ffn", W["norm_ffn_g"].rearrange("l (k p) -> (l k) p", p=128))
        addvec("g_fin", W["norm_final_g"].rearrange("l (k p) -> (l k) p", p=128))
        addvec("rg_cw", W["rg_conv_w"].rearrange("l t (k p) -> (l t k) p", p=128))
        addvec("rg_cb", W["rg_conv_b"].rearrange("l (k p) -> (l k) p", p=128))
        addvec("rg_ba", W["rg_b_a"].rearrange("l (k p) -> (l k) p", p=128))
        addvec("rg_bx", W["rg_b_x"].rearrange("l (k p) -> (l k) p", p=128))
        addvec("rg_lam", W["rg_lambda"].rearrange("l (k p) -> (l k) p", p=128))
        addvec("s5_d", W["s5_d"].rearrange("l (k p) -> (l k) p", p=128))
        addvec("ffn_cw", W["ffn_conv_w"].rearrange("l t (k p) -> (l t k) p", p=128))
        addvec("ffn_cb", W["ffn_conv_b"].rearrange("l (k p) -> (l k) p", p=128))
        addvec("s5_are", W["s5_a_re"].rearrange("l (q p) -> (l q) p", p=128))
        addvec("s5_aim", W["s5_a_im"].rearrange("l (q p) -> (l q) p", p=128))
        nrows = sum(v[1] for v in vec_list)
        assert nrows <= NCONST, nrows

        def C1(name, idx):
            c = crow[name] + idx
            return cst[:, c:c + 1]

        dma("sp", ident[:, :], ident_d[:, :], "cld", [], ["ident"])
        dma("sp", tau[:, :], tau_d[:, :], "cld", [], ["tau"])
        cp("dve", identb[:, :], ident[:, :], ["ident"], ["identb"])
        memset("dve", onesb[:, :], 1.0, ["onesb"])
        memset("dve", halo_rg[:, :, :, :], 0.0, [("halo_rg", l_, c_) for l_ in range(2) for c_ in range(KC)])
        memset("dve", carry_rg[:, :, :], 0.0, [("carry_rg", l_, c_) for l_ in range(2) for c_ in range(KC)])
        memset("dve", halo_ffn[:, :, :, :], 0.0, [("halo_ffn", i_, c_) for i_ in range(4) for c_ in range(2 * NJ)])
        memset("dve", s5carry[:, :, :, :], 0.0, ["s5carry"])
        nblk = (nrows + 127) // 128
        memset("dve", stage[:, :, :], 0.0, [("stage", b) for b in range(nblk)])
        r0 = 0
        for name, n, ap2d in vec_list:
            done = 0
            while done < n:
                blk, off = divmod(r0 + done, 128)
                take = min(n - done, 128 - off)
                dma("sp", stage[off:off + take, blk, :], ap2d[done:done + take, :], "cld", [], [("stage", blk)])
                done += take
            r0 += n
        for blk in range(nblk):
            pi = psn()
            tr(pi, slice(0, 128), stage[:, blk, :], [("stage", blk)])
            cp("dve", cst[:, blk * 128:(blk + 1) * 128], ps[pi][:, 0:128], [("ps", pi)], ["cst"])
        for l in range(2):
            lam = cst[:, crow["rg_lam"] + l * KC: crow["rg_lam"] + (l + 1) * KC]
            actf(rgc[:, l, 0, :], lam, AF.Exp, ["cst"], ["rgc"], scale=-1.0)
            actf(rgc[:, l, 0, :], rgc[:, l, 0, :], AF.Ln, ["rgc"], ["rgc"], bias=1.0)
            ts("dve", rgc[:, l, 1, :], rgc[:, l, 0, :], -16.0, None, ALU.mult, None, ["rgc"], ["rgc"])
            ts("dve", rgc[:, l, 0, :], rgc[:, l, 0, :], -8.0, None, ALU.mult, None, ["rgc"], ["rgc"])

        def phase_load(t):
            dma("sp", xin, x_d[t * T:(t + 1) * T, :].rearrange("(j p) d -> p j d", p=128), "xld",
                [], [("big16", i) for i in range(8)])
            for k in range(KC):
                pi = psn()
                for j in range(4):
                    tr(pi, slice(128 * j, 128 * j + 128), xin[:, j, 128 * k:128 * k + 128],
                       [("big16", 2 * j), ("big16", 2 * j + 1)], last=(j == 3))
                cp("act" if k % 2 else "dve", h[:, k, :], ps[pi][:, :], [("ps", pi)], [("h", k)])

        def phase_norm(gname, gidx0, out_f32=None):
            for k in range(KC):
                actf(sq[:, k, :], h[:, k, :], AF.Square, [("h", k)], [("sq", k)])
            pi = psn()
            for k in range(KC):
                mm(pi, onesb[:, :], sq[:, k, :], k == 0, k == KC - 1, ["onesb", ("sq", k)])
            actf(rstd[:, :], ps[pi][:, :], AF.Sqrt, [("ps", pi), "eps"], ["rstd"], bias=EPS_AP[:, :], scale=1.0 / D)
            P.op("dve", lambda e: e.reciprocal(out=rstd[:, :], in_=rstd[:, :]), reads=["rstd"], writes=["rstd"])
            for k in range(KC):
                if out_f32 is None:
                    stt(hn[:, k, :], h[:, k, :], C1(gname, gidx0 + k), rstd[:, :], ALU.mult, ALU.mult,
                        [("h", k), "rstd", "cst"], [("hn", k)])
                else:
                    o, key = out_f32(k)
                    stt(o, h[:, k, :], C1(gname, gidx0 + k), rstd[:, :], ALU.mult, ALU.mult,
                        [("h", k), "rstd", "cst"], [key])

        def phase_ffn(i):
            cw0 = crow["ffn_cw"] + i * 3 * 48
            cb0 = crow["ffn_cb"] + i * 48
            slabs = {}
            st = {}

            def get_slab(si):
                if si not in slabs:
                    slabs[si] = next_slab()
                return slabs[si]

            def stage_a(hx):
                jg, half = divmod(hx, 2)
                si, cc = divmod(jg, 4)
                slot, gsl = get_slab(si)
                wv = ring_view(slot, KC, 1024)
                ch = jg + NJ * half
                pi = psn()
                for k in range(KC):
                    mm(pi, wv[:, k, 512 * half + 128 * cc: 512 * half + 128 * cc + 128], hn[:, k, :],
                       k == 0, k == KC - 1, [("ring", slot), ("hn", k)])
                if hx % 8 == 7:
                    release(gsl)
                ub_i = upb_i[0]
                upb_i[0] = (ub_i + 1) % NUPB
                cp("pool", upb[:, ub_i, 0:2], halo_ffn[:, i, ch, :], [("halo_ffn", i, ch)], [("upbh", ub_i)])
                cp("act", upb[:, ub_i, 2:2 + T], ps[pi][:, :], [("ps", pi)], [("upb", ub_i)])
                cp("pool", halo_ffn[:, i, ch, :], upb[:, ub_i, T:T + 2], [("upb", ub_i)], [("halo_ffn", i, ch)])
                for t3 in range(3):
                    actf(dgs[:, ub_i, t3, :], identb[:, :], AF.Copy, ["identb", "cst"], [("dgs", ub_i, t3)],
                         scale=cst[:, cw0 + 48 * t3 + ch:cw0 + 48 * t3 + ch + 1])
                st[hx] = ub_i

            def stage_b(hx):
                jg, half = divmod(hx, 2)
                ub_i = st.pop(hx)
                pc = psn()
                for t3 in range(3):
                    mm(pc, dgs[:, ub_i, t3, :], upb[:, ub_i, t3:t3 + T], t3 == 0, t3 == 2,
                       [("dgs", ub_i, t3), ("upb", ub_i), ("upbh", ub_i)])
                if half == 0:
                    g = tslot()
                    actf(TM(g), ps[pc][:, :], AF.Gelu_apprx_tanh, [("ps", pc), "cst"], [("tmp", g)],
                         bias=cst[:, cb0 + jg:cb0 + jg + 1])
                    st[("g", jg)] = g
                else:
                    g = st.pop(("g", jg))
                    stt(act[:, jg, :], ps[pc][:, :], cst[:, cb0 + NJ + jg:cb0 + NJ + jg + 1], TM(g), ALU.add, ALU.mult,
                        [("ps", pc), ("tmp", g), "cst"], [("act", jg)])

            NH = 2 * NJ
            LOOK = FFN_LOOK
            for hx in range(min(LOOK, NH)):
                stage_a(hx)
            for hx in range(NH):
                if hx + LOOK < NH:
                    stage_a(hx + LOOK)
                stage_b(hx)
            for s in range(4):
                slot, gsl = next_slab()
                wv = ring_view(slot, NJ, 256)
                for mh in range(2):
                    m = 2 * s + mh
                    pi = psn()
                    for j in range(NJ):
                        mm(pi, wv[:, j, 128 * mh:128 * mh + 128], act[:, j, :], j == 0, j == NJ - 1,
                           [("ring", slot), ("act", j)])
                    tt("dve", h[:, m, :], h[:, m, :], ps[pi][:, :], ALU.add, [("h", m), ("ps", pi)], [("h", m)])
                release(gsl)

        def ffn_slabs(i):
            for s in range(6):
                slab_list.append(slab_cols("ffn_w_up", i, KC, [(512 * s, 512), (DFF + 512 * s, 512)]))
            for s in range(4):
                slab_list.append(slab_cols("ffn_w_down", i, NJ, [(256 * s, 256)]))

        def phase_rg(l):
            slot_g, g_g = next_slab()
            wg = ring_view(slot_g, KC, 1024)
            for c in range(KC):
                pg = psn()
                for k in range(KC):
                    mm(pg, wg[:, k, 128 * c:128 * c + 128], hn[:, k, :], k == 0, k == KC - 1, [("ring", slot_g), ("hn", k)])
                actf(act[:, 8 + c, :], ps[pg][:, :], AF.Gelu_apprx_tanh, [("ps", pg)], [("act", 8 + c)])
            release(g_g)
            slot_x, g_x = next_slab()
            wx = ring_view(slot_x, KC, 1024)
            slot_a, g_a = next_slab()
            wa = ring_view(slot_a, 16, 128)
            ust = {}

            def rg_a(c):
                pi = psn()
                for k in range(KC):
                    mm(pi, wx[:, k, 128 * c:128 * c + 128], hn[:, k, :], k == 0, k == KC - 1, [("ring", slot_x), ("hn", k)])
                u = tslot()
                cp("pool", TM(u, 0, 3), halo_rg[:, l, c, :], [("halo_rg", l, c)], [("tmp", u)])
                cp("act", TM(u, 3, T), ps[pi][:, :], [("ps", pi)], [("tmp", u)])
                cp("pool", halo_rg[:, l, c, :], TM(u, T, 3), [("tmp", u)], [("halo_rg", l, c)])
                ust[c] = u

            def rg_b(c):
                u = ust.pop(c)
                xc = tslot()
                ts("dve", TM(xc), TM(u, 0, T), C1("rg_cw", l * 32 + 0 * KC + c), C1("rg_cb", l * KC + c),
                   ALU.mult, ALU.add, [("tmp", u), "cst"], [("tmp", xc)])
                for k in range(1, 4):
                    stt(TM(xc), TM(u, k, T), C1("rg_cw", l * 32 + k * KC + c), TM(xc), ALU.mult, ALU.add,
                        [("tmp", u), ("tmp", xc), "cst"], [("tmp", xc)])
                xb = c % 2
                xbv = xcb[:, xb, :]
                cp("pool", xbv, TM(xc), [("tmp", xc)], [("xcb", xb)])
                pa = psn()
                mm(pa, wa[:, c, :], xbv, True, True, [("ring", slot_a), ("xcb", xb)])
                px = psn()
                mm(px, wa[:, 8 + c, :], xbv, True, True, [("ring", slot_a), ("xcb", xb)])
                r = tslot()
                actf(TM(r), ps[pa][:, :], AF.Sigmoid, [("ps", pa), "cst"], [("tmp", r)], bias=C1("rg_ba", l * KC + c))
                ig = tslot()
                actf(TM(ig), ps[px][:, :], AF.Sigmoid, [("ps", px), "cst"], [("tmp", ig)], bias=C1("rg_bx", l * KC + c))
                a = tslot()
                actf(TM(a), TM(r), AF.Exp, [("tmp", r), "rgc"], [("tmp", a)], scale=rgc[:, l, 0, c:c + 1])
                actf(TM(r), TM(r), AF.Exp, [("tmp", r), "rgc"], [("tmp", r)], scale=rgc[:, l, 1, c:c + 1])
                actf(TM(r), TM(r), AF.Sqrt, [("tmp", r), "one"], [("tmp", r)], bias=ONE_AP[:, :], scale=-1.0)
                tt("pool", TM(ig), TM(ig), TM(xc), ALU.mult, [("tmp", ig), ("tmp", xc)], [("tmp", ig)])
                tt("dve", TM(ig), TM(ig), TM(r), ALU.mult, [("tmp", ig), ("tmp", r)], [("tmp", ig)])
                hs = tslot()
                P.op("dve", lambda e, hs=hs, a=a, ig=ig, c=c: e.tensor_tensor_scan(
                    out=TM(hs), data0=TM(a), data1=TM(ig), initial=carry_rg[:, l, c:c + 1],
                    op0=ALU.mult, op1=ALU.add),
                    reads=[("tmp", a), ("tmp", ig), ("carry_rg", l, c)], writes=[("tmp", hs)])
                cp("dve", carry_rg[:, l, c:c + 1], TM(hs, T - 1, 1), [("tmp", hs)], [("carry_rg", l, c)])
                tt("dve", act[:, c, :], TM(hs), act[:, 8 + c, :], ALU.mult, [("tmp", hs), ("act", 8 + c)], [("act", c)])

            if RG_PIPE:
                rg_a(0)
                for c in range(KC):
                    if c + 1 < KC:
                        rg_a(c + 1)
                    rg_b(c)
            else:
                for c in range(KC):
                    rg_a(c)
                    rg_b(c)
            release(g_x)
            release(g_a)
            slot_o, g_o = next_slab()
            wo = ring_view(slot_o, KC, 1024)
            for m in range(KC):
                pi = psn()
                for c in range(KC):
                    mm(pi, wo[:, c, 128 * m:128 * m + 128], act[:, c, :], c == 0, c == KC - 1, [("ring", slot_o), ("act", c)])
                tt("dve", h[:, m, :], h[:, m, :], ps[pi][:, :], ALU.add, [("h", m), ("ps", pi)], [("h", m)])
            release(g_o)

        def rg_slabs(l):
            slab_list.append(slab_cols("rg_w_in", l, KC, [(1024, 1024)]))
            slab_list.append(slab_cols("rg_w_in", l, KC, [(0, 1024)]))

            def fn(dst2d, l=l):
                dst = dst2d[:, 0:2048].rearrange("p (k n) -> p k n", k=16)
                return [(dst[:, 0:8, :], W["rg_w_a"][l].rearrange("h i j -> i h j")),
                        (dst[:, 8:16, :], W["rg_w_x"][l].rearrange("h i j -> i h j"))], "pool"
            slab_list.append(SlabDef(fn, size=2048))
            slab_list.append(slab_cols("rg_w_out", l, KC, [(0, 1024)]))


        I32 = mybir.dt.int32
        TWO_PI = 2.0 * math.pi

        def K5(j, idx, q0=0, n=NQ):
            return s5k[:, j, idx, q0:q0 + n]

        def range_reduce(eng_out_ap, z_ap, zi_ap, kf_ap, width_keys):
            C1_ = 6.28125
            C2_ = TWO_PI - C1_
            PI_LO = 3.1415925
            ts("dve", zi_ap, z_ap, 1.0 / TWO_PI, None, ALU.mult, None, width_keys, width_keys)
            cp("dve", kf_ap, zi_ap, width_keys, width_keys)
            stt(eng_out_ap, kf_ap, -C1_, z_ap, ALU.mult, ALU.add, width_keys, width_keys)
            stt(eng_out_ap, kf_ap, -C2_, eng_out_ap, ALU.mult, ALU.add, width_keys, width_keys)
            ts("dve", eng_out_ap, eng_out_ap, -PI_LO, PI_LO, ALU.max, ALU.min, width_keys, width_keys)

        def s5_prologue(j):
            PK = ["s5p"]
            cp("dve", K5(j, 0), cst[:, crow["s5_are"] + NQ * j:crow["s5_are"] + NQ * (j + 1)], ["cst"], PK)
            cp("dve", K5(j, 1), cst[:, crow["s5_aim"] + NQ * j:crow["s5_aim"] + NQ * (j + 1)], ["cst"], PK)
            dma("sp", ldrow[0:1, 0:64], W["s5_log_dt"][j:j + 1, :], "cld", [], PK)
            for q0 in range(0, NQ, 4):
                dma("sp", s5nat[:, 0, q0:q0 + 4, :], W["s5_b_re"][j].rearrange("(q p) c -> p q c", p=128)[:, q0:q0 + 4, :], "cld", [], PK)
                dma("sp", s5nat[:, 1, q0:q0 + 4, :], W["s5_b_im"][j].rearrange("(q p) c -> p q c", p=128)[:, q0:q0 + 4, :], "cld", [], PK)
            dma("sp", s5cn[:, 0, :, :], W["s5_c_re"][j].rearrange("(k r) n -> r k n", r=128), "cld", [], PK)
            dma("sp", s5cn[:, 1, :, :], W["s5_c_im"][j].rearrange("(k r) n -> r k n", r=128), "cld", [], PK)
            pi = psn()
            P.op("pe", lambda e: e.matmul(ps[pi][:, 0:64], lhsT=onesf[0:1, :], rhs=ldrow[0:1, 0:64], start=True, stop=True),
                 reads=PK + ["onesf"], writes=[("ps", pi)])
            pv = ps[pi][:, 0:64].rearrange("p (q g) -> p q g", g=2)
            cp("dve", s5k[0:64, j, 2, :], pv[0:64, :, 0], [("ps", pi)], PK)
            cp("dve", s5k[64:128, j, 2, :], pv[64:128, :, 1], [("ps", pi)], PK)
            actf(K5(j, 2), K5(j, 2), AF.Exp, PK, PK)
            tt("dve", K5(j, 12), K5(j, 0), K5(j, 2), ALU.mult, PK, PK)
            actf(K5(j, 3), K5(j, 12), AF.Exp, PK, PK)
            tt("dve", K5(j, 4), K5(j, 1), K5(j, 2), ALU.mult, PK, PK)
            zi = s5k[:, j, 13, :].bitcast(I32)
            range_reduce(K5(j, 12), K5(j, 4), zi, K5(j, 14), PK)
            actf(K5(j, 6), K5(j, 12), AF.Sin, PK, PK)
            ts("dve", K5(j, 15), K5(j, 4), math.pi / 2, None, ALU.add, None, PK, PK)
            range_reduce(K5(j, 12), K5(j, 15), zi, K5(j, 14), PK)
            actf(K5(j, 5), K5(j, 12), AF.Sin, PK, PK)
            tt("dve", K5(j, 5), K5(j, 5), K5(j, 3), ALU.mult, PK, PK)
            tt("dve", K5(j, 6), K5(j, 6), K5(j, 3), ALU.mult, PK, PK)
            tt("dve", K5(j, 12), K5(j, 0), K5(j, 0), ALU.mult, PK, PK)
            tt("dve", K5(j, 13), K5(j, 1), K5(j, 1), ALU.mult, PK, PK)
            tt("dve", K5(j, 9), K5(j, 12), K5(j, 13), ALU.add, PK, PK)
            P.op("dve", lambda e: e.reciprocal(out=K5(j, 9), in_=K5(j, 9)), reads=PK, writes=PK)
            ts("dve", K5(j, 10), K5(j, 5), -1.0, None, ALU.add, None, PK, PK)
            tt("dve", K5(j, 12), K5(j, 10), K5(j, 0), ALU.mult, PK, PK)
            tt("dve", K5(j, 13), K5(j, 6), K5(j, 1), ALU.mult, PK, PK)
            tt("dve", K5(j, 12), K5(j, 12), K5(j, 13), ALU.add, PK, PK)
            tt("dve", K5(j, 7), K5(j, 12), K5(j, 9), ALU.mult, PK, PK)
            tt("dve", K5(j, 12), K5(j, 6), K5(j, 0), ALU.mult, PK, PK)
            tt("dve", K5(j, 13), K5(j, 10), K5(j, 1), ALU.mult, PK, PK)
            tt("dve", K5(j, 12), K5(j, 12), K5(j, 13), ALU.subtract, PK, PK)
            tt("dve", K5(j, 8), K5(j, 12), K5(j, 9), ALU.mult, PK, PK)
            ts("dve", K5(j, 10), K5(j, 8), -1.0, None, ALU.mult, None, PK, PK)
            hflat = h[:, :, :].rearrange("p k t -> p (k t)")
            za = big16[:, 0:T]
            zb = big16[:, T:2 * T]
            zc = hflat[:, 0:T]
            zd = hflat[:, T:2 * T]
            ZK = ["zq"]
            for q in range(NQ):
                b_ = q % 2
                stg = tmp[:, 4256 + b_ * 2 * T:4256 + (b_ + 1) * 2 * T].rearrange("p (a t) -> p a t", a=2)
                SK = [("tabstg", b_)]
                ts("dve", za, tau[:, :], s5k[:, j, 4, q:q + 1], None, ALU.mult, None, PK + ZK + ["tau"], ZK)
                range_reduce(zd, za, zb.bitcast(I32), zc, ZK)
                actf(stg[:, 1, :], zd, AF.Sin, ZK, SK)
                ts("dve", za, za, math.pi / 2, None, ALU.add, None, ZK, ZK)
                range_reduce(zd, za, zb.bitcast(I32), zc, ZK)
                actf(stg[:, 0, :], zd, AF.Sin, ZK, SK)
                dma("sp", s5tab_d[j][q].rearrange("p (a t) -> p a t", a=2), stg, f"tbw{b_}", SK, [("s5tab", j)])
            if s5_stage < 1:
                return
            ta = tmp[:, 4224:4240]
            tb = tmp[:, 4240:4256]
            for q in range(NQ):
                k, r4 = divmod(q, 4)
                br = s5nat[:, 0, q, :]
                bi = s5nat[:, 1, q, :]
                ts("dve", ta, br, s5k[:, j, 7, q:q + 1], None, ALU.mult, None, PK, PK)
                stt(ta, bi, s5k[:, j, 10, q:q + 1], ta, ALU.mult, ALU.add, PK, PK)
                ts("dve", tb, bi, s5k[:, j, 7, q:q + 1], None, ALU.mult, None, PK, PK)
                stt(tb, br, s5k[:, j, 8, q:q + 1], tb, ALU.mult, ALU.add, PK, PK)
                FK = ["s5full"]
                memset("dve", s5full[:, :, :], 0.0, FK)
                for a_, src in ((0, ta), (1, tb)):
                    for gl in range(2):
                        ts("dve", s5full[:, a_, 32 * r4 + 16 * gl:32 * r4 + 16 * gl + 16], src, masks[:, 16 + gl:17 + gl], None,
                           ALU.mult, None, PK + FK + ["masks"], FK)
                pi = psn()
                tr(pi, slice(0, 128), s5full[:, 0, :], FK, last=False)
                tr(pi, slice(128, 256), s5full[:, 1, :], FK, last=True)
                cp("act", s5bcs[:, 0:2, :], ps[pi][:, 0:256].rearrange("p (a n) -> p a n", a=2), [("ps", pi)], ["s5bcs"])
                for a_ in range(2):
                    for gl in range(2):
                        mcol = (8 if a_ else 0) + 2 * r4 + gl
                        ts("dve", s5full[:, a_, 64 * gl:64 * gl + 64], s5cn[:, a_, k, :], masks[:, mcol:mcol + 1], None,
                           ALU.mult, None, PK + FK + ["masks"], FK)
                pi = psn()
                tr(pi, slice(0, 128), s5full[:, 0, :], FK, last=False)
                tr(pi, slice(128, 256), s5full[:, 1, :], FK, last=True)
                cp("act", s5bcs[:, 2:4, :], ps[pi][:, 0:256].rearrange("p (a n) -> p a n", a=2), [("ps", pi)], ["s5bcs"])
                dma("sp", s5bc_d[j][:, :, q * 128:(q + 1) * 128].rearrange("a p n -> p a n"), s5bcs[:, :, :], "scr",
                    ["s5bcs"], [("s5bc", j)])

        def s5_slabs(j):
            slab_list.append(slab_cols("s5_w_in", j, KC, [(0, 1024)]))
            for half in range(2):
                def fn(dst2d, j=j, half=half):
                    dst = dst2d[:, 0:8192].rearrange("p (k n) -> p k n", k=64)
                    src = s5bc_d[j][2 * half:2 * half + 2, :, :].rearrange("a p (q n) -> p a q n", q=NQ)
                    return [(dst[:, 0:32, :], src[:, 0, :, :]), (dst[:, 32:64, :], src[:, 1, :, :])], "pool"
                slab_list.append(SlabDef(fn, reads=[("s5bc", j)], direct=True))
            for i2 in range(2):
                slab_list.append(slab_cols("s5_w_glu", j, KC, [(512 * i2, 512), (1024 + 512 * i2, 512)]))
            slab_list.append(slab_cols("s5_w_out", j, KC, [(0, 1024)]))

        def phase_s5(j):
            slot_i, g_i = next_slab()
            wi_ = ring_view(slot_i, KC, 1024)
            for c in range(KC):
                pi = psn()
                for k in range(KC):
                    mm(pi, wi_[:, k, 128 * c:128 * c + 128], hn[:, k, :], k == 0, k == KC - 1, [("ring", slot_i), ("hn", k)])
                cp("act", u32[:, c, :], ps[pi][:, :], [("ps", pi)], [("big16", c)])
                cp("dve", sq[:, c, :], u32[:, c, :], [("big16", c)], [("sq", c)])
            release(g_i)
            slot_b, g_b = next_slab()
            wB = ring_view(slot_b, 64, 128)
            slot_c, g_c = next_slab()
            wC = ring_view(slot_c, 64, 128)
            if s5_stage in (2, 20, 21):
                pi = psn()
                mm(pi, wB[:, 0, :], sq[:, 0, :], True, True, [("ring", slot_b), ("sq", 0)])
                pi = psn()
                mm(pi, wC[:, 0, :], sq[:, 0, :], True, True, [("ring", slot_c), ("sq", 0)])
                release(g_b)
                release(g_c)
                for _ in range(3):
                    sl_, g_ = next_slab()
                    pi = psn()
                    mm(pi, ring[:, sl_, 0:128], sq[:, 0, :], True, True, [("ring", sl_), ("sq", 0)])
                    release(g_)
                return
            ps_mod[0] = 6
            ps_i[0] = 0
            qst = {}
            free = list(range(NTMP))

            def talloc():
                assert free, "S5 tmp slots exhausted"
                return free.pop(0)

            def tfree(*ids):
                free.extend(ids)

            def load_tab(q):
                slot = q % 3
                dma("sp", tabr[:, slot, :, :], s5tab_d[j][q].rearrange("p (a t) -> p a t", a=2), f"tb{slot}",
                    [("s5tab", j)], [("tabr", slot)])

            def s5_a(q):
                c = q // 4
                sl = q % 3
                Cq = tabr[:, sl, 0, :]
                Sq = tabr[:, sl, 1, :]
                TK = [("tabr", sl)]
                pbr = psn()
                mm(pbr, wB[:, q, :], sq[:, c, :], True, True, [("ring", slot_b), ("sq", c)])
                pbi = psn()
                mm(pbi, wB[:, 32 + q, :], sq[:, c, :], True, True, [("ring", slot_b), ("sq", c)])
                br_, bi_, t_, mr, u_, v_ = [talloc() for _ in range(6)]
                cp("act", TM(br_), ps[pbr][:, :], [("ps", pbr)], [("tmp", br_)])
                cp("act", TM(bi_), ps[pbi][:, :], [("ps", pbi)], [("tmp", bi_)])
                tt("dve", TM(t_), TM(br_), Cq, ALU.mult, [("tmp", br_)] + TK, [("tmp", t_)])
                tt("dve", TM(mr), TM(bi_), Sq, ALU.mult, [("tmp", bi_)] + TK, [("tmp", mr)])
                tt("dve", TM(mr), TM(mr), TM(t_), ALU.add, [("tmp", mr), ("tmp", t_)], [("tmp", mr)])
                tt(S5_ENG2, TM(u_), TM(bi_), Cq, ALU.mult, [("tmp", bi_)] + TK, [("tmp", u_)])
                tt(S5_ENG2, TM(v_), TM(br_), Sq, ALU.mult, [("tmp", br_)] + TK, [("tmp", v_)])
                tt(S5_ENG2, TM(u_), TM(u_), TM(v_), ALU.subtract, [("tmp", u_), ("tmp", v_)], [("tmp", u_)])
                qst[q] = (br_, bi_, t_, mr, u_, v_)

            def s5_b(q):
                c, qq = divmod(q, 4)
                py = 6 + (c % 2)
                sl = q % 3
                Cq = tabr[:, sl, 0, :]
                Sq = tabr[:, sl, 1, :]
                Cend = tabr[:, sl, 0, T - 1:T]
                Send = tabr[:, sl, 1, T - 1:T]
                TK = [("tabr", sl)]
                br_, bi_, t_, mr, mi, v_ = qst.pop(q)
                rho = s5k[:, j, 3, q:q + 1].to_broadcast([128, T])
                gr, gi = talloc(), talloc()
                cr = s5carry[:, j, 0, q:q + 1]
                ci = s5carry[:, j, 1, q:q + 1]
                CK = [("s5carry", q)]
                P.op("dve", lambda e, gr=gr, mr=mr, cr=cr, rho=rho: e.tensor_tensor_scan(
                    out=TM(gr), data0=rho, data1=TM(mr), initial=cr, op0=ALU.mult, op1=ALU.add),
                    reads=[("tmp", mr), "s5p"] + CK, writes=[("tmp", gr)])
                P.op("dve", lambda e, gi=gi, mi=mi, ci=ci, rho=rho: e.tensor_tensor_scan(
                    out=TM(gi), data0=rho, data1=TM(mi), initial=ci, op0=ALU.mult, op1=ALU.add),
                    reads=[("tmp", mi), "s5p"] + CK, writes=[("tmp", gi)])
                gre = TM(gr, T - 1, 1)
                gie = TM(gi, T - 1, 1)
                tA = s5k[:, j, 14, q:q + 1]
                tB = s5k[:, j, 15, q:q + 1]
                ts("dve", tA, gie, Send, None, ALU.mult, None, [("tmp", gi)] + TK, [("s5t", q)])
                ts("dve", tB, gre, Send, None, ALU.mult, None, [("tmp", gr)] + TK, [("s5t", q)])
                stt(cr, gre, Cend, tA, ALU.mult, ALU.subtract, [("tmp", gr), ("s5t", q)] + TK, CK)
                stt(ci, gie, Cend, tB, ALU.mult, ALU.add, [("tmp", gi), ("s5t", q)] + TK, CK)
                hb = q % 2
                tt("dve", TM(t_), TM(gr), Cq, ALU.mult, [("tmp", gr)] + TK, [("tmp", t_)])
                tt("dve", TM(mr), TM(gi), Sq, ALU.mult, [("tmp", gi)] + TK, [("tmp", mr)])
                tt("dve", hrb[:, hb, :], TM(t_), TM(mr), ALU.subtract, [("tmp", t_), ("tmp", mr)], [("hrb", hb)])
                tt(S5_ENG2, TM(v_), TM(gr), Sq, ALU.mult, [("tmp", gr)] + TK, [("tmp", v_)])
                tt(S5_ENG2, TM(br_), TM(gi), Cq, ALU.mult, [("tmp", gi)] + TK, [("tmp", br_)])
                tt(S5_ENG2, hib[:, hb, :], TM(v_), TM(br_), ALU.add, [("tmp", v_), ("tmp", br_)], [("hib", hb)])
                mm(py, wC[:, q, :], hrb[:, hb, :], qq == 0, False, [("ring", slot_c), ("hrb", hb)])
                mm(py, wC[:, 32 + q, :], hib[:, hb, :], False, qq == 3, [("ring", slot_c), ("hib", hb)])
                tfree(br_, bi_, t_, mr, mi, v_, gr, gi)
                if qq == 3:
                    yy = talloc()
                    tfree(yy)
                    cp("act", TM(yy), ps[py][:, :], [("ps", py)], [("tmp", yy)])
                    stt(TM(yy), u32[:, c, :], C1("s5_d", j * KC + c), TM(yy), ALU.mult, ALU.add,
                        [("big16", c), ("tmp", yy), "cst"], [("tmp", yy)])
                    actf(act[:, 8 + c, :], TM(yy), AF.Gelu_apprx_tanh, [("tmp", yy)], [("act", 8 + c)])

            load_tab(0)
            load_tab(1)
            if S5_PIPE:
                s5_a(0)
                for q in range(NQ):
                    if q + 2 < NQ:
                        load_tab(q + 2)
                    ra = P.record(lambda: s5_a(q + 1)) if q + 1 < NQ else []
                    rb = P.record(lambda: s5_b(q))
                    P.emit([o for o in ra if o[0] != "dve"])
                    ad = [o for o in ra if o[0] == "dve"]
                    ai = 0
                    for o in rb:
                        P.emit([o])
                        if o[0] == "dve" and ai < len(ad):
                            P.emit([ad[ai]])
                            ai += 1
                    P.emit(ad[ai:])
            else:
                for q in range(NQ):
                    if q + 2 < NQ:
                        load_tab(q + 2)
                    s5_a(q)
                    s5_b(q)
            ps_mod[0] = 8
            ps_i[0] = 0
            tmp_i[0] = 0
            release(g_b)
            release(g_c)
            for i2 in range(2):
                slot_g, g_g = next_slab()
                wg = ring_view(slot_g, KC, 1024)
                for m4 in range(4):
                    m = 4 * i2 + m4
                    pv = psn()
                    for k in range(KC):
                        mm(pv, wg[:, k, 128 * m4:128 * m4 + 128], act[:, 8 + k, :], k == 0, k == KC - 1,
                           [("ring", slot_g), ("act", 8 + k)])
                    pg = psn()
                    for k in range(KC):
                        mm(pg, wg[:, k, 512 + 128 * m4:512 + 128 * m4 + 128], act[:, 8 + k, :], k == 0, k == KC - 1,
                           [("ring", slot_g), ("act", 8 + k)])
                    sg = tslot()
                    actf(TM(sg), ps[pg][:, :], AF.Sigmoid, [("ps", pg)], [("tmp", sg)])
                    tt("dve", act[:, 16 + m, :], ps[pv][:, :], TM(sg), ALU.mult, [("ps", pv), ("tmp", sg)], [("act", 16 + m)])
                release(g_g)
            slot_o, g_o = next_slab()
            wo = ring_view(slot_o, KC, 1024)
            for m in range(KC):
                pi = psn()
                for k in range(KC):
                    mm(pi, wo[:, k, 128 * m:128 * m + 128], act[:, 16 + k, :], k == 0, k == KC - 1,
                       [("ring", slot_o), ("act", 16 + k)])
                tt("dve", h[:, m, :], h[:, m, :], ps[pi][:, :], ALU.add, [("h", m), ("ps", pi)], [("h", m)])
            release(g_o)

        def phase_store(t):
            fs = [tslot() for _ in range(KC)]

            def o(k):
                return TM(fs[k]), ("tmp", fs[k])
            phase_norm("g_fin", 0, out_f32=o)
            for j in range(4):
                for hf in range(2):
                    pi = psn()
                    for kk in range(4):
                        k = 4 * hf + kk
                        tr(pi, slice(128 * kk, 128 * kk + 128), TM(fs[k], 128 * j, 128), [("tmp", fs[k])], last=(kk == 3))
                    cp("act" if hf else "dve", xin[:, j, 512 * hf:512 * hf + 512], ps[pi][:, :], [("ps", pi)],
                       [("big16", 2 * j + hf)])
            dma("sp", out_d[t * T:(t + 1) * T, :].rearrange("(j p) d -> p j d", p=128), xin, "ost",
                [("big16", i) for i in range(8)], [("outd", t)])

        EPS_AP = sb("eps_ap", [128, 1])
        ONE_AP = sb("one_ap", [128, 1])
        memset("dve", EPS_AP[:, :], EPS, ["eps"])
        memset("dve", ONE_AP[:, :], 1.0, ["one"])

        memset("dve", onesf[:, :], 1.0, ["onesf"])
        for gi_ in range(8):
            P.op("dve", lambda e, gi_=gi_: e.reduce_sum(out=masks[:, gi_:gi_ + 1], in_=ident[:, 16 * gi_:16 * gi_ + 16],
                                                        axis=mybir.AxisListType.X), reads=["ident"], writes=["masks"])
        for hf_ in range(2):
            P.op("dve", lambda e, hf_=hf_: e.reduce_sum(out=masks[:, 16 + hf_:17 + hf_], in_=ident[:, 64 * hf_:64 * hf_ + 64],
                                                        axis=mybir.AxisListType.X), reads=["ident"], writes=["masks"])
        ts("dve", masks[:, 8:16], masks[:, 0:8], -1.0, None, ALU.mult, None, ["masks"], ["masks"])
        if "s" in mix:
            for j_ in sorted({i // 2 for i in layers if i % 2 == 1}):
                if s5_stage != 21:
                    s5_prologue(j_)

        for i in layers:
            if i % 2 == 0 and "r" in mix:
                rg_slabs(i // 2)
            if i % 2 == 1 and "s" in mix and s5_stage >= 2:
                s5_slabs(i // 2)
            if "f" in mix:
                ffn_slabs(i)

        if USE_WSC:
            cast_all_slabs()
        P.barrier()
        for t in range(n_tiles):
            cur[0] = t
            phase_load(t)
            for i in layers:
                if i % 2 == 0 and "r" in mix:
                    phase_norm("g_mix", i * KC)
                    phase_rg(i // 2)
                if i % 2 == 1 and "s" in mix and s5_stage >= 2:
                    phase_norm("g_mix", i * KC)
                    phase_s5(i // 2)
                if "f" in mix:
                    phase_norm("g_ffn", i * KC)
                    phase_ffn(i)
            phase_store(t)
        P.final_wait("sp", [("outd", t) for t in range(n_tiles)] + [("dbg", i) for i in range(dbg_n[0])])
        P.dbg_names = dbg_names

        with nc.Block() as block:
            @block.sync
            def _(e):
                P.replay("sp", e, sems)

            @block.gpsimd
            def _(e):
                P.replay("pool", e, sems)

            @block.scalar
            def _(e):
                P.replay("act", e, sems)

            @block.vector
            def _(e):
                P.replay("dve", e, sems)

            @block.tensor
            def _(e):
                P.replay("pe", e, sems)
    return nc, P


def make_consts():
    return {"ident": np.eye(128, dtype=np.float32),
            "tau": np.tile(np.arange(1, T + 1, dtype=np.float32)[None, :], (128, 1))}


def shape_inputs(inputs):
    r = {}
    for k, v in inputs.items():
        if k == "x":
            continue
        v = np.ascontiguousarray(v, dtype=np.float32)
        if k == "norm_final_g":
            v = v.reshape(1, D)
        elif k in ("rg_b_a", "rg_b_x"):
            v = v.reshape(2, D)
        elif k in ("s5_a_re", "s5_a_im"):
            v = v.reshape(2, 4096)
        elif k in ("s5_b_re", "s5_b_im"):
            v = v.reshape(2, 4096, 16)
        elif k in ("s5_c_re", "s5_c_im"):
            v = v.reshape(2, 1024, 64)
        r[k] = v
    return r


def kernel(**inputs):
    x = np.ascontiguousarray(inputs["x"], dtype=np.float32)
    w = shape_inputs(inputs)
    w.update(make_consts())
    nc, _ = build_program()
    in_maps = [dict(w, x=x[b]) for b in range(BATCH)]
    res = run_bass_kernel_spmd(nc, in_maps, core_ids=list(range(BATCH)))
    return np.stack([res.results[b]["out"] for b in range(BATCH)], axis=0)
```

```python
import math
import numpy as np
import concourse.bass as bass
import concourse.mybir as mybir
from concourse.bass_utils import run_bass_kernel_spmd

F32 = mybir.dt.float32
BF16 = mybir.dt.bfloat16
AF = mybir.ActivationFunctionType
ALU = mybir.AluOpType

D = 1024
KC = 8
T = 512
SEQ = 8192
BATCH = 4
DEPTH = 4
DFF = 3072
NJ = 24
EPS = 1e-6
LSUB = 64
NQ = 32
ENGS = ("pe", "act", "dve", "pool", "sp")
SLAB = 8192
import os
S5_PIPE = os.environ.get('K_S5PIPE', '1') == '1'
S5_ENG2 = os.environ.get('K_S5ENG2', 'dve')
FFN_LOOK = int(os.environ.get('K_FFNLOOK', '2'))
RG_PIPE = os.environ.get('K_RGPIPE', '1') == '1'
USE_WSC = True
NSLOT = 3


class Prog:
    def __init__(self):
        self.prog = {e: [] for e in ENGS}
        self.count = {e: 0 for e in ENGS}
        self.dcount = {}
        self.seen = {e: {} for e in ENGS}
        self.last_write = {}
        self.readers = {}
        self.n_ops = 0
        self._rec = None

    def record(self, f):
        assert self._rec is None
        self._rec = []
        try:
            f()
            return self._rec
        finally:
            self._rec = None

    def emit(self, ops):
        for o in ops:
            self.op(*o)

    def _need(self, eng, waits, tok):
        if tok is None:
            return
        sk, v = tok
        if sk == eng and (eng in ("pe", "sp") or v > self.count[eng]):
            return
        if self.seen[eng].get(sk, 0) >= v:
            return
        if waits.get(sk, 0) < v:
            waits[sk] = v

    def op(self, eng, fn, reads=(), writes=(), track=True, dma=None, ninc=1):
        if self._rec is not None:
            self._rec.append((eng, fn, tuple(reads), tuple(writes), track, dma, ninc))
            return
        waits = {}
        for r in reads:
            self._need(eng, waits, self.last_write.get(r))
        for w in writes:
            self._need(eng, waits, self.last_write.get(w))
            for sk, v in self.readers.get(w, {}).items():
                self._need(eng, waits, (sk, v))
        for sk, v in waits.items():
            self.prog[eng].append(("wait", sk, v))
            self.seen[eng][sk] = v
        if dma is not None:
            prev = self.dcount.get(dma, 0)
            if prev > self.seen[eng].get(dma, 0) and (dma.startswith("cld") or dma.startswith("wcs")):
                self.prog[eng].append(("wait", dma, prev))
                self.seen[eng][dma] = prev
            self.dcount[dma] = self.dcount.get(dma, 0) + 16 * ninc
            tok = (dma, self.dcount[dma])
            self.prog[eng].append(("op", fn, dma, 16))
        elif track:
            self.count[eng] += 1
            tok = (eng, self.count[eng])
            self.prog[eng].append(("op", fn, eng, 1))
        else:
            tok = (eng, self.count[eng] + 1)
            self.prog[eng].append(("op", fn, None, 0))
        for w in writes:
            self.last_write[w] = tok
            self.readers[w] = {}
        for r in reads:
            d = self.readers.setdefault(r, {})
            if d.get(tok[0], 0) < tok[1]:
                d[tok[0]] = tok[1]
        self.n_ops += 1

    def barrier(self):
        snap = dict(self.count)
        dsnap = dict(self.dcount)
        for e in ENGS:
            for o, v in list(snap.items()) + list(dsnap.items()):
                if o.startswith("wcs"):
                    continue
                if o != e and v > 0 and self.seen[e].get(o, 0) < v:
                    self.prog[e].append(("wait", o, v))
                    self.seen[e][o] = v

    def final_wait(self, eng, keys):
        waits = {}
        for k in keys:
            self._need(eng, waits, self.last_write.get(k))
        for sk, v in waits.items():
            self.prog[eng].append(("wait", sk, v))
            self.seen[eng][sk] = v

    def replay(self, eng, handle, sems):
        for item in self.prog[eng]:
            if item[0] == "wait":
                handle.wait_ge(sems[item[1]], item[2])
            else:
                _, fn, sk, inc = item
                r = fn(handle)
                if sk is not None:
                    if isinstance(r, (list, tuple)):
                        for ins in r:
                            ins.then_inc(sems[sk], inc)
                    else:
                        r.then_inc(sems[sk], inc)


def build_program(n_tiles=SEQ // T, layers=(0, 1, 2, 3), seq=SEQ, mix="rsf", debug=0, s5_stage=4):
    nc = bass.Bass("TRN2", target_bir_lowering=False)
    P = Prog()

    def dram(name, shape, kind="ExternalInput", dt=F32):
        return nc.dram_tensor(name, list(shape), dt, kind=kind).ap()

    x_d = dram("x", [seq, D])
    out_d = dram("out", [seq, D], kind="ExternalOutput")
    ident_d = dram("ident", [128, 128])
    tau_d = dram("tau", [128, T])
    s5tab_d = [dram(f"s5tab{j}", [NQ, 128, 2 * T], kind="Internal") for j in range(2)]
    W = {}
    for name, shape in [
        ("norm_mix_g", [4, D]), ("norm_ffn_g", [4, D]), ("norm_final_g", [1, D]),
        ("rg_w_in", [2, D, 2 * D]), ("rg_conv_w", [2, 4, D]), ("rg_conv_b", [2, D]),
        ("rg_w_a", [2, 8, 128, 128]), ("rg_b_a", [2, D]), ("rg_w_x", [2, 8, 128, 128]),
        ("rg_b_x", [2, D]), ("rg_lambda", [2, D]), ("rg_w_out", [2, D, D]),
        ("s5_w_in", [2, D, D]), ("s5_a_re", [2, 4096]), ("s5_a_im", [2, 4096]),
        ("s5_log_dt", [2, 64]), ("s5_b_re", [2, 4096, 16]), ("s5_b_im", [2, 4096, 16]),
        ("s5_c_re", [2, 1024, 64]), ("s5_c_im", [2, 1024, 64]), ("s5_d", [2, D]),
        ("s5_w_glu", [2, D, 2 * D]), ("s5_w_out", [2, D, D]),
        ("ffn_w_up", [4, D, 2 * DFF]), ("ffn_conv_w", [4, 3, 2 * DFF]),
        ("ffn_conv_b", [4, 2 * DFF]), ("ffn_w_down", [4, DFF, D]),
    ]:
        W[name] = dram(name, shape)
    s5bc_d = [dram(f"s5bc{j}", [4, 128, NQ * 128], kind="Internal", dt=BF16) for j in range(2)]

    dbg_d = dram("dbg", [max(debug, 1), 128, T], kind="ExternalOutput") if debug else None
    dbg_n = [0]
    cur = [0]
    dbg_names = []

    import contextlib
    es = contextlib.ExitStack()
    with es:
        def sb(name, shape, dt=F32):
            return es.enter_context(nc.sbuf_tensor(name, list(shape), dt))

        big16 = sb("big16", [128, 4096])
        xin = big16[:, :].rearrange("p (j d) -> p j d", j=4)
        u32 = big16[:, :].rearrange("p (k t) -> p k t", k=KC)
        h = sb("h", [128, KC, T])
        sq = sb("sq", [128, KC, T], BF16)
        hn = sb("hn", [128, KC, T], BF16)
        rstd = sb("rstd", [128, T])
        ring = sb("ring", [128, NSLOT, SLAB], BF16)
        act = sb("act", [128, NJ, T], BF16)
        NTMP = 16
        TW = T + 4
        tmp = sb("tmp", [128, NTMP * TW])
        xcb = sb("xcb", [128, 2, T], BF16)
        hrb = sb("hrb", [128, 4, T], BF16)
        hib = sb("hib", [128, 4, T], BF16)
        ident = sb("ident_sb", [128, 128])
        identb = sb("identb", [128, 128], BF16)
        onesb = sb("onesb", [128, 128], BF16)
        tau = sb("tau_sb", [128, T])
        tabr = sb("tabr", [128, 3, 2, T])
        NCONST = 1152
        cst = sb("cst", [128, NCONST])
        stage = tmp[:, 0:1152].rearrange("p (b n) -> p b n", b=9)
        rgc = sb("rgc", [128, 2, 2, KC])
        halo_rg = sb("halo_rg", [128, 2, KC, 3])
        carry_rg = sb("carry_rg", [128, 2, KC])
        halo_ffn = sb("halo_ffn", [128, 4, 2 * NJ, 2], BF16)
        NUPB = 6
        upb = sb("upb", [128, NUPB, T + 2], BF16)
        dgs = sb("dgs", [128, NUPB, 3, 128], BF16)
        upb_i = [0]
        s5k = sb("s5k", [128, 2, 16, NQ])
        s5carry = sb("s5carry", [128, 2, 2, NQ])
        s5nat = tmp[:, 1152:1152 + 2048].rearrange("p (a q c) -> p a q c", a=4, q=NQ)
        s5cn = tmp[:, 3200:3200 + 1024].rearrange("p (a k n) -> p a k n", a=2, k=KC)
        s5full = sb("s5full", [128, 2, 128])
        s5bcs = sb("s5bcs", [128, 4, 128], BF16)
        masks = sb("masks", [128, 20])
        onesf = sb("onesf", [128, 128])
        ldrow = sb("ldrow", [1, 128])

        ps = [es.enter_context(nc.psum_tensor(f"ps{i}", [128, T], F32)) for i in range(8)]
        ps_i = [0]
        ps_mod = [8]

        def psn():
            i = ps_i[0] % ps_mod[0]
            ps_i[0] = (i + 1) % ps_mod[0]
            return i

        NCLD = 8
        sem_names = list(ENGS) + [f"w{s}" for s in range(NSLOT)] + [f"cld{i}" for i in range(NCLD)] + ["xld", "ost", "scr", "tb0", "tb1", "tb2", "tbw0", "tbw1"]
        cld_i = [0]
        NWCS = 8
        wcs_i = [0]
        sem_names += [f"wcs{i_}" for i_ in range(NWCS)]
        sems = {n: es.enter_context(nc.semaphore(n)) for n in sem_names}

        def mm(pi, lhsT, rhs, start, stop, reads, cols=slice(0, T)):
            P.op("pe", lambda e: e.matmul(ps[pi][:, cols], lhsT=lhsT, rhs=rhs, start=start, stop=stop),
                 reads=reads, writes=[("ps", pi)], track=stop)

        def tr(pi, cols, in_, reads, last=True):
            P.op("pe", lambda e: e.transpose(ps[pi][:, cols], in_, ident[:, :]),
                 reads=list(reads) + ["ident"], writes=[("ps", pi)], track=last)

        def actf(out, in_, func, reads, writes, bias=None, scale=None):
            kw = {}
            if bias is not None:
                kw["bias"] = bias
            if scale is not None:
                kw["scale"] = scale
            P.op("act", lambda e: e.activation(out=out, in_=in_, func=func, **kw), reads=reads, writes=writes)

        def tt(eng, out, in0, in1, op, reads, writes):
            P.op(eng, lambda e: e.tensor_tensor(out=out, in0=in0, in1=in1, op=op), reads=reads, writes=writes)

        def ts(eng, out, in0, s1, s2, op0, op1, reads, writes):
            if op1 is None:
                P.op(eng, lambda e: e.tensor_scalar(out=out, in0=in0, scalar1=s1, scalar2=None, op0=op0),
                     reads=reads, writes=writes)
            else:
                P.op(eng, lambda e: e.tensor_scalar(out=out, in0=in0, scalar1=s1, scalar2=s2, op0=op0, op1=op1),
                     reads=reads, writes=writes)

        def stt(out, in0, scalar, in1, op0, op1, reads, writes):
            P.op("dve", lambda e: e.scalar_tensor_tensor(out=out, in0=in0, scalar=scalar, in1=in1, op0=op0, op1=op1),
                 reads=reads, writes=writes)

        def cp(eng, out, in_, reads, writes):
            if eng == "act":
                P.op("act", lambda e: e.copy(out=out, in_=in_), reads=reads, writes=writes)
            else:
                P.op(eng, lambda e: e.tensor_copy(out=out, in_=in_), reads=reads, writes=writes)

        def memset(eng, ap, val, writes):
            P.op(eng, lambda e: e.memset(ap, val), writes=writes)

        def dma(eng, out, in_, sem, reads, writes, slow=False):
            if sem == "cld":
                sem = f"cld{cld_i[0]}"
                cld_i[0] = (cld_i[0] + 1) % NCLD
            if sem == "wcs":
                sem = f"wcs{wcs_i[0]}"
                wcs_i[0] = (wcs_i[0] + 1) % NWCS
            kw = {"allow_slow_non_contiguous": True} if slow else {}
            P.op(eng, lambda e: e.dma_start(out=out, in_=in_, **kw), reads=reads, writes=writes, dma=sem)

        def dbg(name, ap, key):
            if not debug or dbg_n[0] >= debug or cur[0] != n_tiles - 1:
                return
            i = dbg_n[0]
            dbg_n[0] += 1
            dbg_names.append(name)
            dma("sp", dbg_d[i], ap, "scr", [key], [("dbg", i)])

        tmp_i = [0]

        def tslot():
            i = tmp_i[0]
            tmp_i[0] = (i + 1) % NTMP
            return i

        def TM(i, c0=0, n=T):
            return tmp[:, i * TW + c0:i * TW + c0 + n]

        slab_list = []
        slab_state = {"issued": 0, "used": 0}
        PF = NSLOT - 1

        wsc_box = [None]

        def issue_slab(gidx):
            idx = gidx % len(slab_list)
            slot = gidx % NSLOT
            sd = slab_list[idx]
            if sd.direct or not USE_WSC:
                pairs, q = sd(ring[:, slot, :])
                if USE_WSC:
                    q = "sp"

                def fn(e, pairs=pairs):
                    return [e.dma_start(out=o, in_=i) for (o, i) in pairs]
                P.op(q, fn, reads=[r for r in sd.reads], writes=[("ring", slot)],
                     dma=f"w{slot}", ninc=len(pairs))
            else:
                n = sd.size
                src = wsc_box[0][idx, :, 0:n]
                P.op("sp", lambda e, slot=slot, n=n, src=src: e.dma_start(out=ring[:, slot, 0:n], in_=src),
                     reads=[("wsc", idx)], writes=[("ring", slot)], dma=f"w{slot}", ninc=1)

        def cast_all_slabs():
            wsc_box[0] = dram("wsc", [len(slab_list), 128, SLAB], kind="Internal", dt=BF16)
            for idx, sd in enumerate(slab_list):
                if sd.direct:
                    continue
                pairs, _ = sd(wsc_box[0][idx])
                for (o, i) in pairs:
                    dma("pool", o, i, "wcs", [], [("wsc", idx)])

        released = set()

        def pump():
            total = len(slab_list) * n_tiles
            while slab_state["issued"] < min(slab_state["used"] + PF + 1, total):
                n = slab_state["issued"]
                if n >= NSLOT and (n - NSLOT) not in released:
                    break
                issue_slab(n)
                slab_state["issued"] += 1

        def next_slab():
            g = slab_state["used"]
            slab_state["used"] += 1
            pump()
            assert slab_state["issued"] > g, "slab ring deadlock: too many live slabs"
            return g % NSLOT, g

        def release(g):
            released.add(g)
            pump()

        class SlabDef:
            def __init__(self, fn, reads=(), direct=False, size=SLAB):
                self.fn = fn
                self.reads = reads
                self.direct = direct
                self.size = size

            def __call__(self, dst2d):
                return self.fn(dst2d)

        def slab_cols(wname, l, kc, col_sets, q="pool"):
            ntot = sum(n for _, n in col_sets)

            def fn(dst2d):
                dst = dst2d[:, 0:kc * ntot].rearrange("p (k n) -> p k n", k=kc)
                src = W[wname][l].rearrange("(k p) n -> p k n", p=128)
                pairs = []
                o = 0
                for c0, n in col_sets:
                    pairs.append((dst[:, :, o:o + n], src[:, :, c0:c0 + n]))
                    o += n
                return pairs, q
            return SlabDef(fn, size=kc * ntot)

        def ring_view(slot, kc, n):
            return ring[:, slot, 0:kc * n].rearrange("p (k n) -> p k n", k=kc)

        crow = {}
        vec_list = []

        def addvec(name, ap2d):
            crow[name] = sum(v[1] for v in vec_list)
            vec_list.append((name, ap2d.shape[0], ap2d))

        addvec("g_mix", W["norm_mix_g"].rearrange("l (k p) -> (l k) p", p=128))
        addvec("g_ffn", W["norm_ffn_g"].rearrange("l (k p) -> (l k) p", p=128))
        addvec("g_fin", W["norm_final_g"].rearrange("l (k p) -> (l k) p", p=128))
        addvec("rg_cw", W["rg_conv_w"].rearrange("l t (k p) -> (l t k) p", p=128))
        addvec("rg_cb", W["rg_conv_b"].rearrange("l (k p) -> (l k) p", p=128))
        addvec("rg_ba", W["rg_b_a"].rearrange("l (k p) -> (l k) p", p=128))
        addvec("rg_bx", W["rg_b_x"].rearrange("l (k p) -> (l k) p", p=128))
        addvec("rg_lam", W["rg_lambda"].rearrange("l (k p) -> (l k) p", p=128))
        addvec("s5_d", W["s5_d"].rearrange("l (k p) -> (l k) p", p=128))
        addvec("ffn_cw", W["ffn_conv_w"].rearrange("l t (k p) -> (l t k) p", p=128))
        addvec("ffn_cb", W["ffn_conv_b"].rearrange("l (k p) -> (l k) p", p=128))
        addvec("s5_are", W["s5_a_re"].rearrange("l (q p) -> (l q) p", p=128))
        addvec("s5_aim", W["s5_a_im"].rearrange("l (q p) -> (l q) p", p=128))
        nrows = sum(v[1] for v in vec_list)
        assert nrows <= NCONST, nrows

        def C1(name, idx):
            c = crow[name] + idx
            return cst[:, c:c + 1]

        dma("sp", ident[:, :], ident_d[:, :], "cld", [], ["ident"])
        dma("sp", tau[:, :], tau_d[:, :], "cld", [], ["tau"])
        cp("dve", identb[:, :], ident[:, :], ["ident"], ["identb"])
        memset("dve", onesb[:, :], 1.0, ["onesb"])
        memset("dve", halo_rg[:, :, :, :], 0.0, [("halo_rg", l_, c_) for l_ in range(2) for c_ in range(KC)])
        memset("dve", carry_rg[:, :, :], 0.0, [("carry_rg", l_, c_) for l_ in range(2) for c_ in range(KC)])
        memset("dve", halo_ffn[:, :, :, :], 0.0, [("halo_ffn", i_, c_) for i_ in range(4) for c_ in range(2 * NJ)])
        memset("dve", s5carry[:, :, :, :], 0.0, ["s5carry"])
        nblk = (nrows + 127) // 128
        memset("dve", stage[:, :, :], 0.0, [("stage", b) for b in range(nblk)])
        r0 = 0
        for name, n, ap2d in vec_list:
            done = 0
            while done < n:
                blk, off = divmod(r0 + done, 128)
                take = min(n - done, 128 - off)
                dma("sp", stage[off:off + take, blk, :], ap2d[done:done + take, :], "cld", [], [("stage", blk)])
                done += take
            r0 += n
        for blk in range(nblk):
            pi = psn()
            tr(pi, slice(0, 128), stage[:, blk, :], [("stage", blk)])
            cp("dve", cst[:, blk * 128:(blk + 1) * 128], ps[pi][:, 0:128], [("ps", pi)], ["cst"])
        for l in range(2):
            lam = cst[:, crow["rg_lam"] + l * KC: crow["rg_lam"] + (l + 1) * KC]
            actf(rgc[:, l, 0, :], lam, AF.Exp, ["cst"], ["rgc"], scale=-1.0)
            actf(rgc[:, l, 0, :], rgc[:, l, 0, :], AF.Ln, ["rgc"], ["rgc"], bias=1.0)
            ts("dve", rgc[:, l, 1, :], rgc[:, l, 0, :], -16.0, None, ALU.mult, None, ["rgc"], ["rgc"])
            ts("dve", rgc[:, l, 0, :], rgc[:, l, 0, :], -8.0, None, ALU.mult, None, ["rgc"], ["rgc"])

        def phase_load(t):
            dma("sp", xin, x_d[t * T:(t + 1) * T, :].rearrange("(j p) d -> p j d", p=128), "xld",
                [], [("big16", i) for i in range(8)])
            for k in range(KC):
                pi = psn()
                for j in range(4):
                    tr(pi, slice(128 * j, 128 * j + 128), xin[:, j, 128 * k:128 * k + 128],
                       [("big16", 2 * j), ("big16", 2 * j + 1)], last=(j == 3))
                cp("act" if k % 2 else "dve", h[:, k, :], ps[pi][:, :], [("ps", pi)], [("h", k)])

        def phase_norm(gname, gidx0, out_f32=None):
            for k in range(KC):
                actf(sq[:, k, :], h[:, k, :], AF.Square, [("h", k)], [("sq", k)])
            pi = psn()
            for k in range(KC):
                mm(pi, onesb[:, :], sq[:, k, :], k == 0, k == KC - 1, ["onesb", ("sq", k)])
            actf(rstd[:, :], ps[pi][:, :], AF.Sqrt, [("ps", pi), "eps"], ["rstd"], bias=EPS_AP[:, :], scale=1.0 / D)
            P.op("dve", lambda e: e.reciprocal(out=rstd[:, :], in_=rstd[:, :]), reads=["rstd"], writes=["rstd"])
            for k in range(KC):
                if out_f32 is None:
                    stt(hn[:, k, :], h[:, k, :], C1(gname, gidx0 + k), rstd[:, :], ALU.mult, ALU.mult,
                        [("h", k), "rstd", "cst"], [("hn", k)])
                else:
                    o, key = out_f32(k)
                    stt(o, h[:, k, :], C1(gname, gidx0 + k), rstd[:, :], ALU.mult, ALU.mult,
                        [("h", k), "rstd", "cst"], [key])

        def phase_ffn(i):
            cw0 = crow["ffn_cw"] + i * 3 * 48
            cb0 = crow["ffn_cb"] + i * 48
            slabs = {}
            st = {}

            def get_slab(si):
                if si not in slabs:
                    slabs[si] = next_slab()
                return slabs[si]

            def stage_a(hx):
                jg, half = divmod(hx, 2)
                si, cc = divmod(jg, 4)
                slot, gsl = get_slab(si)
                wv = ring_view(slot, KC, 1024)
                ch = jg + NJ * half
                pi = psn()
                for k in range(KC):
                    mm(pi, wv[:, k, 512 * half + 128 * cc: 512 * half + 128 * cc + 128], hn[:, k, :],
                       k == 0, k == KC - 1, [("ring", slot), ("hn", k)])
                if hx % 8 == 7:
                    release(gsl)
                ub_i = upb_i[0]
                upb_i[0] = (ub_i + 1) % NUPB
                cp("pool", upb[:, ub_i, 0:2], halo_ffn[:, i, ch, :], [("halo_ffn", i, ch)], [("upbh", ub_i)])
                cp("act", upb[:, ub_i, 2:2 + T], ps[pi][:, :], [("ps", pi)], [("upb", ub_i)])
                cp("pool", halo_ffn[:, i, ch, :], upb[:, ub_i, T:T + 2], [("upb", ub_i)], [("halo_ffn", i, ch)])
                for t3 in range(3):
                    actf(dgs[:, ub_i, t3, :], identb[:, :], AF.Copy, ["identb", "cst"], [("dgs", ub_i, t3)],
                         scale=cst[:, cw0 + 48 * t3 + ch:cw0 + 48 * t3 + ch + 1])
                st[hx] = ub_i

            def stage_b(hx):
                jg, half = divmod(hx, 2)
                ub_i = st.pop(hx)
                pc = psn()
                for t3 in range(3):
                    mm(pc, dgs[:, ub_i, t3, :], upb[:, ub_i, t3:t3 + T], t3 == 0, t3 == 2,
                       [("dgs", ub_i, t3), ("upb", ub_i), ("upbh", ub_i)])
                if half == 0:
                    g = tslot()
                    actf(TM(g), ps[pc][:, :], AF.Gelu_apprx_tanh, [("ps", pc), "cst"], [("tmp", g)],
                         bias=cst[:, cb0 + jg:cb0 + jg + 1])
                    st[("g", jg)] = g
                else:
                    g = st.pop(("g", jg))
                    stt(act[:, jg, :], ps[pc][:, :], cst[:, cb0 + NJ + jg:cb0 + NJ + jg + 1], TM(g), ALU.add, ALU.mult,
                        [("ps", pc), ("tmp", g), "cst"], [("act", jg)])

            NH = 2 * NJ
            LOOK = FFN_LOOK
            for hx in range(min(LOOK, NH)):
                stage_a(hx)
            for hx in range(NH):
                if hx + LOOK < NH:
                    stage_a(hx + LOOK)
                stage_b(hx)
            for s in range(4):
                slot, gsl = next_slab()
                wv = ring_view(slot, NJ, 256)
                for mh in range(2):
                    m = 2 * s + mh
                    pi = psn()
                    for j in range(NJ):
                        mm(pi, wv[:, j, 128 * mh:128 * mh + 128], act[:, j, :], j == 0, j == NJ - 1,
                           [("ring", slot), ("act", j)])
                    tt("dve", h[:, m, :], h[:, m, :], ps[pi][:, :], ALU.add, [("h", m), ("ps", pi)], [("h", m)])
                release(gsl)

        def ffn_slabs(i):
            for s in range(6):
                slab_list.append(slab_cols("ffn_w_up", i, KC, [(512 * s, 512), (DFF + 512 * s, 512)]))
            for s in range(4):
                slab_list.append(slab_cols("ffn_w_down", i, NJ, [(256 * s, 256)]))

        def phase_rg(l):
            slot_g, g_g = next_slab()
            wg = ring_view(slot_g, KC, 1024)
            for c in range(KC):
                pg = psn()
                for k in range(KC):
                    mm(pg, wg[:, k, 128 * c:128 * c + 128], hn[:, k, :], k == 0, k == KC - 1, [("ring", slot_g), ("hn", k)])
                actf(act[:, 8 + c, :], ps[pg][:, :], AF.Gelu_apprx_tanh, [("ps", pg)], [("act", 8 + c)])
            release(g_g)
            slot_x, g_x = next_slab()
            wx = ring_view(slot_x, KC, 1024)
            slot_a, g_a = next_slab()
            wa = ring_view(slot_a, 16, 128)
            ust = {}
            rfree = list(range(NTMP))

            def ralloc():
                assert rfree, "RG tmp slots exhausted"
                return rfree.pop(0)

            def rg_a(c):
                pi = psn()
                for k in range(KC):
                    mm(pi, wx[:, k, 128 * c:128 * c + 128], hn[:, k, :], k == 0, k == KC - 1, [("ring", slot_x), ("hn", k)])
                u = ralloc()
                cp("pool", TM(u, 0, 3), halo_rg[:, l, c, :], [("halo_rg", l, c)], [("tmp", u)])
                cp("act", TM(u, 3, T), ps[pi][:, :], [("ps", pi)], [("tmp", u)])
                cp("pool", halo_rg[:, l, c, :], TM(u, T, 3), [("tmp", u)], [("halo_rg", l, c)])
                ust[c] = u

            def rg_b1(c):
                u = ust.pop(c)
                xc = ralloc()
                ts("dve", TM(xc), TM(u, 0, T), C1("rg_cw", l * 32 + 0 * KC + c), C1("rg_cb", l * KC + c),
                   ALU.mult, ALU.add, [("tmp", u), "cst"], [("tmp", xc)])
                for k in range(1, 4):
                    stt(TM(xc), TM(u, k, T), C1("rg_cw", l * 32 + k * KC + c), TM(xc), ALU.mult, ALU.add,
                        [("tmp", u), ("tmp", xc), "cst"], [("tmp", xc)])
                rfree.append(u)
                xb = c % 2
                xbv = xcb[:, xb, :]
                cp("pool", xbv, TM(xc), [("tmp", xc)], [("xcb", xb)])
                pa = psn()
                mm(pa, wa[:, c, :], xbv, True, True, [("ring", slot_a), ("xcb", xb)])
                px = psn()
                mm(px, wa[:, 8 + c, :], xbv, True, True, [("ring", slot_a), ("xcb", xb)])
                r = ralloc()
                actf(TM(r), ps[pa][:, :], AF.Sigmoid, [("ps", pa), "cst"], [("tmp", r)], bias=C1("rg_ba", l * KC + c))
                ig = ralloc()
                actf(TM(ig), ps[px][:, :], AF.Sigmoid, [("ps", px), "cst"], [("tmp", ig)], bias=C1("rg_bx", l * KC + c))
                a = ralloc()
                actf(TM(a), TM(r), AF.Exp, [("tmp", r), "rgc"], [("tmp", a)], scale=rgc[:, l, 0, c:c + 1])
                actf(TM(r), TM(r), AF.Exp, [("tmp", r), "rgc"], [("tmp", r)], scale=rgc[:, l, 1, c:c + 1])
                actf(TM(r), TM(r), AF.Sqrt, [("tmp", r), "one"], [("tmp", r)], bias=ONE_AP[:, :], scale=-1.0)
                tt("pool", TM(ig), TM(ig), TM(xc), ALU.mult, [("tmp", ig), ("tmp", xc)], [("tmp", ig)])
                ust[("b", c)] = (xc, r, ig, a)

            def rg_b2(c):
                xc, r, ig, a = ust.pop(("b", c))
                tt("dve", TM(ig), TM(ig), TM(r), ALU.mult, [("tmp", ig), ("tmp", r)], [("tmp", ig)])
                hs = ralloc()
                P.op("dve", lambda e, hs=hs, a=a, ig=ig, c=c: e.tensor_tensor_scan(
                    out=TM(hs), data0=TM(a), data1=TM(ig), initial=carry_rg[:, l, c:c + 1],
                    op0=ALU.mult, op1=ALU.add),
                    reads=[("tmp", a), ("tmp", ig), ("carry_rg", l, c)], writes=[("tmp", hs)])
                cp("dve", carry_rg[:, l, c:c + 1], TM(hs, T - 1, 1), [("tmp", hs)], [("carry_rg", l, c)])
                tt("dve", act[:, c, :], TM(hs), act[:, 8 + c, :], ALU.mult, [("tmp", hs), ("act", 8 + c)], [("act", c)])
                rfree.extend([xc, r, ig, a, hs])

            if RG_PIPE:
                rg_a(0)
                for c in range(KC):
                    if c + 1 < KC:
                        rg_a(c + 1)
                    rg_b1(c)
                    rg_b2(c)
            else:
                for c in range(KC):
                    rg_a(c)
                    rg_b1(c)
                    rg_b2(c)
            tmp_i[0] = 0
            release(g_x)
            release(g_a)
            slot_o, g_o = next_slab()
            wo = ring_view(slot_o, KC, 1024)
            for m in range(KC):
                pi = psn()
                for c in range(KC):
                    mm(pi, wo[:, c, 128 * m:128 * m + 128], act[:, c, :], c == 0, c == KC - 1, [("ring", slot_o), ("act", c)])
                tt("dve", h[:, m, :], h[:, m, :], ps[pi][:, :], ALU.add, [("h", m), ("ps", pi)], [("h", m)])
            release(g_o)

        def rg_slabs(l):
            slab_list.append(slab_cols("rg_w_in", l, KC, [(1024, 1024)]))
            slab_list.append(slab_cols("rg_w_in", l, KC, [(0, 1024)]))

            def fn(dst2d, l=l):
                dst = dst2d[:, 0:2048].rearrange("p (k n) -> p k n", k=16)
                return [(dst[:, 0:8, :], W["rg_w_a"][l].rearrange("h i j -> i h j")),
                        (dst[:, 8:16, :], W["rg_w_x"][l].rearrange("h i j -> i h j"))], "pool"
            slab_list.append(SlabDef(fn, size=2048))
            slab_list.append(slab_cols("rg_w_out", l, KC, [(0, 1024)]))


        I32 = mybir.dt.int32
        TWO_PI = 2.0 * math.pi

        def K5(j, idx, q0=0, n=NQ):
            return s5k[:, j, idx, q0:q0 + n]

        def range_reduce(eng_out_ap, z_ap, zi_ap, kf_ap, width_keys):
            C1_ = 6.28125
            C2_ = TWO_PI - C1_
            PI_LO = 3.1415925
            ts("dve", zi_ap, z_ap, 1.0 / TWO_PI, None, ALU.mult, None, width_keys, width_keys)
            cp("dve", kf_ap, zi_ap, width_keys, width_keys)
            stt(eng_out_ap, kf_ap, -C1_, z_ap, ALU.mult, ALU.add, width_keys, width_keys)
            stt(eng_out_ap, kf_ap, -C2_, eng_out_ap, ALU.mult, ALU.add, width_keys, width_keys)
            ts("dve", eng_out_ap, eng_out_ap, -PI_LO, PI_LO, ALU.max, ALU.min, width_keys, width_keys)

        def s5_prologue(j):
            PK = ["s5p"]
            cp("dve", K5(j, 0), cst[:, crow["s5_are"] + NQ * j:crow["s5_are"] + NQ * (j + 1)], ["cst"], PK)
            cp("dve", K5(j, 1), cst[:, crow["s5_aim"] + NQ * j:crow["s5_aim"] + NQ * (j + 1)], ["cst"], PK)
            dma("sp", ldrow[0:1, 0:64], W["s5_log_dt"][j:j + 1, :], "cld", [], PK)
            for q0 in range(0, NQ, 4):
                dma("sp", s5nat[:, 0, q0:q0 + 4, :], W["s5_b_re"][j].rearrange("(q p) c -> p q c", p=128)[:, q0:q0 + 4, :], "cld", [], PK)
                dma("sp", s5nat[:, 1, q0:q0 + 4, :], W["s5_b_im"][j].rearrange("(q p) c -> p q c", p=128)[:, q0:q0 + 4, :], "cld", [], PK)
            dma("sp", s5cn[:, 0, :, :], W["s5_c_re"][j].rearrange("(k r) n -> r k n", r=128), "cld", [], PK)
            dma("sp", s5cn[:, 1, :, :], W["s5_c_im"][j].rearrange("(k r) n -> r k n", r=128), "cld", [], PK)
            pi = psn()
            P.op("pe", lambda e: e.matmul(ps[pi][:, 0:64], lhsT=onesf[0:1, :], rhs=ldrow[0:1, 0:64], start=True, stop=True),
                 reads=PK + ["onesf"], writes=[("ps", pi)])
            pv = ps[pi][:, 0:64].rearrange("p (q g) -> p q g", g=2)
            cp("dve", s5k[0:64, j, 2, :], pv[0:64, :, 0], [("ps", pi)], PK)
            cp("dve", s5k[64:128, j, 2, :], pv[64:128, :, 1], [("ps", pi)], PK)
            actf(K5(j, 2), K5(j, 2), AF.Exp, PK, PK)
            tt("dve", K5(j, 12), K5(j, 0), K5(j, 2), ALU.mult, PK, PK)
            actf(K5(j, 3), K5(j, 12), AF.Exp, PK, PK)
            tt("dve", K5(j, 4), K5(j, 1), K5(j, 2), ALU.mult, PK, PK)
            zi = s5k[:, j, 13, :].bitcast(I32)
            range_reduce(K5(j, 12), K5(j, 4), zi, K5(j, 14), PK)
            actf(K5(j, 6), K5(j, 12), AF.Sin, PK, PK)
            ts("dve", K5(j, 15), K5(j, 4), math.pi / 2, None, ALU.add, None, PK, PK)
            range_reduce(K5(j, 12), K5(j, 15), zi, K5(j, 14), PK)
            actf(K5(j, 5), K5(j, 12), AF.Sin, PK, PK)
            tt("dve", K5(j, 5), K5(j, 5), K5(j, 3), ALU.mult, PK, PK)
            tt("dve", K5(j, 6), K5(j, 6), K5(j, 3), ALU.mult, PK, PK)
            tt("dve", K5(j, 12), K5(j, 0), K5(j, 0), ALU.mult, PK, PK)
            tt("dve", K5(j, 13), K5(j, 1), K5(j, 1), ALU.mult, PK, PK)
            tt("dve", K5(j, 9), K5(j, 12), K5(j, 13), ALU.add, PK, PK)
            P.op("dve", lambda e: e.reciprocal(out=K5(j, 9), in_=K5(j, 9)), reads=PK, writes=PK)
            ts("dve", K5(j, 10), K5(j, 5), -1.0, None, ALU.add, None, PK, PK)
            tt("dve", K5(j, 12), K5(j, 10), K5(j, 0), ALU.mult, PK, PK)
            tt("dve", K5(j, 13), K5(j, 6), K5(j, 1), ALU.mult, PK, PK)
            tt("dve", K5(j, 12), K5(j, 12), K5(j, 13), ALU.add, PK, PK)
            tt("dve", K5(j, 7), K5(j, 12), K5(j, 9), ALU.mult, PK, PK)
            tt("dve", K5(j, 12), K5(j, 6), K5(j, 0), ALU.mult, PK, PK)
            tt("dve", K5(j, 13), K5(j, 10), K5(j, 1), ALU.mult, PK, PK)
            tt("dve", K5(j, 12), K5(j, 12), K5(j, 13), ALU.subtract, PK, PK)
            tt("dve", K5(j, 8), K5(j, 12), K5(j, 9), ALU.mult, PK, PK)
            ts("dve", K5(j, 10), K5(j, 8), -1.0, None, ALU.mult, None, PK, PK)
            hflat = h[:, :, :].rearrange("p k t -> p (k t)")
            za = big16[:, 0:T]
            zb = big16[:, T:2 * T]
            zc = hflat[:, 0:T]
            zd = hflat[:, T:2 * T]
            ZK = ["zq"]
            for q in range(NQ):
                b_ = q % 2
                stg = tmp[:, 4256 + b_ * 2 * T:4256 + (b_ + 1) * 2 * T].rearrange("p (a t) -> p a t", a=2)
                SK = [("tabstg", b_)]
                ts("dve", za, tau[:, :], s5k[:, j, 4, q:q + 1], None, ALU.mult, None, PK + ZK + ["tau"], ZK)
                range_reduce(zd, za, zb.bitcast(I32), zc, ZK)
                actf(stg[:, 1, :], zd, AF.Sin, ZK, SK)
                ts("dve", za, za, math.pi / 2, None, ALU.add, None, ZK, ZK)
                range_reduce(zd, za, zb.bitcast(I32), zc, ZK)
                actf(stg[:, 0, :], zd, AF.Sin, ZK, SK)
                dma("sp", s5tab_d[j][q].rearrange("p (a t) -> p a t", a=2), stg, f"tbw{b_}", SK, [("s5tab", j)])
            if s5_stage < 1:
                return
            ta = tmp[:, 4224:4240]
            tb = tmp[:, 4240:4256]
            for q in range(NQ):
                k, r4 = divmod(q, 4)
                br = s5nat[:, 0, q, :]
                bi = s5nat[:, 1, q, :]
                ts("dve", ta, br, s5k[:, j, 7, q:q + 1], None, ALU.mult, None, PK, PK)
                stt(ta, bi, s5k[:, j, 10, q:q + 1], ta, ALU.mult, ALU.add, PK, PK)
                ts("dve", tb, bi, s5k[:, j, 7, q:q + 1], None, ALU.mult, None, PK, PK)
                stt(tb, br, s5k[:, j, 8, q:q + 1], tb, ALU.mult, ALU.add, PK, PK)
                FK = ["s5full"]
                memset("dve", s5full[:, :, :], 0.0, FK)
                for a_, src in ((0, ta), (1, tb)):
                    for gl in range(2):
                        ts("dve", s5full[:, a_, 32 * r4 + 16 * gl:32 * r4 + 16 * gl + 16], src, masks[:, 16 + gl:17 + gl], None,
                           ALU.mult, None, PK + FK + ["masks"], FK)
                pi = psn()
                tr(pi, slice(0, 128), s5full[:, 0, :], FK, last=False)
                tr(pi, slice(128, 256), s5full[:, 1, :], FK, last=True)
                cp("act", s5bcs[:, 0:2, :], ps[pi][:, 0:256].rearrange("p (a n) -> p a n", a=2), [("ps", pi)], ["s5bcs"])
                for a_ in range(2):
                    for gl in range(2):
                        mcol = (8 if a_ else 0) + 2 * r4 + gl
                        ts("dve", s5full[:, a_, 64 * gl:64 * gl + 64], s5cn[:, a_, k, :], masks[:, mcol:mcol + 1], None,
                           ALU.mult, None, PK + FK + ["masks"], FK)
                pi = psn()
                tr(pi, slice(0, 128), s5full[:, 0, :], FK, last=False)
                tr(pi, slice(128, 256), s5full[:, 1, :], FK, last=True)
                cp("act", s5bcs[:, 2:4, :], ps[pi][:, 0:256].rearrange("p (a n) -> p a n", a=2), [("ps", pi)], ["s5bcs"])
                dma("sp", s5bc_d[j][:, :, q * 128:(q + 1) * 128].rearrange("a p n -> p a n"), s5bcs[:, :, :], "scr",
                    ["s5bcs"], [("s5bc", j)])

        def s5_slabs(j):
            slab_list.append(slab_cols("s5_w_in", j, KC, [(0, 1024)]))
            for half in range(2):
                def fn(dst2d, j=j, half=half):
                    dst = dst2d[:, 0:8192].rearrange("p (k n) -> p k n", k=64)
                    src = s5bc_d[j][2 * half:2 * half + 2, :, :].rearrange("a p (q n) -> p a q n", q=NQ)
                    return [(dst[:, 0:32, :], src[:, 0, :, :]), (dst[:, 32:64, :], src[:, 1, :, :])], "pool"
                slab_list.append(SlabDef(fn, reads=[("s5bc", j)], direct=True))
            for i2 in range(2):
                slab_list.append(slab_cols("s5_w_glu", j, KC, [(512 * i2, 512), (1024 + 512 * i2, 512)]))
            slab_list.append(slab_cols("s5_w_out", j, KC, [(0, 1024)]))

        def phase_s5(j):
            slot_i, g_i = next_slab()
            wi_ = ring_view(slot_i, KC, 1024)
            for c in range(KC):
                pi = psn()
                for k in range(KC):
                    mm(pi, wi_[:, k, 128 * c:128 * c + 128], hn[:, k, :], k == 0, k == KC - 1, [("ring", slot_i), ("hn", k)])
                cp("act", u32[:, c, :], ps[pi][:, :], [("ps", pi)], [("big16", c)])
                cp("dve", sq[:, c, :], u32[:, c, :], [("big16", c)], [("sq", c)])
            release(g_i)
            slot_b, g_b = next_slab()
            wB = ring_view(slot_b, 64, 128)
            slot_c, g_c = next_slab()
            wC = ring_view(slot_c, 64, 128)
            if s5_stage in (2, 20, 21):
                pi = psn()
                mm(pi, wB[:, 0, :], sq[:, 0, :], True, True, [("ring", slot_b), ("sq", 0)])
                pi = psn()
                mm(pi, wC[:, 0, :], sq[:, 0, :], True, True, [("ring", slot_c), ("sq", 0)])
                release(g_b)
                release(g_c)
                for _ in range(3):
                    sl_, g_ = next_slab()
                    pi = psn()
                    mm(pi, ring[:, sl_, 0:128], sq[:, 0, :], True, True, [("ring", sl_), ("sq", 0)])
                    release(g_)
                return
            ps_mod[0] = 6
            ps_i[0] = 0
            qst = {}
            free = list(range(NTMP))

            def talloc():
                assert free, "S5 tmp slots exhausted"
                return free.pop(0)

            def tfree(*ids):
                free.extend(ids)

            def load_tab(q):
                slot = q % 3
                dma("sp", tabr[:, slot, :, :], s5tab_d[j][q].rearrange("p (a t) -> p a t", a=2), f"tb{slot}",
                    [("s5tab", j)], [("tabr", slot)])

            def s5_a(q):
                c = q // 4
                sl = q % 3
                Cq = tabr[:, sl, 0, :]
                Sq = tabr[:, sl, 1, :]
                TK = [("tabr", sl)]
                pbr = psn()
                mm(pbr, wB[:, q, :], sq[:, c, :], True, True, [("ring", slot_b), ("sq", c)])
                pbi = psn()
                mm(pbi, wB[:, 32 + q, :], sq[:, c, :], True, True, [("ring", slot_b), ("sq", c)])
                br_, bi_, t_, mr, u_, v_ = [talloc() for _ in range(6)]
                cp("act", TM(br_), ps[pbr][:, :], [("ps", pbr)], [("tmp", br_)])
                cp("act", TM(bi_), ps[pbi][:, :], [("ps", pbi)], [("tmp", bi_)])
                tt("dve", TM(t_), TM(br_), Cq, ALU.mult, [("tmp", br_)] + TK, [("tmp", t_)])
                tt("dve", TM(mr), TM(bi_), Sq, ALU.mult, [("tmp", bi_)] + TK, [("tmp", mr)])
                tt("dve", TM(mr), TM(mr), TM(t_), ALU.add, [("tmp", mr), ("tmp", t_)], [("tmp", mr)])
                tt(S5_ENG2, TM(u_), TM(bi_), Cq, ALU.mult, [("tmp", bi_)] + TK, [("tmp", u_)])
                tt(S5_ENG2, TM(v_), TM(br_), Sq, ALU.mult, [("tmp", br_)] + TK, [("tmp", v_)])
                tt(S5_ENG2, TM(u_), TM(u_), TM(v_), ALU.subtract, [("tmp", u_), ("tmp", v_)], [("tmp", u_)])
                qst[q] = (br_, bi_, t_, mr, u_, v_)

            def s5_b(q):
                c, qq = divmod(q, 4)
                py = 6 + (c % 2)
                sl = q % 3
                Cq = tabr[:, sl, 0, :]
                Sq = tabr[:, sl, 1, :]
                Cend = tabr[:, sl, 0, T - 1:T]
                Send = tabr[:, sl, 1, T - 1:T]
                TK = [("tabr", sl)]
                br_, bi_, t_, mr, mi, v_ = qst.pop(q)
                rho = s5k[:, j, 3, q:q + 1].to_broadcast([128, T])
                gr, gi = talloc(), talloc()
                cr = s5carry[:, j, 0, q:q + 1]
                ci = s5carry[:, j, 1, q:q + 1]
                CK = [("s5carry", q)]
                P.op("dve", lambda e, gr=gr, mr=mr, cr=cr, rho=rho: e.tensor_tensor_scan(
                    out=TM(gr), data0=rho, data1=TM(mr), initial=cr, op0=ALU.mult, op1=ALU.add),
                    reads=[("tmp", mr), "s5p"] + CK, writes=[("tmp", gr)])
                P.op("dve", lambda e, gi=gi, mi=mi, ci=ci, rho=rho: e.tensor_tensor_scan(
                    out=TM(gi), data0=rho, data1=TM(mi), initial=ci, op0=ALU.mult, op1=ALU.add),
                    reads=[("tmp", mi), "s5p"] + CK, writes=[("tmp", gi)])
                gre = TM(gr, T - 1, 1)
                gie = TM(gi, T - 1, 1)
                tA = s5k[:, j, 14, q:q + 1]
                tB = s5k[:, j, 15, q:q + 1]
                ts("dve", tA, gie, Send, None, ALU.mult, None, [("tmp", gi)] + TK, [("s5t", q)])
                ts("dve", tB, gre, Send, None, ALU.mult, None, [("tmp", gr)] + TK, [("s5t", q)])
                stt(cr, gre, Cend, tA, ALU.mult, ALU.subtract, [("tmp", gr), ("s5t", q)] + TK, CK)
                stt(ci, gie, Cend, tB, ALU.mult, ALU.add, [("tmp", gi), ("s5t", q)] + TK, CK)
                hb = q % 2
                tt("dve", hrb[:, 2 * hb, :], TM(gr), Cq, ALU.mult, [("tmp", gr)] + TK, [("hrb", 2 * hb)])
                stt(hrb[:, 2 * hb + 1, :], TM(gi), -1.0, Sq, ALU.mult, ALU.mult, [("tmp", gi)] + TK, [("hrb", 2 * hb + 1)])
                tt(S5_ENG2, hib[:, 2 * hb, :], TM(gr), Sq, ALU.mult, [("tmp", gr)] + TK, [("hib", 2 * hb)])
                tt(S5_ENG2, hib[:, 2 * hb + 1, :], TM(gi), Cq, ALU.mult, [("tmp", gi)] + TK, [("hib", 2 * hb + 1)])
                mm(py, wC[:, q, :], hrb[:, 2 * hb, :], qq == 0, False, [("ring", slot_c), ("hrb", 2 * hb)])
                mm(py, wC[:, q, :], hrb[:, 2 * hb + 1, :], False, False, [("ring", slot_c), ("hrb", 2 * hb + 1)])
                mm(py, wC[:, 32 + q, :], hib[:, 2 * hb, :], False, False, [("ring", slot_c), ("hib", 2 * hb)])
                mm(py, wC[:, 32 + q, :], hib[:, 2 * hb + 1, :], False, qq == 3, [("ring", slot_c), ("hib", 2 * hb + 1)])
                tfree(br_, bi_, t_, mr, mi, v_, gr, gi)
                if qq == 3:
                    yy = talloc()
                    tfree(yy)
                    cp("act", TM(yy), ps[py][:, :], [("ps", py)], [("tmp", yy)])
                    stt(TM(yy), u32[:, c, :], C1("s5_d", j * KC + c), TM(yy), ALU.mult, ALU.add,
                        [("big16", c), ("tmp", yy), "cst"], [("tmp", yy)])
                    actf(act[:, 8 + c, :], TM(yy), AF.Gelu_apprx_tanh, [("tmp", yy)], [("act", 8 + c)])

            load_tab(0)
            load_tab(1)
            if S5_PIPE:
                s5_a(0)
                for q in range(NQ):
                    if q + 2 < NQ:
                        load_tab(q + 2)
                    ra = P.record(lambda: s5_a(q + 1)) if q + 1 < NQ else []
                    rb = P.record(lambda: s5_b(q))
                    P.emit([o for o in ra if o[0] != "dve"])
                    ad = [o for o in ra if o[0] == "dve"]
                    ai = 0
                    for o in rb:
                        P.emit([o])
                        if o[0] == "dve" and ai < len(ad):
                            P.emit([ad[ai]])
                            ai += 1
                    P.emit(ad[ai:])
            else:
                for q in range(NQ):
                    if q + 2 < NQ:
                        load_tab(q + 2)
                    s5_a(q)
                    s5_b(q)
            ps_mod[0] = 8
            ps_i[0] = 0
            tmp_i[0] = 0
            release(g_b)
            release(g_c)
            for i2 in range(2):
                slot_g, g_g = next_slab()
                wg = ring_view(slot_g, KC, 1024)
                for m4 in range(4):
                    m = 4 * i2 + m4
                    pv = psn()
                    for k in range(KC):
                        mm(pv, wg[:, k, 128 * m4:128 * m4 + 128], act[:, 8 + k, :], k == 0, k == KC - 1,
                           [("ring", slot_g), ("act", 8 + k)])
                    pg = psn()
                    for k in range(KC):
                        mm(pg, wg[:, k, 512 + 128 * m4:512 + 128 * m4 + 128], act[:, 8 + k, :], k == 0, k == KC - 1,
                           [("ring", slot_g), ("act", 8 + k)])
                    sg = tslot()
                    actf(TM(sg), ps[pg][:, :], AF.Sigmoid, [("ps", pg)], [("tmp", sg)])
                    tt("dve", act[:, 16 + m, :], ps[pv][:, :], TM(sg), ALU.mult, [("ps", pv), ("tmp", sg)], [("act", 16 + m)])
                release(g_g)
            slot_o, g_o = next_slab()
            wo = ring_view(slot_o, KC, 1024)
            for m in range(KC):
                pi = psn()
                for k in range(KC):
                    mm(pi, wo[:, k, 128 * m:128 * m + 128], act[:, 16 + k, :], k == 0, k == KC - 1,
                       [("ring", slot_o), ("act", 16 + k)])
                tt("dve", h[:, m, :], h[:, m, :], ps[pi][:, :], ALU.add, [("h", m), ("ps", pi)], [("h", m)])
            release(g_o)

        def phase_store(t):
            fs = [tslot() for _ in range(KC)]

            def o(k):
                return TM(fs[k]), ("tmp", fs[k])
            phase_norm("g_fin", 0, out_f32=o)
            for j in range(4):
                for hf in range(2):
                    pi = psn()
                    for kk in range(4):
                        k = 4 * hf + kk
                        tr(pi, slice(128 * kk, 128 * kk + 128), TM(fs[k], 128 * j, 128), [("tmp", fs[k])], last=(kk == 3))
                    cp("act" if hf else "dve", xin[:, j, 512 * hf:512 * hf + 512], ps[pi][:, :], [("ps", pi)],
                       [("big16", 2 * j + hf)])
            dma("sp", out_d[t * T:(t + 1) * T, :].rearrange("(j p) d -> p j d", p=128), xin, "ost",
                [("big16", i) for i in range(8)], [("outd", t)])

        EPS_AP = sb("eps_ap", [128, 1])
        ONE_AP = sb("one_ap", [128, 1])
        memset("dve", EPS_AP[:, :], EPS, ["eps"])
        memset("dve", ONE_AP[:, :], 1.0, ["one"])

        memset("dve", onesf[:, :], 1.0, ["onesf"])
        for gi_ in range(8):
            P.op("dve", lambda e, gi_=gi_: e.reduce_sum(out=masks[:, gi_:gi_ + 1], in_=ident[:, 16 * gi_:16 * gi_ + 16],
                                                        axis=mybir.AxisListType.X), reads=["ident"], writes=["masks"])
        for hf_ in range(2):
            P.op("dve", lambda e, hf_=hf_: e.reduce_sum(out=masks[:, 16 + hf_:17 + hf_], in_=ident[:, 64 * hf_:64 * hf_ + 64],
                                                        axis=mybir.AxisListType.X), reads=["ident"], writes=["masks"])
        ts("dve", masks[:, 8:16], masks[:, 0:8], -1.0, None, ALU.mult, None, ["masks"], ["masks"])
        if "s" in mix:
            for j_ in sorted({i // 2 for i in layers if i % 2 == 1}):
                if s5_stage != 21:
                    s5_prologue(j_)

        for i in layers:
            if i % 2 == 0 and "r" in mix:
                rg_slabs(i // 2)
            if i % 2 == 1 and "s" in mix and s5_stage >= 2:
                s5_slabs(i // 2)
            if "f" in mix:
                ffn_slabs(i)

        if USE_WSC:
            cast_all_slabs()
        P.barrier()
        for t in range(n_tiles):
            cur[0] = t
            phase_load(t)
            for i in layers:
                if i % 2 == 0 and "r" in mix:
                    phase_norm("g_mix", i * KC)
                    phase_rg(i // 2)
                if i % 2 == 1 and "s" in mix and s5_stage >= 2:
                    phase_norm("g_mix", i * KC)
                    phase_s5(i // 2)
                if "f" in mix:
                    phase_norm("g_ffn", i * KC)
                    phase_ffn(i)
            phase_store(t)
        P.final_wait("sp", [("outd", t) for t in range(n_tiles)] + [("dbg", i) for i in range(dbg_n[0])])
        P.dbg_names = dbg_names

        with nc.Block() as block:
            @block.sync
            def _(e):
                P.replay("sp", e, sems)

            @block.gpsimd
            def _(e):
                P.replay("pool", e, sems)

            @block.scalar
            def _(e):
                P.replay("act", e, sems)

            @block.vector
            def _(e):
                P.replay("dve", e, sems)

            @block.tensor
            def _(e):
                P.replay("pe", e, sems)
    return nc, P


def make_consts():
    return {"ident": np.eye(128, dtype=np.float32),
            "tau": np.tile(np.arange(1, T + 1, dtype=np.float32)[None, :], (128, 1))}


def shape_inputs(inputs):
    r = {}
    for k, v in inputs.items():
        if k == "x":
            continue
        v = np.ascontiguousarray(v, dtype=np.float32)
        if k == "norm_final_g":
            v = v.reshape(1, D)
        elif k in ("rg_b_a", "rg_b_x"):
            v = v.reshape(2, D)
        elif k in ("s5_a_re", "s5_a_im"):
            v = v.reshape(2, 4096)
        elif k in ("s5_b_re", "s5_b_im"):
            v = v.reshape(2, 4096, 16)
        elif k in ("s5_c_re", "s5_c_im"):
            v = v.reshape(2, 1024, 64)
        r[k] = v
    return r


def kernel(**inputs):
    x = np.ascontiguousarray(inputs["x"], dtype=np.float32)
    w = shape_inputs(inputs)
    w.update(make_consts())
    nc, _ = build_program()
    in_maps = [dict(w, x=x[b]) for b in range(BATCH)]
    res = run_bass_kernel_spmd(nc, in_maps, core_ids=list(range(BATCH)))
    return np.stack([res.results[b]["out"] for b in range(BATCH)], axis=0)
```

```python
import math
import numpy as np
import concourse.bass as bass
import concourse.mybir as mybir
from concourse.bass_utils import run_bass_kernel_spmd

F32 = mybir.dt.float32
BF16 = mybir.dt.bfloat16
AF = mybir.ActivationFunctionType
ALU = mybir.AluOpType

D = 1024
KC = 8
T = 512
SEQ = 8192
BATCH = 4
DEPTH = 4
DFF = 3072
NJ = 24
EPS = 1e-6
LSUB = 64
NQ = 32
ENGS = ("pe", "act", "dve", "pool", "sp")
SLAB = 8192
import os
S5_PIPE = os.environ.get('K_S5PIPE', '1') == '1'
S5_ENG2 = os.environ.get('K_S5ENG2', 'dve')
FFN_LOOK = int(os.environ.get('K_FFNLOOK', '2'))
RG_PIPE = os.environ.get('K_RGPIPE', '1') == '1'
USE_WSC = True
NSLOT = 3


class Prog:
    def __init__(self):
        self.prog = {e: [] for e in ENGS}
        self.count = {e: 0 for e in ENGS}
        self.dcount = {}
        self.seen = {e: {} for e in ENGS}
        self.last_write = {}
        self.readers = {}
        self.n_ops = 0
        self._rec = None

    def record(self, f):
        assert self._rec is None
        self._rec = []
        try:
            f()
            return self._rec
        finally:
            self._rec = None

    def emit(self, ops):
        for o in ops:
            self.op(*o)

    def _need(self, eng, waits, tok):
        if tok is None:
            return
        sk, v = tok
        if sk == eng and (eng in ("pe", "sp") or v > self.count[eng]):
            return
        if self.seen[eng].get(sk, 0) >= v:
            return
        if waits.get(sk, 0) < v:
            waits[sk] = v

    def op(self, eng, fn, reads=(), writes=(), track=True, dma=None, ninc=1):
        if self._rec is not None:
            self._rec.append((eng, fn, tuple(reads), tuple(writes), track, dma, ninc))
            return
        waits = {}
        for r in reads:
            self._need(eng, waits, self.last_write.get(r))
        for w in writes:
            self._need(eng, waits, self.last_write.get(w))
            for sk, v in self.readers.get(w, {}).items():
                self._need(eng, waits, (sk, v))
        for sk, v in waits.items():
            self.prog[eng].append(("wait", sk, v))
            self.seen[eng][sk] = v
        if dma is not None:
            prev = self.dcount.get(dma, 0)
            if prev > self.seen[eng].get(dma, 0) and (dma.startswith("cld") or dma.startswith("wcs")):
                self.prog[eng].append(("wait", dma, prev))
                self.seen[eng][dma] = prev
            self.dcount[dma] = self.dcount.get(dma, 0) + 16 * ninc
            tok = (dma, self.dcount[dma])
            self.prog[eng].append(("op", fn, dma, 16))
        elif track:
            self.count[eng] += 1
            tok = (eng, self.count[eng])
            self.prog[eng].append(("op", fn, eng, 1))
        else:
            tok = (eng, self.count[eng] + 1)
            self.prog[eng].append(("op", fn, None, 0))
        for w in writes:
            self.last_write[w] = tok
            self.readers[w] = {}
        for r in reads:
            d = self.readers.setdefault(r, {})
            if d.get(tok[0], 0) < tok[1]:
                d[tok[0]] = tok[1]
        self.n_ops += 1

    def barrier(self):
        snap = dict(self.count)
        dsnap = dict(self.dcount)
        for e in ENGS:
            for o, v in list(snap.items()) + list(dsnap.items()):
                if o.startswith("wcs"):
                    continue
                if o != e and v > 0 and self.seen[e].get(o, 0) < v:
                    self.prog[e].append(("wait", o, v))
                    self.seen[e][o] = v

    def final_wait(self, eng, keys):
        waits = {}
        for k in keys:
            self._need(eng, waits, self.last_write.get(k))
        for sk, v in waits.items():
            self.prog[eng].append(("wait", sk, v))
            self.seen[eng][sk] = v

    def replay(self, eng, handle, sems):
        for item in self.prog[eng]:
            if item[0] == "wait":
                handle.wait_ge(sems[item[1]], item[2])
            else:
                _, fn, sk, inc = item
                r = fn(handle)
                if sk is not None:
                    if isinstance(r, (list, tuple)):
                        for ins in r:
                            ins.then_inc(sems[sk], inc)
                    else:
                        r.then_inc(sems[sk], inc)


def build_program(n_tiles=SEQ // T, layers=(0, 1, 2, 3), seq=SEQ, mix="rsf", debug=0, s5_stage=4):
    nc = bass.Bass("TRN2", target_bir_lowering=False)
    P = Prog()

    def dram(name, shape, kind="ExternalInput", dt=F32):
        return nc.dram_tensor(name, list(shape), dt, kind=kind).ap()

    x_d = dram("x", [seq, D])
    out_d = dram("out", [seq, D], kind="ExternalOutput")
    ident_d = dram("ident", [128, 128])
    tau_d = dram("tau", [128, T])
    s5tab_d = [dram(f"s5tab{j}", [NQ, 128, 2 * T], kind="Internal") for j in range(2)]
    W = {}
    for name, shape in [
        ("norm_mix_g", [4, D]), ("norm_ffn_g", [4, D]), ("norm_final_g", [1, D]),
        ("rg_w_in", [2, D, 2 * D]), ("rg_conv_w", [2, 4, D]), ("rg_conv_b", [2, D]),
        ("rg_w_a", [2, 8, 128, 128]), ("rg_b_a", [2, D]), ("rg_w_x", [2, 8, 128, 128]),
        ("rg_b_x", [2, D]), ("rg_lambda", [2, D]), ("rg_w_out", [2, D, D]),
        ("s5_w_in", [2, D, D]), ("s5_a_re", [2, 4096]), ("s5_a_im", [2, 4096]),
        ("s5_log_dt", [2, 64]), ("s5_b_re", [2, 4096, 16]), ("s5_b_im", [2, 4096, 16]),
        ("s5_c_re", [2, 1024, 64]), ("s5_c_im", [2, 1024, 64]), ("s5_d", [2, D]),
        ("s5_w_glu", [2, D, 2 * D]), ("s5_w_out", [2, D, D]),
        ("ffn_w_up", [4, D, 2 * DFF]), ("ffn_conv_w", [4, 3, 2 * DFF]),
        ("ffn_conv_b", [4, 2 * DFF]), ("ffn_w_down", [4, DFF, D]),
    ]:
        W[name] = dram(name, shape)
    s5bc_d = [dram(f"s5bc{j}", [4, 128, NQ * 128], kind="Internal", dt=BF16) for j in range(2)]

    dbg_d = dram("dbg", [max(debug, 1), 128, T], kind="ExternalOutput") if debug else None
    dbg_n = [0]
    cur = [0]
    dbg_names = []

    import contextlib
    es = contextlib.ExitStack()
    with es:
        def sb(name, shape, dt=F32):
            return es.enter_context(nc.sbuf_tensor(name, list(shape), dt))

        big16 = sb("big16", [128, 4096])
        xin = big16[:, :].rearrange("p (j d) -> p j d", j=4)
        u32 = big16[:, :].rearrange("p (k t) -> p k t", k=KC)
        h = sb("h", [128, KC, T])
        sq = sb("sq", [128, KC, T], BF16)
        hn = sb("hn", [128, KC, T], BF16)
        rstd = sb("rstd", [128, T])
        ring = sb("ring", [128, NSLOT, SLAB], BF16)
        act = sb("act", [128, NJ, T], BF16)
        NTMP = 16
        TW = T + 4
        tmp = sb("tmp", [128, NTMP * TW])
        xcb = sb("xcb", [128, 2, T], BF16)
        hrb = sb("hrb", [128, 4, T], BF16)
        hib = sb("hib", [128, 4, T], BF16)
        ident = sb("ident_sb", [128, 128])
        identb = sb("identb", [128, 128], BF16)
        onesb = sb("onesb", [128, 128], BF16)
        tau = sb("tau_sb", [128, T])
        tabr = sb("tabr", [128, 3, 2, T])
        NCONST = 1152
        cst = sb("cst", [128, NCONST])
        stage = tmp[:, 0:1152].rearrange("p (b n) -> p b n", b=9)
        rgc = sb("rgc", [128, 2, 2, KC])
        halo_rg = sb("halo_rg", [128, 2, KC, 3])
        carry_rg = sb("carry_rg", [128, 2, KC])
        halo_ffn = sb("halo_ffn", [128, 4, 2 * NJ, 2], BF16)
        NUPB = 6
        upb = sb("upb", [128, NUPB, T + 2], BF16)
        dgs = sb("dgs", [128, NUPB, 3, 128], BF16)
        upb_i = [0]
        s5k = sb("s5k", [128, 2, 16, NQ])
        s5carry = sb("s5carry", [128, 2, 2, NQ])
        s5nat = tmp[:, 1152:1152 + 2048].rearrange("p (a q c) -> p a q c", a=4, q=NQ)
        s5cn = tmp[:, 3200:3200 + 1024].rearrange("p (a k n) -> p a k n", a=2, k=KC)
        s5full = sb("s5full", [128, 2, 128])
        s5bcs = sb("s5bcs", [128, 4, 128], BF16)
        masks = sb("masks", [128, 20])
        onesf = sb("onesf", [128, 128])
        ldrow = sb("ldrow", [1, 128])

        ps = [es.enter_context(nc.psum_tensor(f"ps{i}", [128, T], F32)) for i in range(8)]
        ps_i = [0]
        ps_mod = [8]

        def psn():
            i = ps_i[0] % ps_mod[0]
            ps_i[0] = (i + 1) % ps_mod[0]
            return i

        NCLD = 8
        sem_names = list(ENGS) + [f"w{s}" for s in range(NSLOT)] + [f"cld{i}" for i in range(NCLD)] + ["xld", "ost", "scr", "tb0", "tb1", "tb2", "tbw0", "tbw1"]
        cld_i = [0]
        NWCS = 8
        wcs_i = [0]
        sem_names += [f"wcs{i_}" for i_ in range(NWCS)]
        sems = {n: es.enter_context(nc.semaphore(n)) for n in sem_names}

        def mm(pi, lhsT, rhs, start, stop, reads, cols=slice(0, T)):
            P.op("pe", lambda e: e.matmul(ps[pi][:, cols], lhsT=lhsT, rhs=rhs, start=start, stop=stop),
                 reads=reads, writes=[("ps", pi)], track=stop)

        def tr(pi, cols, in_, reads, last=True):
            P.op("pe", lambda e: e.transpose(ps[pi][:, cols], in_, ident[:, :]),
                 reads=list(reads) + ["ident"], writes=[("ps", pi)], track=last)

        def actf(out, in_, func, reads, writes, bias=None, scale=None):
            kw = {}
            if bias is not None:
                kw["bias"] = bias
            if scale is not None:
                kw["scale"] = scale
            P.op("act", lambda e: e.activation(out=out, in_=in_, func=func, **kw), reads=reads, writes=writes)

        def tt(eng, out, in0, in1, op, reads, writes):
            P.op(eng, lambda e: e.tensor_tensor(out=out, in0=in0, in1=in1, op=op), reads=reads, writes=writes)

        def ts(eng, out, in0, s1, s2, op0, op1, reads, writes):
            if op1 is None:
                P.op(eng, lambda e: e.tensor_scalar(out=out, in0=in0, scalar1=s1, scalar2=None, op0=op0),
                     reads=reads, writes=writes)
            else:
                P.op(eng, lambda e: e.tensor_scalar(out=out, in0=in0, scalar1=s1, scalar2=s2, op0=op0, op1=op1),
                     reads=reads, writes=writes)

        def stt(out, in0, scalar, in1, op0, op1, reads, writes):
            P.op("dve", lambda e: e.scalar_tensor_tensor(out=out, in0=in0, scalar=scalar, in1=in1, op0=op0, op1=op1),
                 reads=reads, writes=writes)

        def cp(eng, out, in_, reads, writes):
            if eng == "act":
                P.op("act", lambda e: e.copy(out=out, in_=in_), reads=reads, writes=writes)
            else:
                P.op(eng, lambda e: e.tensor_copy(out=out, in_=in_), reads=reads, writes=writes)

        def memset(eng, ap, val, writes):
            P.op(eng, lambda e: e.memset(ap, val), writes=writes)

        def dma(eng, out, in_, sem, reads, writes, slow=False):
            if sem == "cld":
                sem = f"cld{cld_i[0]}"
                cld_i[0] = (cld_i[0] + 1) % NCLD
            if sem == "wcs":
                sem = f"wcs{wcs_i[0]}"
                wcs_i[0] = (wcs_i[0] + 1) % NWCS
            kw = {"allow_slow_non_contiguous": True} if slow else {}
            P.op(eng, lambda e: e.dma_start(out=out, in_=in_, **kw), reads=reads, writes=writes, dma=sem)

        def dbg(name, ap, key):
            if not debug or dbg_n[0] >= debug or cur[0] != n_tiles - 1:
                return
            i = dbg_n[0]
            dbg_n[0] += 1
            dbg_names.append(name)
            dma("sp", dbg_d[i], ap, "scr", [key], [("dbg", i)])

        tmp_i = [0]

        def tslot():
            i = tmp_i[0]
            tmp_i[0] = (i + 1) % NTMP
            return i

        def TM(i, c0=0, n=T):
            return tmp[:, i * TW + c0:i * TW + c0 + n]

        slab_list = []
        slab_state = {"issued": 0, "used": 0}
        PF = NSLOT - 1

        wsc_box = [None]

        def issue_slab(gidx):
            idx = gidx % len(slab_list)
            slot = gidx % NSLOT
            sd = slab_list[idx]
            if sd.direct or not USE_WSC:
                pairs, q = sd(ring[:, slot, :])
                if USE_WSC:
                    q = "sp"

                def fn(e, pairs=pairs):
                    return [e.dma_start(out=o, in_=i) for (o, i) in pairs]
                P.op(q, fn, reads=[r for r in sd.reads], writes=[("ring", slot)],
                     dma=f"w{slot}", ninc=len(pairs))
            else:
                n = sd.size
                src = wsc_box[0][idx, :, 0:n]
                P.op("sp", lambda e, slot=slot, n=n, src=src: e.dma_start(out=ring[:, slot, 0:n], in_=src),
                     reads=[("wsc", idx)], writes=[("ring", slot)], dma=f"w{slot}", ninc=1)

        def cast_all_slabs():
            wsc_box[0] = dram("wsc", [len(slab_list), 128, SLAB], kind="Internal", dt=BF16)
            for idx, sd in enumerate(slab_list):
                if sd.direct:
                    continue
                pairs, _ = sd(wsc_box[0][idx])
                for (o, i) in pairs:
                    dma("pool", o, i, "wcs", [], [("wsc", idx)])

        released = set()

        def pump():
            total = len(slab_list) * n_tiles
            while slab_state["issued"] < min(slab_state["used"] + PF + 1, total):
                n = slab_state["issued"]
                if n >= NSLOT and (n - NSLOT) not in released:
                    break
                issue_slab(n)
                slab_state["issued"] += 1

        def next_slab():
            g = slab_state["used"]
            slab_state["used"] += 1
            pump()
            assert slab_state["issued"] > g, "slab ring deadlock: too many live slabs"
            return g % NSLOT, g

        def release(g):
            released.add(g)
            pump()

        class SlabDef:
            def __init__(self, fn, reads=(), direct=False, size=SLAB):
                self.fn = fn
                self.reads = reads
                self.direct = direct
                self.size = size

            def __call__(self, dst2d):
                return self.fn(dst2d)

        def slab_cols(wname, l, kc, col_sets, q="pool"):
            ntot = sum(n for _, n in col_sets)

            def fn(dst2d):
                dst = dst2d[:, 0:kc * ntot].rearrange("p (k n) -> p k n", k=kc)
                src = W[wname][l].rearrange("(k p) n -> p k n", p=128)
                pairs = []
                o = 0
                for c0, n in col_sets:
                    pairs.append((dst[:, :, o:o + n], src[:, :, c0:c0 + n]))
                    o += n
                return pairs, q
            return SlabDef(fn, size=kc * ntot)

        def ring_view(slot, kc, n):
            return ring[:, slot, 0:kc * n].rearrange("p (k n) -> p k n", k=kc)

        crow = {}
        vec_list = []

        def addvec(name, ap2d):
            crow[name] = sum(v[1] for v in vec_list)
            vec_list.append((name, ap2d.shape[0], ap2d))

        addvec("g_mix", W["norm_mix_g"].rearrange("l (k p) -> (l k) p", p=128))
        addvec("g_ffn", W["norm_ffn_g"].rearrange("l (k p) -> (l k) p", p=128))
        addvec("g_fin", W["norm_final_g"].rearrange("l (k p) -> (l k) p", p=128))
        addvec("rg_cw", W["rg_conv_w"].rearrange("l t (k p) -> (l t k) p", p=128))
        addvec("rg_cb", W["rg_conv_b"].rearrange("l (k p) -> (l k) p", p=128))
        addvec("rg_ba", W["rg_b_a"].rearrange("l (k p) -> (l k) p", p=128))
        addvec("rg_bx", W["rg_b_x"].rearrange("l (k p) -> (l k) p", p=128))
        addvec("rg_lam", W["rg_lambda"].rearrange("l (k p) -> (l k) p", p=128))
        addvec("s5_d", W["s5_d"].rearrange("l (k p) -> (l k) p", p=128))
        addvec("ffn_cw", W["ffn_conv_w"].rearrange("l t (k p) -> (l t k) p", p=128))
        addvec("ffn_cb", W["ffn_conv_b"].rearrange("l (k p) -> (l k) p", p=128))
        addvec("s5_are", W["s5_a_re"].rearrange("l (q p) -> (l q) p", p=128))
        addvec("s5_aim", W["s5_a_im"].rearrange("l (q p) -> (l q) p", p=128))
        nrows = sum(v[1] for v in vec_list)
        assert nrows <= NCONST, nrows

        def C1(name, idx):
            c = crow[name] + idx
            return cst[:, c:c + 1]

        dma("sp", ident[:, :], ident_d[:, :], "cld", [], ["ident"])
        dma("sp", tau[:, :], tau_d[:, :], "cld", [], ["tau"])
        cp("dve", identb[:, :], ident[:, :], ["ident"], ["identb"])
        memset("dve", onesb[:, :], 1.0, ["onesb"])
        memset("dve", halo_rg[:, :, :, :], 0.0, [("halo_rg", l_, c_) for l_ in range(2) for c_ in range(KC)])
        memset("dve", carry_rg[:, :, :], 0.0, [("carry_rg", l_, c_) for l_ in range(2) for c_ in range(KC)])
        memset("dve", halo_ffn[:, :, :, :], 0.0, [("halo_ffn", i_, c_) for i_ in range(4) for c_ in range(2 * NJ)])
        memset("dve", s5carry[:, :, :, :], 0.0, ["s5carry"])
        nblk = (nrows + 127) // 128
        memset("dve", stage[:, :, :], 0.0, [("stage", b) for b in range(nblk)])
        r0 = 0
        for name, n, ap2d in vec_list:
            done = 0
            while done < n:
                blk, off = divmod(r0 + done, 128)
                take = min(n - done, 128 - off)
                dma("sp", stage[off:off + take, blk, :], ap2d[done:done + take, :], "cld", [], [("stage", blk)])
                done += take
            r0 += n
        for blk in range(nblk):
            pi = psn()
            tr(pi, slice(0, 128), stage[:, blk, :], [("stage", blk)])
            cp("dve", cst[:, blk * 128:(blk + 1) * 128], ps[pi][:, 0:128], [("ps", pi)], ["cst"])
        for l in range(2):
            lam = cst[:, crow["rg_lam"] + l * KC: crow["rg_lam"] + (l + 1) * KC]
            actf(rgc[:, l, 0, :], lam, AF.Exp, ["cst"], ["rgc"], scale=-1.0)
            actf(rgc[:, l, 0, :], rgc[:, l, 0, :], AF.Ln, ["rgc"], ["rgc"], bias=1.0)
            ts("dve", rgc[:, l, 1, :], rgc[:, l, 0, :], -16.0, None, ALU.mult, None, ["rgc"], ["rgc"])
            ts("dve", rgc[:, l, 0, :], rgc[:, l, 0, :], -8.0, None, ALU.mult, None, ["rgc"], ["rgc"])

        def phase_load(t):
            dma("sp", xin, x_d[t * T:(t + 1) * T, :].rearrange("(j p) d -> p j d", p=128), "xld",
                [], [("big16", i) for i in range(8)])
            for k in range(KC):
                pi = psn()
                for j in range(4):
                    tr(pi, slice(128 * j, 128 * j + 128), xin[:, j, 128 * k:128 * k + 128],
                       [("big16", 2 * j), ("big16", 2 * j + 1)], last=(j == 3))
                cp("act" if k % 2 else "dve", h[:, k, :], ps[pi][:, :], [("ps", pi)], [("h", k)])

        def phase_norm(gname, gidx0, out_f32=None):
            for k in range(KC):
                actf(sq[:, k, :], h[:, k, :], AF.Square, [("h", k)], [("sq", k)])
            pi = psn()
            for k in range(KC):
                mm(pi, onesb[:, :], sq[:, k, :], k == 0, k == KC - 1, ["onesb", ("sq", k)])
            actf(rstd[:, :], ps[pi][:, :], AF.Sqrt, [("ps", pi), "eps"], ["rstd"], bias=EPS_AP[:, :], scale=1.0 / D)
            P.op("dve", lambda e: e.reciprocal(out=rstd[:, :], in_=rstd[:, :]), reads=["rstd"], writes=["rstd"])
            for k in range(KC):
                if out_f32 is None:
                    stt(hn[:, k, :], h[:, k, :], C1(gname, gidx0 + k), rstd[:, :], ALU.mult, ALU.mult,
                        [("h", k), "rstd", "cst"], [("hn", k)])
                else:
                    o, key = out_f32(k)
                    stt(o, h[:, k, :], C1(gname, gidx0 + k), rstd[:, :], ALU.mult, ALU.mult,
                        [("h", k), "rstd", "cst"], [key])

        def phase_ffn(i):
            cw0 = crow["ffn_cw"] + i * 3 * 48
            cb0 = crow["ffn_cb"] + i * 48
            slabs = {}
            st = {}

            def get_slab(si):
                if si not in slabs:
                    slabs[si] = next_slab()
                return slabs[si]

            def stage_a(hx):
                jg, half = divmod(hx, 2)
                si, cc = divmod(jg, 4)
                slot, gsl = get_slab(si)
                wv = ring_view(slot, KC, 1024)
                ch = jg + NJ * half
                pi = psn()
                for k in range(KC):
                    mm(pi, wv[:, k, 512 * half + 128 * cc: 512 * half + 128 * cc + 128], hn[:, k, :],
                       k == 0, k == KC - 1, [("ring", slot), ("hn", k)])
                if hx % 8 == 7:
                    release(gsl)
                ub_i = upb_i[0]
                upb_i[0] = (ub_i + 1) % NUPB
                cp("pool", upb[:, ub_i, 0:2], halo_ffn[:, i, ch, :], [("halo_ffn", i, ch)], [("upbh", ub_i)])
                cp("act", upb[:, ub_i, 2:2 + T], ps[pi][:, :], [("ps", pi)], [("upb", ub_i)])
                cp("pool", halo_ffn[:, i, ch, :], upb[:, ub_i, T:T + 2], [("upb", ub_i)], [("halo_ffn", i, ch)])
                for t3 in range(3):
                    actf(dgs[:, ub_i, t3, :], identb[:, :], AF.Copy, ["identb", "cst"], [("dgs", ub_i, t3)],
                         scale=cst[:, cw0 + 48 * t3 + ch:cw0 + 48 * t3 + ch + 1])
                st[hx] = ub_i

            def stage_b(hx):
                jg, half = divmod(hx, 2)
                ub_i = st.pop(hx)
                pc = psn()
                for t3 in range(3):
                    mm(pc, dgs[:, ub_i, t3, :], upb[:, ub_i, t3:t3 + T], t3 == 0, t3 == 2,
                       [("dgs", ub_i, t3), ("upb", ub_i), ("upbh", ub_i)])
                if half == 0:
                    g = tslot()
                    actf(TM(g), ps[pc][:, :], AF.Gelu_apprx_tanh, [("ps", pc), "cst"], [("tmp", g)],
                         bias=cst[:, cb0 + jg:cb0 + jg + 1])
                    st[("g", jg)] = g
                else:
                    g = st.pop(("g", jg))
                    stt(act[:, jg, :], ps[pc][:, :], cst[:, cb0 + NJ + jg:cb0 + NJ + jg + 1], TM(g), ALU.add, ALU.mult,
                        [("ps", pc), ("tmp", g), "cst"], [("act", jg)])

            NH = 2 * NJ
            LOOK = FFN_LOOK
            for hx in range(min(LOOK, NH)):
                stage_a(hx)
            for hx in range(NH):
                if hx + LOOK < NH:
                    stage_a(hx + LOOK)
                stage_b(hx)
            for s in range(4):
                slot, gsl = next_slab()
                wv = ring_view(slot, NJ, 256)
                for mh in range(2):
                    m = 2 * s + mh
                    pi = psn()
                    for j in range(NJ):
                        mm(pi, wv[:, j, 128 * mh:128 * mh + 128], act[:, j, :], j == 0, j == NJ - 1,
                           [("ring", slot), ("act", j)])
                    tt("dve", h[:, m, :], h[:, m, :], ps[pi][:, :], ALU.add, [("h", m), ("ps", pi)], [("h", m)])
                release(gsl)

        def ffn_slabs(i):
            for s in range(6):
                slab_list.append(slab_cols("ffn_w_up", i, KC, [(512 * s, 512), (DFF + 512 * s, 512)]))
            for s in range(4):
                slab_list.append(slab_cols("ffn_w_down", i, NJ, [(256 * s, 256)]))

        def phase_rg(l):
            slot_g, g_g = next_slab()
            wg = ring_view(slot_g, KC, 1024)
            for c in range(KC):
                pg = psn()
                for k in range(KC):
                    mm(pg, wg[:, k, 128 * c:128 * c + 128], hn[:, k, :], k == 0, k == KC - 1, [("ring", slot_g), ("hn", k)])
                actf(act[:, 8 + c, :], ps[pg][:, :], AF.Gelu_apprx_tanh, [("ps", pg)], [("act", 8 + c)])
            release(g_g)
            slot_x, g_x = next_slab()
            wx = ring_view(slot_x, KC, 1024)
            slot_a, g_a = next_slab()
            wa = ring_view(slot_a, 16, 128)
            RG_BATCH = 4

            def rg_a(c):
                pi = psn()
                for k in range(KC):
                    mm(pi, wx[:, k, 128 * c:128 * c + 128], hn[:, k, :], k == 0, k == KC - 1, [("ring", slot_x), ("hn", k)])
                u = 12 + (c % 4)
                cp("pool", TM(u, 0, 3), halo_rg[:, l, c, :], [("halo_rg", l, c)], [("tmp", u)])
                cp("act", TM(u, 3, T), ps[pi][:, :], [("ps", pi)], [("tmp", u)])
                cp("pool", halo_rg[:, l, c, :], TM(u, T, 3), [("tmp", u)], [("halo_rg", l, c)])
                return u

            def rg_conv(c, u):
                xc = u32[:, c, :]
                XK = [("big16", c)]
                ts("dve", xc, TM(u, 0, T), C1("rg_cw", l * 32 + 0 * KC + c), C1("rg_cb", l * KC + c),
                   ALU.mult, ALU.add, [("tmp", u), "cst"], XK)
                for k in range(1, 4):
                    stt(xc, TM(u, k, T), C1("rg_cw", l * 32 + k * KC + c), xc, ALU.mult, ALU.add,
                        [("tmp", u), "cst"] + XK, XK)
                cp("pool", sq[:, c, :], xc, XK, [("sq", c)])

            us = {0: rg_a(0)}
            for c in range(KC):
                if c + 1 < KC:
                    us[c + 1] = rg_a(c + 1)
                rg_conv(c, us.pop(c))
            release(g_x)

            for c0 in range(0, KC, RG_BATCH):
                cs = list(range(c0, c0 + RG_BATCH))
                sl = {c: (c % 4, 4 + c % 4, 8 + c % 4) for c in cs}
                for c in cs:
                    r, ig, a = sl[c]
                    pa = psn()
                    mm(pa, wa[:, c, :], sq[:, c, :], True, True, [("ring", slot_a), ("sq", c)])
                    px = psn()
                    mm(px, wa[:, 8 + c, :], sq[:, c, :], True, True, [("ring", slot_a), ("sq", c)])
                    actf(TM(r), ps[pa][:, :], AF.Sigmoid, [("ps", pa), "cst"], [("tmp", r)], bias=C1("rg_ba", l * KC + c))
                    actf(TM(ig), ps[px][:, :], AF.Sigmoid, [("ps", px), "cst"], [("tmp", ig)], bias=C1("rg_bx", l * KC + c))
                for c in cs:
                    r, ig, a = sl[c]
                    actf(TM(a), TM(r), AF.Exp, [("tmp", r), "rgc"], [("tmp", a)], scale=rgc[:, l, 0, c:c + 1])
                    actf(TM(r), TM(r), AF.Exp, [("tmp", r), "rgc"], [("tmp", r)], scale=rgc[:, l, 1, c:c + 1])
                for c in cs:
                    r, ig, a = sl[c]
                    actf(TM(r), TM(r), AF.Sqrt, [("tmp", r), "one"], [("tmp", r)], bias=ONE_AP[:, :], scale=-1.0)
                for c in cs:
                    r, ig, a = sl[c]
                    tt("dve", TM(ig), TM(ig), u32[:, c, :], ALU.mult, [("tmp", ig), ("big16", c)], [("tmp", ig)])
                    tt("dve", TM(ig), TM(ig), TM(r), ALU.mult, [("tmp", ig), ("tmp", r)], [("tmp", ig)])
                    hs = 12 + (c % 4)
                    P.op("dve", lambda e, hs=hs, a=a, ig=ig, c=c: e.tensor_tensor_scan(
                        out=TM(hs), data0=TM(a), data1=TM(ig), initial=carry_rg[:, l, c:c + 1],
                        op0=ALU.mult, op1=ALU.add),
                        reads=[("tmp", a), ("tmp", ig), ("carry_rg", l, c)], writes=[("tmp", hs)])
                    cp("dve", carry_rg[:, l, c:c + 1], TM(hs, T - 1, 1), [("tmp", hs)], [("carry_rg", l, c)])
                    tt("dve", act[:, c, :], TM(hs), act[:, 8 + c, :], ALU.mult, [("tmp", hs), ("act", 8 + c)], [("act", c)])
            tmp_i[0] = 0
            release(g_a)
            slot_o, g_o = next_slab()
            wo = ring_view(slot_o, KC, 1024)
            for m in range(KC):
                pi = psn()
                for c in range(KC):
                    mm(pi, wo[:, c, 128 * m:128 * m + 128], act[:, c, :], c == 0, c == KC - 1, [("ring", slot_o), ("act", c)])
                tt("dve", h[:, m, :], h[:, m, :], ps[pi][:, :], ALU.add, [("h", m), ("ps", pi)], [("h", m)])
            release(g_o)

        def rg_slabs(l):
            slab_list.append(slab_cols("rg_w_in", l, KC, [(1024, 1024)]))
            slab_list.append(slab_cols("rg_w_in", l, KC, [(0, 1024)]))

            def fn(dst2d, l=l):
                dst = dst2d[:, 0:2048].rearrange("p (k n) -> p k n", k=16)
                return [(dst[:, 0:8, :], W["rg_w_a"][l].rearrange("h i j -> i h j")),
                        (dst[:, 8:16, :], W["rg_w_x"][l].rearrange("h i j -> i h j"))], "pool"
            slab_list.append(SlabDef(fn, size=2048))
            slab_list.append(slab_cols("rg_w_out", l, KC, [(0, 1024)]))


        I32 = mybir.dt.int32
        TWO_PI = 2.0 * math.pi

        def K5(j, idx, q0=0, n=NQ):
            return s5k[:, j, idx, q0:q0 + n]

        def range_reduce(eng_out_ap, z_ap, zi_ap, kf_ap, width_keys):
            C1_ = 6.28125
            C2_ = TWO_PI - C1_
            PI_LO = 3.1415925
            ts("dve", zi_ap, z_ap, 1.0 / TWO_PI, None, ALU.mult, None, width_keys, width_keys)
            cp("dve", kf_ap, zi_ap, width_keys, width_keys)
            stt(eng_out_ap, kf_ap, -C1_, z_ap, ALU.mult, ALU.add, width_keys, width_keys)
            stt(eng_out_ap, kf_ap, -C2_, eng_out_ap, ALU.mult, ALU.add, width_keys, width_keys)
            ts("dve", eng_out_ap, eng_out_ap, -PI_LO, PI_LO, ALU.max, ALU.min, width_keys, width_keys)

        def s5_prologue(j):
            PK = ["s5p"]
            cp("dve", K5(j, 0), cst[:, crow["s5_are"] + NQ * j:crow["s5_are"] + NQ * (j + 1)], ["cst"], PK)
            cp("dve", K5(j, 1), cst[:, crow["s5_aim"] + NQ * j:crow["s5_aim"] + NQ * (j + 1)], ["cst"], PK)
            dma("sp", ldrow[0:1, 0:64], W["s5_log_dt"][j:j + 1, :], "cld", [], PK)
            for q0 in range(0, NQ, 4):
                dma("sp", s5nat[:, 0, q0:q0 + 4, :], W["s5_b_re"][j].rearrange("(q p) c -> p q c", p=128)[:, q0:q0 + 4, :], "cld", [], PK)
                dma("sp", s5nat[:, 1, q0:q0 + 4, :], W["s5_b_im"][j].rearrange("(q p) c -> p q c", p=128)[:, q0:q0 + 4, :], "cld", [], PK)
            dma("sp", s5cn[:, 0, :, :], W["s5_c_re"][j].rearrange("(k r) n -> r k n", r=128), "cld", [], PK)
            dma("sp", s5cn[:, 1, :, :], W["s5_c_im"][j].rearrange("(k r) n -> r k n", r=128), "cld", [], PK)
            pi = psn()
            P.op("pe", lambda e: e.matmul(ps[pi][:, 0:64], lhsT=onesf[0:1, :], rhs=ldrow[0:1, 0:64], start=True, stop=True),
                 reads=PK + ["onesf"], writes=[("ps", pi)])
            pv = ps[pi][:, 0:64].rearrange("p (q g) -> p q g", g=2)
            cp("dve", s5k[0:64, j, 2, :], pv[0:64, :, 0], [("ps", pi)], PK)
            cp("dve", s5k[64:128, j, 2, :], pv[64:128, :, 1], [("ps", pi)], PK)
            actf(K5(j, 2), K5(j, 2), AF.Exp, PK, PK)
            tt("dve", K5(j, 12), K5(j, 0), K5(j, 2), ALU.mult, PK, PK)
            actf(K5(j, 3), K5(j, 12), AF.Exp, PK, PK)
            tt("dve", K5(j, 4), K5(j, 1), K5(j, 2), ALU.mult, PK, PK)
            zi = s5k[:, j, 13, :].bitcast(I32)
            range_reduce(K5(j, 12), K5(j, 4), zi, K5(j, 14), PK)
            actf(K5(j, 6), K5(j, 12), AF.Sin, PK, PK)
            ts("dve", K5(j, 15), K5(j, 4), math.pi / 2, None, ALU.add, None, PK, PK)
            range_reduce(K5(j, 12), K5(j, 15), zi, K5(j, 14), PK)
            actf(K5(j, 5), K5(j, 12), AF.Sin, PK, PK)
            tt("dve", K5(j, 5), K5(j, 5), K5(j, 3), ALU.mult, PK, PK)
            tt("dve", K5(j, 6), K5(j, 6), K5(j, 3), ALU.mult, PK, PK)
            tt("dve", K5(j, 12), K5(j, 0), K5(j, 0), ALU.mult, PK, PK)
            tt("dve", K5(j, 13), K5(j, 1), K5(j, 1), ALU.mult, PK, PK)
            tt("dve", K5(j, 9), K5(j, 12), K5(j, 13), ALU.add, PK, PK)
            P.op("dve", lambda e: e.reciprocal(out=K5(j, 9), in_=K5(j, 9)), reads=PK, writes=PK)
            ts("dve", K5(j, 10), K5(j, 5), -1.0, None, ALU.add, None, PK, PK)
            tt("dve", K5(j, 12), K5(j, 10), K5(j, 0), ALU.mult, PK, PK)
            tt("dve", K5(j, 13), K5(j, 6), K5(j, 1), ALU.mult, PK, PK)
            tt("dve", K5(j, 12), K5(j, 12), K5(j, 13), ALU.add, PK, PK)
            tt("dve", K5(j, 7), K5(j, 12), K5(j, 9), ALU.mult, PK, PK)
            tt("dve", K5(j, 12), K5(j, 6), K5(j, 0), ALU.mult, PK, PK)
            tt("dve", K5(j, 13), K5(j, 10), K5(j, 1), ALU.mult, PK, PK)
            tt("dve", K5(j, 12), K5(j, 12), K5(j, 13), ALU.subtract, PK, PK)
            tt("dve", K5(j, 8), K5(j, 12), K5(j, 9), ALU.mult, PK, PK)
            ts("dve", K5(j, 10), K5(j, 8), -1.0, None, ALU.mult, None, PK, PK)
            hflat = h[:, :, :].rearrange("p k t -> p (k t)")
            za = big16[:, 0:T]
            zb = big16[:, T:2 * T]
            zc = hflat[:, 0:T]
            zd = hflat[:, T:2 * T]
            ZK = ["zq"]
            for q in range(NQ):
                b_ = q % 2
                stg = tmp[:, 4256 + b_ * 2 * T:4256 + (b_ + 1) * 2 * T].rearrange("p (a t) -> p a t", a=2)
                SK = [("tabstg", b_)]
                ts("dve", za, tau[:, :], s5k[:, j, 4, q:q + 1], None, ALU.mult, None, PK + ZK + ["tau"], ZK)
                range_reduce(zd, za, zb.bitcast(I32), zc, ZK)
                actf(stg[:, 1, :], zd, AF.Sin, ZK, SK)
                ts("dve", za, za, math.pi / 2, None, ALU.add, None, ZK, ZK)
                range_reduce(zd, za, zb.bitcast(I32), zc, ZK)
                actf(stg[:, 0, :], zd, AF.Sin, ZK, SK)
                dma("sp", s5tab_d[j][q].rearrange("p (a t) -> p a t", a=2), stg, f"tbw{b_}", SK, [("s5tab", j)])
            if s5_stage < 1:
                return
            ta = tmp[:, 4224:4240]
            tb = tmp[:, 4240:4256]
            for q in range(NQ):
                k, r4 = divmod(q, 4)
                br = s5nat[:, 0, q, :]
                bi = s5nat[:, 1, q, :]
                ts("dve", ta, br, s5k[:, j, 7, q:q + 1], None, ALU.mult, None, PK, PK)
                stt(ta, bi, s5k[:, j, 10, q:q + 1], ta, ALU.mult, ALU.add, PK, PK)
                ts("dve", tb, bi, s5k[:, j, 7, q:q + 1], None, ALU.mult, None, PK, PK)
                stt(tb, br, s5k[:, j, 8, q:q + 1], tb, ALU.mult, ALU.add, PK, PK)
                FK = ["s5full"]
                memset("dve", s5full[:, :, :], 0.0, FK)
                for a_, src in ((0, ta), (1, tb)):
                    for gl in range(2):
                        ts("dve", s5full[:, a_, 32 * r4 + 16 * gl:32 * r4 + 16 * gl + 16], src, masks[:, 16 + gl:17 + gl], None,
                           ALU.mult, None, PK + FK + ["masks"], FK)
                pi = psn()
                tr(pi, slice(0, 128), s5full[:, 0, :], FK, last=False)
                tr(pi, slice(128, 256), s5full[:, 1, :], FK, last=True)
                cp("act", s5bcs[:, 0:2, :], ps[pi][:, 0:256].rearrange("p (a n) -> p a n", a=2), [("ps", pi)], ["s5bcs"])
                for a_ in range(2):
                    for gl in range(2):
                        mcol = (8 if a_ else 0) + 2 * r4 + gl
                        ts("dve", s5full[:, a_, 64 * gl:64 * gl + 64], s5cn[:, a_, k, :], masks[:, mcol:mcol + 1], None,
                           ALU.mult, None, PK + FK + ["masks"], FK)
                pi = psn()
                tr(pi, slice(0, 128), s5full[:, 0, :], FK, last=False)
                tr(pi, slice(128, 256), s5full[:, 1, :], FK, last=True)
                cp("act", s5bcs[:, 2:4, :], ps[pi][:, 0:256].rearrange("p (a n) -> p a n", a=2), [("ps", pi)], ["s5bcs"])
                dma("sp", s5bc_d[j][:, :, q * 128:(q + 1) * 128].rearrange("a p n -> p a n"), s5bcs[:, :, :], "scr",
                    ["s5bcs"], [("s5bc", j)])

        def s5_slabs(j):
            slab_list.append(slab_cols("s5_w_in", j, KC, [(0, 1024)]))
            for half in range(2):
                def fn(dst2d, j=j, half=half):
                    dst = dst2d[:, 0:8192].rearrange("p (k n) -> p k n", k=64)
                    src = s5bc_d[j][2 * half:2 * half + 2, :, :].rearrange("a p (q n) -> p a q n", q=NQ)
                    return [(dst[:, 0:32, :], src[:, 0, :, :]), (dst[:, 32:64, :], src[:, 1, :, :])], "pool"
                slab_list.append(SlabDef(fn, reads=[("s5bc", j)], direct=True))
            for i2 in range(2):
                slab_list.append(slab_cols("s5_w_glu", j, KC, [(512 * i2, 512), (1024 + 512 * i2, 512)]))
            slab_list.append(slab_cols("s5_w_out", j, KC, [(0, 1024)]))

        def phase_s5(j):
            slot_i, g_i = next_slab()
            wi_ = ring_view(slot_i, KC, 1024)
            for c in range(KC):
                pi = psn()
                for k in range(KC):
                    mm(pi, wi_[:, k, 128 * c:128 * c + 128], hn[:, k, :], k == 0, k == KC - 1, [("ring", slot_i), ("hn", k)])
                cp("act", u32[:, c, :], ps[pi][:, :], [("ps", pi)], [("big16", c)])
                cp("dve", sq[:, c, :], u32[:, c, :], [("big16", c)], [("sq", c)])
            release(g_i)
            slot_b, g_b = next_slab()
            wB = ring_view(slot_b, 64, 128)
            slot_c, g_c = next_slab()
            wC = ring_view(slot_c, 64, 128)
            if s5_stage in (2, 20, 21):
                pi = psn()
                mm(pi, wB[:, 0, :], sq[:, 0, :], True, True, [("ring", slot_b), ("sq", 0)])
                pi = psn()
                mm(pi, wC[:, 0, :], sq[:, 0, :], True, True, [("ring", slot_c), ("sq", 0)])
                release(g_b)
                release(g_c)
                for _ in range(3):
                    sl_, g_ = next_slab()
                    pi = psn()
                    mm(pi, ring[:, sl_, 0:128], sq[:, 0, :], True, True, [("ring", sl_), ("sq", 0)])
                    release(g_)
                return
            ps_mod[0] = 6
            ps_i[0] = 0
            qst = {}
            free = list(range(NTMP))

            def talloc():
                assert free, "S5 tmp slots exhausted"
                return free.pop(0)

            def tfree(*ids):
                free.extend(ids)

            def load_tab(q):
                slot = q % 3
                dma("sp", tabr[:, slot, :, :], s5tab_d[j][q].rearrange("p (a t) -> p a t", a=2), f"tb{slot}",
                    [("s5tab", j)], [("tabr", slot)])

            def s5_a(q):
                c = q // 4
                sl = q % 3
                Cq = tabr[:, sl, 0, :]
                Sq = tabr[:, sl, 1, :]
                TK = [("tabr", sl)]
                pbr = psn()
                mm(pbr, wB[:, q, :], sq[:, c, :], True, True, [("ring", slot_b), ("sq", c)])
                pbi = psn()
                mm(pbi, wB[:, 32 + q, :], sq[:, c, :], True, True, [("ring", slot_b), ("sq", c)])
                br_, bi_, t_, mr, u_, v_ = [talloc() for _ in range(6)]
                cp("act", TM(br_), ps[pbr][:, :], [("ps", pbr)], [("tmp", br_)])
                cp("act", TM(bi_), ps[pbi][:, :], [("ps", pbi)], [("tmp", bi_)])
                tt("dve", TM(t_), TM(br_), Cq, ALU.mult, [("tmp", br_)] + TK, [("tmp", t_)])
                tt("dve", TM(mr), TM(bi_), Sq, ALU.mult, [("tmp", bi_)] + TK, [("tmp", mr)])
                tt("dve", TM(mr), TM(mr), TM(t_), ALU.add, [("tmp", mr), ("tmp", t_)], [("tmp", mr)])
                tt(S5_ENG2, TM(u_), TM(bi_), Cq, ALU.mult, [("tmp", bi_)] + TK, [("tmp", u_)])
                tt(S5_ENG2, TM(v_), TM(br_), Sq, ALU.mult, [("tmp", br_)] + TK, [("tmp", v_)])
                tt(S5_ENG2, TM(u_), TM(u_), TM(v_), ALU.subtract, [("tmp", u_), ("tmp", v_)], [("tmp", u_)])
                qst[q] = (br_, bi_, t_, mr, u_, v_)

            def s5_b(q):
                c, qq = divmod(q, 4)
                py = 6 + (c % 2)
                sl = q % 3
                Cq = tabr[:, sl, 0, :]
                Sq = tabr[:, sl, 1, :]
                Cend = tabr[:, sl, 0, T - 1:T]
                Send = tabr[:, sl, 1, T - 1:T]
                TK = [("tabr", sl)]
                br_, bi_, t_, mr, mi, v_ = qst.pop(q)
                rho = s5k[:, j, 3, q:q + 1].to_broadcast([128, T])
                gr, gi = talloc(), talloc()
                cr = s5carry[:, j, 0, q:q + 1]
                ci = s5carry[:, j, 1, q:q + 1]
                CK = [("s5carry", q)]
                P.op("dve", lambda e, gr=gr, mr=mr, cr=cr, rho=rho: e.tensor_tensor_scan(
                    out=TM(gr), data0=rho, data1=TM(mr), initial=cr, op0=ALU.mult, op1=ALU.add),
                    reads=[("tmp", mr), "s5p"] + CK, writes=[("tmp", gr)])
                P.op("dve", lambda e, gi=gi, mi=mi, ci=ci, rho=rho: e.tensor_tensor_scan(
                    out=TM(gi), data0=rho, data1=TM(mi), initial=ci, op0=ALU.mult, op1=ALU.add),
                    reads=[("tmp", mi), "s5p"] + CK, writes=[("tmp", gi)])
                gre = TM(gr, T - 1, 1)
                gie = TM(gi, T - 1, 1)
                tA = s5k[:, j, 14, q:q + 1]
                tB = s5k[:, j, 15, q:q + 1]
                ts("dve", tA, gie, Send, None, ALU.mult, None, [("tmp", gi)] + TK, [("s5t", q)])
                ts("dve", tB, gre, Send, None, ALU.mult, None, [("tmp", gr)] + TK, [("s5t", q)])
                stt(cr, gre, Cend, tA, ALU.mult, ALU.subtract, [("tmp", gr), ("s5t", q)] + TK, CK)
                stt(ci, gie, Cend, tB, ALU.mult, ALU.add, [("tmp", gi), ("s5t", q)] + TK, CK)
                hb = q % 2
                tt("dve", hrb[:, 2 * hb, :], TM(gr), Cq, ALU.mult, [("tmp", gr)] + TK, [("hrb", 2 * hb)])
                stt(hrb[:, 2 * hb + 1, :], TM(gi), -1.0, Sq, ALU.mult, ALU.mult, [("tmp", gi)] + TK, [("hrb", 2 * hb + 1)])
                tt(S5_ENG2, hib[:, 2 * hb, :], TM(gr), Sq, ALU.mult, [("tmp", gr)] + TK, [("hib", 2 * hb)])
                tt(S5_ENG2, hib[:, 2 * hb + 1, :], TM(gi), Cq, ALU.mult, [("tmp", gi)] + TK, [("hib", 2 * hb + 1)])
                mm(py, wC[:, q, :], hrb[:, 2 * hb, :], qq == 0, False, [("ring", slot_c), ("hrb", 2 * hb)])
                mm(py, wC[:, q, :], hrb[:, 2 * hb + 1, :], False, False, [("ring", slot_c), ("hrb", 2 * hb + 1)])
                mm(py, wC[:, 32 + q, :], hib[:, 2 * hb, :], False, False, [("ring", slot_c), ("hib", 2 * hb)])
                mm(py, wC[:, 32 + q, :], hib[:, 2 * hb + 1, :], False, qq == 3, [("ring", slot_c), ("hib", 2 * hb + 1)])
                tfree(br_, bi_, t_, mr, mi, v_, gr, gi)
                if qq == 3:
                    yy = talloc()
                    tfree(yy)
                    cp("act", TM(yy), ps[py][:, :], [("ps", py)], [("tmp", yy)])
                    stt(TM(yy), u32[:, c, :], C1("s5_d", j * KC + c), TM(yy), ALU.mult, ALU.add,
                        [("big16", c), ("tmp", yy), "cst"], [("tmp", yy)])
                    actf(act[:, 8 + c, :], TM(yy), AF.Gelu_apprx_tanh, [("tmp", yy)], [("act", 8 + c)])

            load_tab(0)
            load_tab(1)
            if S5_PIPE:
                s5_a(0)
                for q in range(NQ):
                    if q + 2 < NQ:
                        load_tab(q + 2)
                    ra = P.record(lambda: s5_a(q + 1)) if q + 1 < NQ else []
                    rb = P.record(lambda: s5_b(q))
                    P.emit([o for o in ra if o[0] != "dve"])
                    ad = [o for o in ra if o[0] == "dve"]
                    ai = 0
                    for o in rb:
                        P.emit([o])
                        if o[0] == "dve" and ai < len(ad):
                            P.emit([ad[ai]])
                            ai += 1
                    P.emit(ad[ai:])
            else:
                for q in range(NQ):
                    if q + 2 < NQ:
                        load_tab(q + 2)
                    s5_a(q)
                    s5_b(q)
            ps_mod[0] = 8
            ps_i[0] = 0
            tmp_i[0] = 0
            release(g_b)
            release(g_c)
            for i2 in range(2):
                slot_g, g_g = next_slab()
                wg = ring_view(slot_g, KC, 1024)
                for m4 in range(4):
                    m = 4 * i2 + m4
                    pv = psn()
                    for k in range(KC):
                        mm(pv, wg[:, k, 128 * m4:128 * m4 + 128], act[:, 8 + k, :], k == 0, k == KC - 1,
                           [("ring", slot_g), ("act", 8 + k)])
                    pg = psn()
                    for k in range(KC):
                        mm(pg, wg[:, k, 512 + 128 * m4:512 + 128 * m4 + 128], act[:, 8 + k, :], k == 0, k == KC - 1,
                           [("ring", slot_g), ("act", 8 + k)])
                    sg = tslot()
                    actf(TM(sg), ps[pg][:, :], AF.Sigmoid, [("ps", pg)], [("tmp", sg)])
                    tt("dve", act[:, 16 + m, :], ps[pv][:, :], TM(sg), ALU.mult, [("ps", pv), ("tmp", sg)], [("act", 16 + m)])
                release(g_g)
            slot_o, g_o = next_slab()
            wo = ring_view(slot_o, KC, 1024)
            for m in range(KC):
                pi = psn()
                for k in range(KC):
                    mm(pi, wo[:, k, 128 * m:128 * m + 128], act[:, 16 + k, :], k == 0, k == KC - 1,
                       [("ring", slot_o), ("act", 16 + k)])
                tt("dve", h[:, m, :], h[:, m, :], ps[pi][:, :], ALU.add, [("h", m), ("ps", pi)], [("h", m)])
            release(g_o)

        def phase_store(t):
            fs = [tslot() for _ in range(KC)]

            def o(k):
                return TM(fs[k]), ("tmp", fs[k])
            phase_norm("g_fin", 0, out_f32=o)
            for j in range(4):
                for hf in range(2):
                    pi = psn()
                    for kk in range(4):
                        k = 4 * hf + kk
                        tr(pi, slice(128 * kk, 128 * kk + 128), TM(fs[k], 128 * j, 128), [("tmp", fs[k])], last=(kk == 3))
                    cp("act" if hf else "dve", xin[:, j, 512 * hf:512 * hf + 512], ps[pi][:, :], [("ps", pi)],
                       [("big16", 2 * j + hf)])
            dma("sp", out_d[t * T:(t + 1) * T, :].rearrange("(j p) d -> p j d", p=128), xin, "ost",
                [("big16", i) for i in range(8)], [("outd", t)])

        EPS_AP = sb("eps_ap", [128, 1])
        ONE_AP = sb("one_ap", [128, 1])
        memset("dve", EPS_AP[:, :], EPS, ["eps"])
        memset("dve", ONE_AP[:, :], 1.0, ["one"])

        memset("dve", onesf[:, :], 1.0, ["onesf"])
        for gi_ in range(8):
            P.op("dve", lambda e, gi_=gi_: e.reduce_sum(out=masks[:, gi_:gi_ + 1], in_=ident[:, 16 * gi_:16 * gi_ + 16],
                                                        axis=mybir.AxisListType.X), reads=["ident"], writes=["masks"])
        for hf_ in range(2):
            P.op("dve", lambda e, hf_=hf_: e.reduce_sum(out=masks[:, 16 + hf_:17 + hf_], in_=ident[:, 64 * hf_:64 * hf_ + 64],
                                                        axis=mybir.AxisListType.X), reads=["ident"], writes=["masks"])
        ts("dve", masks[:, 8:16], masks[:, 0:8], -1.0, None, ALU.mult, None, ["masks"], ["masks"])
        if "s" in mix:
            for j_ in sorted({i // 2 for i in layers if i % 2 == 1}):
                if s5_stage != 21:
                    s5_prologue(j_)

        for i in layers:
            if i % 2 == 0 and "r" in mix:
                rg_slabs(i // 2)
            if i % 2 == 1 and "s" in mix and s5_stage >= 2:
                s5_slabs(i // 2)
            if "f" in mix:
                ffn_slabs(i)

        if USE_WSC:
            cast_all_slabs()
        P.barrier()
        for t in range(n_tiles):
            cur[0] = t
            phase_load(t)
            for i in layers:
                if i % 2 == 0 and "r" in mix:
                    phase_norm("g_mix", i * KC)
                    phase_rg(i // 2)
                if i % 2 == 1 and "s" in mix and s5_stage >= 2:
                    phase_norm("g_mix", i * KC)
                    phase_s5(i // 2)
                if "f" in mix:
                    phase_norm("g_ffn", i * KC)
                    phase_ffn(i)
            phase_store(t)
        P.final_wait("sp", [("outd", t) for t in range(n_tiles)] + [("dbg", i) for i in range(dbg_n[0])])
        P.dbg_names = dbg_names

        with nc.Block() as block:
            @block.sync
            def _(e):
                P.replay("sp", e, sems)

            @block.gpsimd
            def _(e):
                P.replay("pool", e, sems)

            @block.scalar
            def _(e):
                P.replay("act", e, sems)

            @block.vector
            def _(e):
                P.replay("dve", e, sems)

            @block.tensor
            def _(e):
                P.replay("pe", e, sems)
    return nc, P


def make_consts():
    return {"ident": np.eye(128, dtype=np.float32),
            "tau": np.tile(np.arange(1, T + 1, dtype=np.float32)[None, :], (128, 1))}


def shape_inputs(inputs):
    r = {}
    for k, v in inputs.items():
        if k == "x":
            continue
        v = np.ascontiguousarray(v, dtype=np.float32)
        if k == "norm_final_g":
            v = v.reshape(1, D)
        elif k in ("rg_b_a", "rg_b_x"):
            v = v.reshape(2, D)
        elif k in ("s5_a_re", "s5_a_im"):
            v = v.reshape(2, 4096)
        elif k in ("s5_b_re", "s5_b_im"):
            v = v.reshape(2, 4096, 16)
        elif k in ("s5_c_re", "s5_c_im"):
            v = v.reshape(2, 1024, 64)
        r[k] = v
    return r


def kernel(**inputs):
    x = np.ascontiguousarray(inputs["x"], dtype=np.float32)
    w = shape_inputs(inputs)
    w.update(make_consts())
    nc, _ = build_program()
    in_maps = [dict(w, x=x[b]) for b in range(BATCH)]
    res = run_bass_kernel_spmd(nc, in_maps, core_ids=list(range(BATCH)))
    return np.stack([res.results[b]["out"] for b in range(BATCH)], axis=0)
```

```python
import math
import numpy as np
import concourse.bass as bass
import concourse.mybir as mybir
from concourse.bass_utils import run_bass_kernel_spmd

F32 = mybir.dt.float32
BF16 = mybir.dt.bfloat16
AF = mybir.ActivationFunctionType
ALU = mybir.AluOpType

D = 1024
KC = 8
T = 512
SEQ = 8192
BATCH = 4
DEPTH = 4
DFF = 3072
NJ = 24
EPS = 1e-6
LSUB = 64
NQ = 32
ENGS = ("pe", "act", "dve", "pool", "sp")
SLAB = 8192
import os
S5_PIPE = os.environ.get('K_S5PIPE', '1') == '1'
S5_ENG2 = os.environ.get('K_S5ENG2', 'dve')
FFN_LOOK = int(os.environ.get('K_FFNLOOK', '2'))
RG_PIPE = os.environ.get('K_RGPIPE', '1') == '1'
USE_WSC = True
NSLOT = 3


class Prog:
    def __init__(self):
        self.prog = {e: [] for e in ENGS}
        self.count = {e: 0 for e in ENGS}
        self.dcount = {}
        self.seen = {e: {} for e in ENGS}
        self.last_write = {}
        self.readers = {}
        self.n_ops = 0
        self._rec = None

    def record(self, f):
        assert self._rec is None
        self._rec = []
        try:
            f()
            return self._rec
        finally:
            self._rec = None

    def emit(self, ops):
        for o in ops:
            self.op(*o)

    def _need(self, eng, waits, tok):
        if tok is None:
            return
        sk, v = tok
        if sk == eng and (eng in ("pe", "sp") or v > self.count[eng]):
            return
        if self.seen[eng].get(sk, 0) >= v:
            return
        if waits.get(sk, 0) < v:
            waits[sk] = v

    def op(self, eng, fn, reads=(), writes=(), track=True, dma=None, ninc=1):
        if self._rec is not None:
            self._rec.append((eng, fn, tuple(reads), tuple(writes), track, dma, ninc))
            return
        waits = {}
        for r in reads:
            self._need(eng, waits, self.last_write.get(r))
        for w in writes:
            self._need(eng, waits, self.last_write.get(w))
            for sk, v in self.readers.get(w, {}).items():
                self._need(eng, waits, (sk, v))
        for sk, v in waits.items():
            self.prog[eng].append(("wait", sk, v))
            self.seen[eng][sk] = v
        if dma is not None:
            prev = self.dcount.get(dma, 0)
            if prev > self.seen[eng].get(dma, 0) and (dma.startswith("cld") or dma.startswith("wcs")):
                self.prog[eng].append(("wait", dma, prev))
                self.seen[eng][dma] = prev
            self.dcount[dma] = self.dcount.get(dma, 0) + 16 * ninc
            tok = (dma, self.dcount[dma])
            self.prog[eng].append(("op", fn, dma, 16))
        elif track:
            self.count[eng] += 1
            tok = (eng, self.count[eng])
            self.prog[eng].append(("op", fn, eng, 1))
        else:
            tok = (eng, self.count[eng] + 1)
            self.prog[eng].append(("op", fn, None, 0))
        for w in writes:
            self.last_write[w] = tok
            self.readers[w] = {}
        for r in reads:
            d = self.readers.setdefault(r, {})
            if d.get(tok[0], 0) < tok[1]:
                d[tok[0]] = tok[1]
        self.n_ops += 1

    def barrier(self):
        snap = dict(self.count)
        dsnap = dict(self.dcount)
        for e in ENGS:
            for o, v in list(snap.items()) + list(dsnap.items()):
                if o.startswith("wcs"):
                    continue
                if o != e and v > 0 and self.seen[e].get(o, 0) < v:
                    self.prog[e].append(("wait", o, v))
                    self.seen[e][o] = v

    def final_wait(self, eng, keys):
        waits = {}
        for k in keys:
            self._need(eng, waits, self.last_write.get(k))
        for sk, v in waits.items():
            self.prog[eng].append(("wait", sk, v))
            self.seen[eng][sk] = v

    def replay(self, eng, handle, sems):
        for item in self.prog[eng]:
            if item[0] == "wait":
                handle.wait_ge(sems[item[1]], item[2])
            else:
                _, fn, sk, inc = item
                r = fn(handle)
                if sk is not None:
                    if isinstance(r, (list, tuple)):
                        for ins in r:
                            ins.then_inc(sems[sk], inc)
                    else:
                        r.then_inc(sems[sk], inc)


def build_program(n_tiles=SEQ // T, layers=(0, 1, 2, 3), seq=SEQ, mix="rsf", debug=0, s5_stage=4):
    nc = bass.Bass("TRN2", target_bir_lowering=False)
    P = Prog()

    def dram(name, shape, kind="ExternalInput", dt=F32):
        return nc.dram_tensor(name, list(shape), dt, kind=kind).ap()

    x_d = dram("x", [seq, D])
    out_d = dram("out", [seq, D], kind="ExternalOutput")
    ident_d = dram("ident", [128, 128])
    tau_d = dram("tau", [128, T])
    s5tab_d = [dram(f"s5tab{j}", [NQ, 128, 2 * T], kind="Internal") for j in range(2)]
    W = {}
    for name, shape in [
        ("norm_mix_g", [4, D]), ("norm_ffn_g", [4, D]), ("norm_final_g", [1, D]),
        ("rg_w_in", [2, D, 2 * D]), ("rg_conv_w", [2, 4, D]), ("rg_conv_b", [2, D]),
        ("rg_w_a", [2, 8, 128, 128]), ("rg_b_a", [2, D]), ("rg_w_x", [2, 8, 128, 128]),
        ("rg_b_x", [2, D]), ("rg_lambda", [2, D]), ("rg_w_out", [2, D, D]),
        ("s5_w_in", [2, D, D]), ("s5_a_re", [2, 4096]), ("s5_a_im", [2, 4096]),
        ("s5_log_dt", [2, 64]), ("s5_b_re", [2, 4096, 16]), ("s5_b_im", [2, 4096, 16]),
        ("s5_c_re", [2, 1024, 64]), ("s5_c_im", [2, 1024, 64]), ("s5_d", [2, D]),
        ("s5_w_glu", [2, D, 2 * D]), ("s5_w_out", [2, D, D]),
        ("ffn_w_up", [4, D, 2 * DFF]), ("ffn_conv_w", [4, 3, 2 * DFF]),
        ("ffn_conv_b", [4, 2 * DFF]), ("ffn_w_down", [4, DFF, D]),
    ]:
        W[name] = dram(name, shape)
    s5bc_d = [dram(f"s5bc{j}", [4, 128, NQ * 128], kind="Internal", dt=BF16) for j in range(2)]

    dbg_d = dram("dbg", [max(debug, 1), 128, T], kind="ExternalOutput") if debug else None
    dbg_n = [0]
    cur = [0]
    dbg_names = []

    import contextlib
    es = contextlib.ExitStack()
    with es:
        def sb(name, shape, dt=F32):
            return es.enter_context(nc.sbuf_tensor(name, list(shape), dt))

        big16 = sb("big16", [128, 4096])
        xin = big16[:, :].rearrange("p (j d) -> p j d", j=4)
        u32 = big16[:, :].rearrange("p (k t) -> p k t", k=KC)
        h = sb("h", [128, KC, T])
        sq = sb("sq", [128, KC, T], BF16)
        hn = sb("hn", [128, KC, T], BF16)
        rstd = sb("rstd", [128, T])
        ring = sb("ring", [128, NSLOT, SLAB], BF16)
        act = sb("act", [128, NJ, T], BF16)
        NTMP = 16
        TW = T + 4
        tmp = sb("tmp", [128, NTMP * TW])
        xcb = sb("xcb", [128, 2, T], BF16)
        hrb = sb("hrb", [128, 4, T], BF16)
        hib = sb("hib", [128, 4, T], BF16)
        ident = sb("ident_sb", [128, 128])
        identb = sb("identb", [128, 128], BF16)
        onesb = sb("onesb", [128, 128], BF16)
        tau = sb("tau_sb", [128, T])
        tabr = sb("tabr", [128, 3, 2, T])
        NCONST = 1152
        cst = sb("cst", [128, NCONST])
        stage = tmp[:, 0:1152].rearrange("p (b n) -> p b n", b=9)
        rgc = sb("rgc", [128, 2, 2, KC])
        halo_rg = sb("halo_rg", [128, 2, KC, 3])
        carry_rg = sb("carry_rg", [128, 2, KC])
        halo_ffn = sb("halo_ffn", [128, 4, 2 * NJ, 2], BF16)
        NUPB = 6
        upb = sb("upb", [128, NUPB, T + 2], BF16)
        dgs = sb("dgs", [128, NUPB, 3, 128], BF16)
        upb_i = [0]
        s5k = sb("s5k", [128, 2, 16, NQ])
        s5carry = sb("s5carry", [128, 2, 2, NQ])
        s5nat = tmp[:, 1152:1152 + 2048].rearrange("p (a q c) -> p a q c", a=4, q=NQ)
        s5cn = tmp[:, 3200:3200 + 1024].rearrange("p (a k n) -> p a k n", a=2, k=KC)
        s5full = sb("s5full", [128, 2, 128])
        s5bcs = sb("s5bcs", [128, 4, 128], BF16)
        masks = sb("masks", [128, 20])
        onesf = sb("onesf", [128, 128])
        ldrow = sb("ldrow", [1, 128])

        ps = [es.enter_context(nc.psum_tensor(f"ps{i}", [128, T], F32)) for i in range(8)]
        ps_i = [0]
        ps_mod = [8]

        def psn():
            i = ps_i[0] % ps_mod[0]
            ps_i[0] = (i + 1) % ps_mod[0]
            return i

        NCLD = 8
        sem_names = list(ENGS) + [f"w{s}" for s in range(NSLOT)] + [f"cld{i}" for i in range(NCLD)] + ["xld", "ost", "scr", "tb0", "tb1", "tb2", "tbw0", "tbw1"]
        cld_i = [0]
        NWCS = 8
        wcs_i = [0]
        sem_names += [f"wcs{i_}" for i_ in range(NWCS)]
        sems = {n: es.enter_context(nc.semaphore(n)) for n in sem_names}

        def mm(pi, lhsT, rhs, start, stop, reads, cols=slice(0, T)):
            P.op("pe", lambda e: e.matmul(ps[pi][:, cols], lhsT=lhsT, rhs=rhs, start=start, stop=stop),
                 reads=reads, writes=[("ps", pi)], track=stop)

        def tr(pi, cols, in_, reads, last=True):
            P.op("pe", lambda e: e.transpose(ps[pi][:, cols], in_, ident[:, :]),
                 reads=list(reads) + ["ident"], writes=[("ps", pi)], track=last)

        def actf(out, in_, func, reads, writes, bias=None, scale=None):
            kw = {}
            if bias is not None:
                kw["bias"] = bias
            if scale is not None:
                kw["scale"] = scale
            P.op("act", lambda e: e.activation(out=out, in_=in_, func=func, **kw), reads=reads, writes=writes)

        def tt(eng, out, in0, in1, op, reads, writes):
            P.op(eng, lambda e: e.tensor_tensor(out=out, in0=in0, in1=in1, op=op), reads=reads, writes=writes)

        def ts(eng, out, in0, s1, s2, op0, op1, reads, writes):
            if op1 is None:
                P.op(eng, lambda e: e.tensor_scalar(out=out, in0=in0, scalar1=s1, scalar2=None, op0=op0),
                     reads=reads, writes=writes)
            else:
                P.op(eng, lambda e: e.tensor_scalar(out=out, in0=in0, scalar1=s1, scalar2=s2, op0=op0, op1=op1),
                     reads=reads, writes=writes)

        def stt(out, in0, scalar, in1, op0, op1, reads, writes):
            P.op("dve", lambda e: e.scalar_tensor_tensor(out=out, in0=in0, scalar=scalar, in1=in1, op0=op0, op1=op1),
                 reads=reads, writes=writes)

        def cp(eng, out, in_, reads, writes):
            if eng == "act":
                P.op("act", lambda e: e.copy(out=out, in_=in_), reads=reads, writes=writes)
            else:
                P.op(eng, lambda e: e.tensor_copy(out=out, in_=in_), reads=reads, writes=writes)

        def memset(eng, ap, val, writes):
            P.op(eng, lambda e: e.memset(ap, val), writes=writes)

        def dma(eng, out, in_, sem, reads, writes, slow=False):
            if sem == "cld":
                sem = f"cld{cld_i[0]}"
                cld_i[0] = (cld_i[0] + 1) % NCLD
            if sem == "wcs":
                sem = f"wcs{wcs_i[0]}"
                wcs_i[0] = (wcs_i[0] + 1) % NWCS
            kw = {"allow_slow_non_contiguous": True} if slow else {}
            P.op(eng, lambda e: e.dma_start(out=out, in_=in_, **kw), reads=reads, writes=writes, dma=sem)

        def dbg(name, ap, key):
            if not debug or dbg_n[0] >= debug or cur[0] != n_tiles - 1:
                return
            i = dbg_n[0]
            dbg_n[0] += 1
            dbg_names.append(name)
            dma("sp", dbg_d[i], ap, "scr", [key], [("dbg", i)])

        tmp_i = [0]

        def tslot():
            i = tmp_i[0]
            tmp_i[0] = (i + 1) % NTMP
            return i

        def TM(i, c0=0, n=T):
            return tmp[:, i * TW + c0:i * TW + c0 + n]

        slab_list = []
        slab_state = {"issued": 0, "used": 0}
        PF = NSLOT - 1

        wsc_box = [None]

        def issue_slab(gidx):
            idx = gidx % len(slab_list)
            slot = gidx % NSLOT
            sd = slab_list[idx]
            if sd.direct or not USE_WSC:
                pairs, q = sd(ring[:, slot, :])
                if USE_WSC:
                    q = "sp"

                def fn(e, pairs=pairs):
                    return [e.dma_start(out=o, in_=i) for (o, i) in pairs]
                P.op(q, fn, reads=[r for r in sd.reads], writes=[("ring", slot)],
                     dma=f"w{slot}", ninc=len(pairs))
            else:
                n = sd.size
                src = wsc_box[0][idx, :, 0:n]
                P.op("sp", lambda e, slot=slot, n=n, src=src: e.dma_start(out=ring[:, slot, 0:n], in_=src),
                     reads=[("wsc", idx)], writes=[("ring", slot)], dma=f"w{slot}", ninc=1)

        def cast_all_slabs():
            wsc_box[0] = dram("wsc", [len(slab_list), 128, SLAB], kind="Internal", dt=BF16)
            for idx, sd in enumerate(slab_list):
                if sd.direct:
                    continue
                pairs, _ = sd(wsc_box[0][idx])
                for (o, i) in pairs:
                    dma("pool", o, i, "wcs", [], [("wsc", idx)])

        released = set()

        def pump():
            total = len(slab_list) * n_tiles
            while slab_state["issued"] < min(slab_state["used"] + PF + 1, total):
                n = slab_state["issued"]
                if n >= NSLOT and (n - NSLOT) not in released:
                    break
                issue_slab(n)
                slab_state["issued"] += 1

        def next_slab():
            g = slab_state["used"]
            slab_state["used"] += 1
            pump()
            assert slab_state["issued"] > g, "slab ring deadlock: too many live slabs"
            return g % NSLOT, g

        def release(g):
            released.add(g)
            pump()

        class SlabDef:
            def __init__(self, fn, reads=(), direct=False, size=SLAB):
                self.fn = fn
                self.reads = reads
                self.direct = direct
                self.size = size

            def __call__(self, dst2d):
                return self.fn(dst2d)

        def slab_cols(wname, l, kc, col_sets, q="pool"):
            ntot = sum(n for _, n in col_sets)

            def fn(dst2d):
                dst = dst2d[:, 0:kc * ntot].rearrange("p (k n) -> p k n", k=kc)
                src = W[wname][l].rearrange("(k p) n -> p k n", p=128)
                pairs = []
                o = 0
                for c0, n in col_sets:
                    pairs.append((dst[:, :, o:o + n], src[:, :, c0:c0 + n]))
                    o += n
                return pairs, q
            return SlabDef(fn, size=kc * ntot)

        def ring_view(slot, kc, n):
            return ring[:, slot, 0:kc * n].rearrange("p (k n) -> p k n", k=kc)

        crow = {}
        vec_list = []

        def addvec(name, ap2d):
            crow[name] = sum(v[1] for v in vec_list)
            vec_list.append((name, ap2d.shape[0], ap2d))

        addvec("g_mix", W["norm_mix_g"].rearrange("l (k p) -> (l k) p", p=128))
        addvec("g_ffn", W["norm_ffn_g"].rearrange("l (k p) -> (l k) p", p=128))
        addvec("g_fin", W["norm_final_g"].rearrange("l (k p) -> (l k) p", p=128))
        addvec("rg_cw", W["rg_conv_w"].rearrange("l t (k p) -> (l t k) p", p=128))
        addvec("rg_cb", W["rg_conv_b"].rearrange("l (k p) -> (l k) p", p=128))
        addvec("rg_ba", W["rg_b_a"].rearrange("l (k p) -> (l k) p", p=128))
        addvec("rg_bx", W["rg_b_x"].rearrange("l (k p) -> (l k) p", p=128))
        addvec("rg_lam", W["rg_lambda"].rearrange("l (k p) -> (l k) p", p=128))
        addvec("s5_d", W["s5_d"].rearrange("l (k p) -> (l k) p", p=128))
        addvec("ffn_cw", W["ffn_conv_w"].rearrange("l t (k p) -> (l t k) p", p=128))
        addvec("ffn_cb", W["ffn_conv_b"].rearrange("l (k p) -> (l k) p", p=128))
        addvec("s5_are", W["s5_a_re"].rearrange("l (q p) -> (l q) p", p=128))
        addvec("s5_aim", W["s5_a_im"].rearrange("l (q p) -> (l q) p", p=128))
        nrows = sum(v[1] for v in vec_list)
        assert nrows <= NCONST, nrows

        def C1(name, idx):
            c = crow[name] + idx
            return cst[:, c:c + 1]

        dma("sp", ident[:, :], ident_d[:, :], "cld", [], ["ident"])
        dma("sp", tau[:, :], tau_d[:, :], "cld", [], ["tau"])
        cp("dve", identb[:, :], ident[:, :], ["ident"], ["identb"])
        memset("dve", onesb[:, :], 1.0, ["onesb"])
        memset("dve", halo_rg[:, :, :, :], 0.0, [("halo_rg", l_, c_) for l_ in range(2) for c_ in range(KC)])
        memset("dve", carry_rg[:, :, :], 0.0, [("carry_rg", l_, c_) for l_ in range(2) for c_ in range(KC)])
        memset("dve", halo_ffn[:, :, :, :], 0.0, [("halo_ffn", i_, c_) for i_ in range(4) for c_ in range(2 * NJ)])
        memset("dve", s5carry[:, :, :, :], 0.0, ["s5carry"])
        nblk = (nrows + 127) // 128
        memset("dve", stage[:, :, :], 0.0, [("stage", b) for b in range(nblk)])
        r0 = 0
        for name, n, ap2d in vec_list:
            done = 0
            while done < n:
                blk, off = divmod(r0 + done, 128)
                take = min(n - done, 128 - off)
                dma("sp", stage[off:off + take, blk, :], ap2d[done:done + take, :], "cld", [], [("stage", blk)])
                done += take
            r0 += n
        for blk in range(nblk):
            pi = psn()
            tr(pi, slice(0, 128), stage[:, blk, :], [("stage", blk)])
            cp("dve", cst[:, blk * 128:(blk + 1) * 128], ps[pi][:, 0:128], [("ps", pi)], ["cst"])
        for l in range(2):
            lam = cst[:, crow["rg_lam"] + l * KC: crow["rg_lam"] + (l + 1) * KC]
            actf(rgc[:, l, 0, :], lam, AF.Exp, ["cst"], ["rgc"], scale=-1.0)
            actf(rgc[:, l, 0, :], rgc[:, l, 0, :], AF.Ln, ["rgc"], ["rgc"], bias=1.0)
            ts("dve", rgc[:, l, 1, :], rgc[:, l, 0, :], -16.0, None, ALU.mult, None, ["rgc"], ["rgc"])
            ts("dve", rgc[:, l, 0, :], rgc[:, l, 0, :], -8.0, None, ALU.mult, None, ["rgc"], ["rgc"])

        def phase_load(t):
            dma("sp", xin, x_d[t * T:(t + 1) * T, :].rearrange("(j p) d -> p j d", p=128), "xld",
                [], [("big16", i) for i in range(8)])
            for k in range(KC):
                pi = psn()
                for j in range(4):
                    tr(pi, slice(128 * j, 128 * j + 128), xin[:, j, 128 * k:128 * k + 128],
                       [("big16", 2 * j), ("big16", 2 * j + 1)], last=(j == 3))
                cp("act" if k % 2 else "dve", h[:, k, :], ps[pi][:, :], [("ps", pi)], [("h", k)])

        def phase_norm(gname, gidx0, out_f32=None):
            for k in range(KC):
                if k % 2 == 0:
                    actf(sq[:, k, :], h[:, k, :], AF.Square, [("h", k)], [("sq", k)])
                else:
                    tt("dve", sq[:, k, :], h[:, k, :], h[:, k, :], ALU.mult, [("h", k)], [("sq", k)])
            pi = psn()
            for k in range(KC):
                mm(pi, onesb[:, :], sq[:, k, :], k == 0, k == KC - 1, ["onesb", ("sq", k)])
            actf(rstd[:, :], ps[pi][:, :], AF.Sqrt, [("ps", pi), "eps"], ["rstd"], bias=EPS_AP[:, :], scale=1.0 / D)
            P.op("dve", lambda e: e.reciprocal(out=rstd[:, :], in_=rstd[:, :]), reads=["rstd"], writes=["rstd"])
            for k in range(KC):
                if out_f32 is None:
                    stt(hn[:, k, :], h[:, k, :], C1(gname, gidx0 + k), rstd[:, :], ALU.mult, ALU.mult,
                        [("h", k), "rstd", "cst"], [("hn", k)])
                else:
                    o, key = out_f32(k)
                    stt(o, h[:, k, :], C1(gname, gidx0 + k), rstd[:, :], ALU.mult, ALU.mult,
                        [("h", k), "rstd", "cst"], [key])

        def phase_ffn(i):
            cw0 = crow["ffn_cw"] + i * 3 * 48
            cb0 = crow["ffn_cb"] + i * 48
            slabs = {}
            st = {}

            def get_slab(si):
                if si not in slabs:
                    slabs[si] = next_slab()
                return slabs[si]

            def stage_a(hx):
                jg, half = divmod(hx, 2)
                si, cc = divmod(jg, 4)
                slot, gsl = get_slab(si)
                wv = ring_view(slot, KC, 1024)
                ch = jg + NJ * half
                pi = psn()
                for k in range(KC):
                    mm(pi, wv[:, k, 512 * half + 128 * cc: 512 * half + 128 * cc + 128], hn[:, k, :],
                       k == 0, k == KC - 1, [("ring", slot), ("hn", k)])
                if hx % 8 == 7:
                    release(gsl)
                ub_i = upb_i[0]
                upb_i[0] = (ub_i + 1) % NUPB
                cp("pool", upb[:, ub_i, 0:2], halo_ffn[:, i, ch, :], [("halo_ffn", i, ch)], [("upbh", ub_i)])
                cp("act", upb[:, ub_i, 2:2 + T], ps[pi][:, :], [("ps", pi)], [("upb", ub_i)])
                cp("pool", halo_ffn[:, i, ch, :], upb[:, ub_i, T:T + 2], [("upb", ub_i)], [("halo_ffn", i, ch)])
                for t3 in range(3):
                    actf(dgs[:, ub_i, t3, :], identb[:, :], AF.Copy, ["identb", "cst"], [("dgs", ub_i, t3)],
                         scale=cst[:, cw0 + 48 * t3 + ch:cw0 + 48 * t3 + ch + 1])
                st[hx] = ub_i

            def stage_b(hx):
                jg, half = divmod(hx, 2)
                ub_i = st.pop(hx)
                pc = psn()
                for t3 in range(3):
                    mm(pc, dgs[:, ub_i, t3, :], upb[:, ub_i, t3:t3 + T], t3 == 0, t3 == 2,
                       [("dgs", ub_i, t3), ("upb", ub_i), ("upbh", ub_i)])
                if half == 0:
                    g = tslot()
                    actf(TM(g), ps[pc][:, :], AF.Gelu_apprx_tanh, [("ps", pc), "cst"], [("tmp", g)],
                         bias=cst[:, cb0 + jg:cb0 + jg + 1])
                    st[("g", jg)] = g
                else:
                    g = st.pop(("g", jg))
                    stt(act[:, jg, :], ps[pc][:, :], cst[:, cb0 + NJ + jg:cb0 + NJ + jg + 1], TM(g), ALU.add, ALU.mult,
                        [("ps", pc), ("tmp", g), "cst"], [("act", jg)])

            NH = 2 * NJ
            LOOK = FFN_LOOK
            for hx in range(min(LOOK, NH)):
                stage_a(hx)
            for hx in range(NH):
                if hx + LOOK < NH:
                    stage_a(hx + LOOK)
                stage_b(hx)
            for s in range(4):
                slot, gsl = next_slab()
                wv = ring_view(slot, NJ, 256)
                for mh in range(2):
                    m = 2 * s + mh
                    pi = psn()
                    for j in range(NJ):
                        mm(pi, wv[:, j, 128 * mh:128 * mh + 128], act[:, j, :], j == 0, j == NJ - 1,
                           [("ring", slot), ("act", j)])
                    tt("dve", h[:, m, :], h[:, m, :], ps[pi][:, :], ALU.add, [("h", m), ("ps", pi)], [("h", m)])
                release(gsl)

        def ffn_slabs(i):
            for s in range(6):
                slab_list.append(slab_cols("ffn_w_up", i, KC, [(512 * s, 512), (DFF + 512 * s, 512)]))
            for s in range(4):
                slab_list.append(slab_cols("ffn_w_down", i, NJ, [(256 * s, 256)]))

        def phase_rg(l):
            slot_g, g_g = next_slab()
            wg = ring_view(slot_g, KC, 1024)
            for c in range(KC):
                pg = psn()
                for k in range(KC):
                    mm(pg, wg[:, k, 128 * c:128 * c + 128], hn[:, k, :], k == 0, k == KC - 1, [("ring", slot_g), ("hn", k)])
                actf(act[:, 8 + c, :], ps[pg][:, :], AF.Gelu_apprx_tanh, [("ps", pg)], [("act", 8 + c)])
            release(g_g)
            slot_x, g_x = next_slab()
            wx = ring_view(slot_x, KC, 1024)
            slot_a, g_a = next_slab()
            wa = ring_view(slot_a, 16, 128)
            RG_BATCH = 4

            def rg_a(c):
                pi = psn()
                for k in range(KC):
                    mm(pi, wx[:, k, 128 * c:128 * c + 128], hn[:, k, :], k == 0, k == KC - 1, [("ring", slot_x), ("hn", k)])
                u = 12 + (c % 4)
                cp("pool", TM(u, 0, 3), halo_rg[:, l, c, :], [("halo_rg", l, c)], [("tmp", u)])
                cp("act", TM(u, 3, T), ps[pi][:, :], [("ps", pi)], [("tmp", u)])
                cp("pool", halo_rg[:, l, c, :], TM(u, T, 3), [("tmp", u)], [("halo_rg", l, c)])
                return u

            def rg_conv(c, u):
                xc = u32[:, c, :]
                XK = [("big16", c)]
                ts("dve", xc, TM(u, 0, T), C1("rg_cw", l * 32 + 0 * KC + c), C1("rg_cb", l * KC + c),
                   ALU.mult, ALU.add, [("tmp", u), "cst"], XK)
                for k in range(1, 4):
                    stt(xc, TM(u, k, T), C1("rg_cw", l * 32 + k * KC + c), xc, ALU.mult, ALU.add,
                        [("tmp", u), "cst"] + XK, XK)
                cp("pool", sq[:, c, :], xc, XK, [("sq", c)])

            us = {0: rg_a(0)}
            for c in range(KC):
                if c + 1 < KC:
                    us[c + 1] = rg_a(c + 1)
                rg_conv(c, us.pop(c))
            release(g_x)

            for c0 in range(0, KC, RG_BATCH):
                cs = list(range(c0, c0 + RG_BATCH))
                sl = {c: (c % 4, 4 + c % 4, 8 + c % 4) for c in cs}
                for c in cs:
                    r, ig, a = sl[c]
                    pa = psn()
                    mm(pa, wa[:, c, :], sq[:, c, :], True, True, [("ring", slot_a), ("sq", c)])
                    px = psn()
                    mm(px, wa[:, 8 + c, :], sq[:, c, :], True, True, [("ring", slot_a), ("sq", c)])
                    actf(TM(r), ps[pa][:, :], AF.Sigmoid, [("ps", pa), "cst"], [("tmp", r)], bias=C1("rg_ba", l * KC + c))
                    actf(TM(ig), ps[px][:, :], AF.Sigmoid, [("ps", px), "cst"], [("tmp", ig)], bias=C1("rg_bx", l * KC + c))
                for c in cs:
                    r, ig, a = sl[c]
                    actf(TM(a), TM(r), AF.Exp, [("tmp", r), "rgc"], [("tmp", a)], scale=rgc[:, l, 0, c:c + 1])
                    actf(TM(r), TM(r), AF.Exp, [("tmp", r), "rgc"], [("tmp", r)], scale=rgc[:, l, 1, c:c + 1])
                for c in cs:
                    r, ig, a = sl[c]
                    actf(TM(r), TM(r), AF.Sqrt, [("tmp", r), "one"], [("tmp", r)], bias=ONE_AP[:, :], scale=-1.0)
                for c in cs:
                    r, ig, a = sl[c]
                    tt("dve", TM(ig), TM(ig), u32[:, c, :], ALU.mult, [("tmp", ig), ("big16", c)], [("tmp", ig)])
                    tt("dve", TM(ig), TM(ig), TM(r), ALU.mult, [("tmp", ig), ("tmp", r)], [("tmp", ig)])
                    hs = 12 + (c % 4)
                    P.op("dve", lambda e, hs=hs, a=a, ig=ig, c=c: e.tensor_tensor_scan(
                        out=TM(hs), data0=TM(a), data1=TM(ig), initial=carry_rg[:, l, c:c + 1],
                        op0=ALU.mult, op1=ALU.add),
                        reads=[("tmp", a), ("tmp", ig), ("carry_rg", l, c)], writes=[("tmp", hs)])
                    cp("dve", carry_rg[:, l, c:c + 1], TM(hs, T - 1, 1), [("tmp", hs)], [("carry_rg", l, c)])
                    tt("dve", act[:, c, :], TM(hs), act[:, 8 + c, :], ALU.mult, [("tmp", hs), ("act", 8 + c)], [("act", c)])
            tmp_i[0] = 0
            release(g_a)
            slot_o, g_o = next_slab()
            wo = ring_view(slot_o, KC, 1024)
            for m in range(KC):
                pi = psn()
                for c in range(KC):
                    mm(pi, wo[:, c, 128 * m:128 * m + 128], act[:, c, :], c == 0, c == KC - 1, [("ring", slot_o), ("act", c)])
                tt("dve", h[:, m, :], h[:, m, :], ps[pi][:, :], ALU.add, [("h", m), ("ps", pi)], [("h", m)])
            release(g_o)

        def rg_slabs(l):
            slab_list.append(slab_cols("rg_w_in", l, KC, [(1024, 1024)]))
            slab_list.append(slab_cols("rg_w_in", l, KC, [(0, 1024)]))

            def fn(dst2d, l=l):
                dst = dst2d[:, 0:2048].rearrange("p (k n) -> p k n", k=16)
                return [(dst[:, 0:8, :], W["rg_w_a"][l].rearrange("h i j -> i h j")),
                        (dst[:, 8:16, :], W["rg_w_x"][l].rearrange("h i j -> i h j"))], "pool"
            slab_list.append(SlabDef(fn, size=2048))
            slab_list.append(slab_cols("rg_w_out", l, KC, [(0, 1024)]))


        I32 = mybir.dt.int32
        TWO_PI = 2.0 * math.pi

        def K5(j, idx, q0=0, n=NQ):
            return s5k[:, j, idx, q0:q0 + n]

        def range_reduce(eng_out_ap, z_ap, zi_ap, kf_ap, width_keys):
            C1_ = 6.28125
            C2_ = TWO_PI - C1_
            PI_LO = 3.1415925
            ts("dve", zi_ap, z_ap, 1.0 / TWO_PI, None, ALU.mult, None, width_keys, width_keys)
            cp("dve", kf_ap, zi_ap, width_keys, width_keys)
            stt(eng_out_ap, kf_ap, -C1_, z_ap, ALU.mult, ALU.add, width_keys, width_keys)
            stt(eng_out_ap, kf_ap, -C2_, eng_out_ap, ALU.mult, ALU.add, width_keys, width_keys)
            ts("dve", eng_out_ap, eng_out_ap, -PI_LO, PI_LO, ALU.max, ALU.min, width_keys, width_keys)

        def s5_prologue(j):
            PK = ["s5p"]
            cp("dve", K5(j, 0), cst[:, crow["s5_are"] + NQ * j:crow["s5_are"] + NQ * (j + 1)], ["cst"], PK)
            cp("dve", K5(j, 1), cst[:, crow["s5_aim"] + NQ * j:crow["s5_aim"] + NQ * (j + 1)], ["cst"], PK)
            dma("sp", ldrow[0:1, 0:64], W["s5_log_dt"][j:j + 1, :], "cld", [], PK)
            for q0 in range(0, NQ, 4):
                dma("sp", s5nat[:, 0, q0:q0 + 4, :], W["s5_b_re"][j].rearrange("(q p) c -> p q c", p=128)[:, q0:q0 + 4, :], "cld", [], PK)
                dma("sp", s5nat[:, 1, q0:q0 + 4, :], W["s5_b_im"][j].rearrange("(q p) c -> p q c", p=128)[:, q0:q0 + 4, :], "cld", [], PK)
            dma("sp", s5cn[:, 0, :, :], W["s5_c_re"][j].rearrange("(k r) n -> r k n", r=128), "cld", [], PK)
            dma("sp", s5cn[:, 1, :, :], W["s5_c_im"][j].rearrange("(k r) n -> r k n", r=128), "cld", [], PK)
            pi = psn()
            P.op("pe", lambda e: e.matmul(ps[pi][:, 0:64], lhsT=onesf[0:1, :], rhs=ldrow[0:1, 0:64], start=True, stop=True),
                 reads=PK + ["onesf"], writes=[("ps", pi)])
            pv = ps[pi][:, 0:64].rearrange("p (q g) -> p q g", g=2)
            cp("dve", s5k[0:64, j, 2, :], pv[0:64, :, 0], [("ps", pi)], PK)
            cp("dve", s5k[64:128, j, 2, :], pv[64:128, :, 1], [("ps", pi)], PK)
            actf(K5(j, 2), K5(j, 2), AF.Exp, PK, PK)
            tt("dve", K5(j, 12), K5(j, 0), K5(j, 2), ALU.mult, PK, PK)
            actf(K5(j, 3), K5(j, 12), AF.Exp, PK, PK)
            tt("dve", K5(j, 4), K5(j, 1), K5(j, 2), ALU.mult, PK, PK)
            zi = s5k[:, j, 13, :].bitcast(I32)
            range_reduce(K5(j, 12), K5(j, 4), zi, K5(j, 14), PK)
            actf(K5(j, 6), K5(j, 12), AF.Sin, PK, PK)
            ts("dve", K5(j, 15), K5(j, 4), math.pi / 2, None, ALU.add, None, PK, PK)
            range_reduce(K5(j, 12), K5(j, 15), zi, K5(j, 14), PK)
            actf(K5(j, 5), K5(j, 12), AF.Sin, PK, PK)
            tt("dve", K5(j, 5), K5(j, 5), K5(j, 3), ALU.mult, PK, PK)
            tt("dve", K5(j, 6), K5(j, 6), K5(j, 3), ALU.mult, PK, PK)
            tt("dve", K5(j, 12), K5(j, 0), K5(j, 0), ALU.mult, PK, PK)
            tt("dve", K5(j, 13), K5(j, 1), K5(j, 1), ALU.mult, PK, PK)
            tt("dve", K5(j, 9), K5(j, 12), K5(j, 13), ALU.add, PK, PK)
            P.op("dve", lambda e: e.reciprocal(out=K5(j, 9), in_=K5(j, 9)), reads=PK, writes=PK)
            ts("dve", K5(j, 10), K5(j, 5), -1.0, None, ALU.add, None, PK, PK)
            tt("dve", K5(j, 12), K5(j, 10), K5(j, 0), ALU.mult, PK, PK)
            tt("dve", K5(j, 13), K5(j, 6), K5(j, 1), ALU.mult, PK, PK)
            tt("dve", K5(j, 12), K5(j, 12), K5(j, 13), ALU.add, PK, PK)
            tt("dve", K5(j, 7), K5(j, 12), K5(j, 9), ALU.mult, PK, PK)
            tt("dve", K5(j, 12), K5(j, 6), K5(j, 0), ALU.mult, PK, PK)
            tt("dve", K5(j, 13), K5(j, 10), K5(j, 1), ALU.mult, PK, PK)
            tt("dve", K5(j, 12), K5(j, 12), K5(j, 13), ALU.subtract, PK, PK)
            tt("dve", K5(j, 8), K5(j, 12), K5(j, 9), ALU.mult, PK, PK)
            ts("dve", K5(j, 10), K5(j, 8), -1.0, None, ALU.mult, None, PK, PK)
            hflat = h[:, :, :].rearrange("p k t -> p (k t)")
            za = big16[:, 0:T]
            zb = big16[:, T:2 * T]
            zc = hflat[:, 0:T]
            zd = hflat[:, T:2 * T]
            ZK = ["zq"]
            for q in range(NQ):
                b_ = q % 2
                stg = tmp[:, 4256 + b_ * 2 * T:4256 + (b_ + 1) * 2 * T].rearrange("p (a t) -> p a t", a=2)
                SK = [("tabstg", b_)]
                ts("dve", za, tau[:, :], s5k[:, j, 4, q:q + 1], None, ALU.mult, None, PK + ZK + ["tau"], ZK)
                range_reduce(zd, za, zb.bitcast(I32), zc, ZK)
                actf(stg[:, 1, :], zd, AF.Sin, ZK, SK)
                ts("dve", za, za, math.pi / 2, None, ALU.add, None, ZK, ZK)
                range_reduce(zd, za, zb.bitcast(I32), zc, ZK)
                actf(stg[:, 0, :], zd, AF.Sin, ZK, SK)
                dma("sp", s5tab_d[j][q].rearrange("p (a t) -> p a t", a=2), stg, f"tbw{b_}", SK, [("s5tab", j)])
            if s5_stage < 1:
                return
            ta = tmp[:, 4224:4240]
            tb = tmp[:, 4240:4256]
            for q in range(NQ):
                k, r4 = divmod(q, 4)
                br = s5nat[:, 0, q, :]
                bi = s5nat[:, 1, q, :]
                ts("dve", ta, br, s5k[:, j, 7, q:q + 1], None, ALU.mult, None, PK, PK)
                stt(ta, bi, s5k[:, j, 10, q:q + 1], ta, ALU.mult, ALU.add, PK, PK)
                ts("dve", tb, bi, s5k[:, j, 7, q:q + 1], None, ALU.mult, None, PK, PK)
                stt(tb, br, s5k[:, j, 8, q:q + 1], tb, ALU.mult, ALU.add, PK, PK)
                FK = ["s5full"]
                memset("dve", s5full[:, :, :], 0.0, FK)
                for a_, src in ((0, ta), (1, tb)):
                    for gl in range(2):
                        ts("dve", s5full[:, a_, 32 * r4 + 16 * gl:32 * r4 + 16 * gl + 16], src, masks[:, 16 + gl:17 + gl], None,
                           ALU.mult, None, PK + FK + ["masks"], FK)
                pi = psn()
                tr(pi, slice(0, 128), s5full[:, 0, :], FK, last=False)
                tr(pi, slice(128, 256), s5full[:, 1, :], FK, last=True)
                cp("act", s5bcs[:, 0:2, :], ps[pi][:, 0:256].rearrange("p (a n) -> p a n", a=2), [("ps", pi)], ["s5bcs"])
                for a_ in range(2):
                    for gl in range(2):
                        mcol = (8 if a_ else 0) + 2 * r4 + gl
                        ts("dve", s5full[:, a_, 64 * gl:64 * gl + 64], s5cn[:, a_, k, :], masks[:, mcol:mcol + 1], None,
                           ALU.mult, None, PK + FK + ["masks"], FK)
                pi = psn()
                tr(pi, slice(0, 128), s5full[:, 0, :], FK, last=False)
                tr(pi, slice(128, 256), s5full[:, 1, :], FK, last=True)
                cp("act", s5bcs[:, 2:4, :], ps[pi][:, 0:256].rearrange("p (a n) -> p a n", a=2), [("ps", pi)], ["s5bcs"])
                dma("sp", s5bc_d[j][:, :, q * 128:(q + 1) * 128].rearrange("a p n -> p a n"), s5bcs[:, :, :], "scr",
                    ["s5bcs"], [("s5bc", j)])

        def s5_slabs(j):
            slab_list.append(slab_cols("s5_w_in", j, KC, [(0, 1024)]))
            for half in range(2):
                def fn(dst2d, j=j, half=half):
                    dst = dst2d[:, 0:8192].rearrange("p (k n) -> p k n", k=64)
                    src = s5bc_d[j][2 * half:2 * half + 2, :, :].rearrange("a p (q n) -> p a q n", q=NQ)
                    return [(dst[:, 0:32, :], src[:, 0, :, :]), (dst[:, 32:64, :], src[:, 1, :, :])], "pool"
                slab_list.append(SlabDef(fn, reads=[("s5bc", j)], direct=True))
            for i2 in range(2):
                slab_list.append(slab_cols("s5_w_glu", j, KC, [(512 * i2, 512), (1024 + 512 * i2, 512)]))
            slab_list.append(slab_cols("s5_w_out", j, KC, [(0, 1024)]))

        def phase_s5(j):
            slot_i, g_i = next_slab()
            wi_ = ring_view(slot_i, KC, 1024)
            for c in range(KC):
                pi = psn()
                for k in range(KC):
                    mm(pi, wi_[:, k, 128 * c:128 * c + 128], hn[:, k, :], k == 0, k == KC - 1, [("ring", slot_i), ("hn", k)])
                cp("act", u32[:, c, :], ps[pi][:, :], [("ps", pi)], [("big16", c)])
                cp("dve", sq[:, c, :], u32[:, c, :], [("big16", c)], [("sq", c)])
            release(g_i)
            slot_b, g_b = next_slab()
            wB = ring_view(slot_b, 64, 128)
            slot_c, g_c = next_slab()
            wC = ring_view(slot_c, 64, 128)
            if s5_stage in (2, 20, 21):
                pi = psn()
                mm(pi, wB[:, 0, :], sq[:, 0, :], True, True, [("ring", slot_b), ("sq", 0)])
                pi = psn()
                mm(pi, wC[:, 0, :], sq[:, 0, :], True, True, [("ring", slot_c), ("sq", 0)])
                release(g_b)
                release(g_c)
                for _ in range(3):
                    sl_, g_ = next_slab()
                    pi = psn()
                    mm(pi, ring[:, sl_, 0:128], sq[:, 0, :], True, True, [("ring", sl_), ("sq", 0)])
                    release(g_)
                return
            ps_mod[0] = 6
            ps_i[0] = 0
            qst = {}
            free = list(range(NTMP))

            def talloc():
                assert free, "S5 tmp slots exhausted"
                return free.pop(0)

            def tfree(*ids):
                free.extend(ids)

            def load_tab(q):
                slot = q % 3
                dma("sp", tabr[:, slot, :, :], s5tab_d[j][q].rearrange("p (a t) -> p a t", a=2), f"tb{slot}",
                    [("s5tab", j)], [("tabr", slot)])

            def s5_a(q):
                c = q // 4
                sl = q % 3
                Cq = tabr[:, sl, 0, :]
                Sq = tabr[:, sl, 1, :]
                TK = [("tabr", sl)]
                pbr = psn()
                mm(pbr, wB[:, q, :], sq[:, c, :], True, True, [("ring", slot_b), ("sq", c)])
                pbi = psn()
                mm(pbi, wB[:, 32 + q, :], sq[:, c, :], True, True, [("ring", slot_b), ("sq", c)])
                br_, bi_, t_, mr, u_, v_ = [talloc() for _ in range(6)]
                cp("act", TM(br_), ps[pbr][:, :], [("ps", pbr)], [("tmp", br_)])
                cp("act", TM(bi_), ps[pbi][:, :], [("ps", pbi)], [("tmp", bi_)])
                tt("dve", TM(t_), TM(br_), Cq, ALU.mult, [("tmp", br_)] + TK, [("tmp", t_)])
                tt("dve", TM(mr), TM(bi_), Sq, ALU.mult, [("tmp", bi_)] + TK, [("tmp", mr)])
                tt("dve", TM(mr), TM(mr), TM(t_), ALU.add, [("tmp", mr), ("tmp", t_)], [("tmp", mr)])
                tt(S5_ENG2, TM(u_), TM(bi_), Cq, ALU.mult, [("tmp", bi_)] + TK, [("tmp", u_)])
                tt(S5_ENG2, TM(v_), TM(br_), Sq, ALU.mult, [("tmp", br_)] + TK, [("tmp", v_)])
                tt(S5_ENG2, TM(u_), TM(u_), TM(v_), ALU.subtract, [("tmp", u_), ("tmp", v_)], [("tmp", u_)])
                qst[q] = (br_, bi_, t_, mr, u_, v_)

            def s5_b(q):
                c, qq = divmod(q, 4)
                py = 6 + (c % 2)
                sl = q % 3
                Cq = tabr[:, sl, 0, :]
                Sq = tabr[:, sl, 1, :]
                Cend = tabr[:, sl, 0, T - 1:T]
                Send = tabr[:, sl, 1, T - 1:T]
                TK = [("tabr", sl)]
                br_, bi_, t_, mr, mi, v_ = qst.pop(q)
                rho = s5k[:, j, 3, q:q + 1].to_broadcast([128, T])
                gr, gi = talloc(), talloc()
                cr = s5carry[:, j, 0, q:q + 1]
                ci = s5carry[:, j, 1, q:q + 1]
                CK = [("s5carry", q)]
                P.op("dve", lambda e, gr=gr, mr=mr, cr=cr, rho=rho: e.tensor_tensor_scan(
                    out=TM(gr), data0=rho, data1=TM(mr), initial=cr, op0=ALU.mult, op1=ALU.add),
                    reads=[("tmp", mr), "s5p"] + CK, writes=[("tmp", gr)])
                P.op("dve", lambda e, gi=gi, mi=mi, ci=ci, rho=rho: e.tensor_tensor_scan(
                    out=TM(gi), data0=rho, data1=TM(mi), initial=ci, op0=ALU.mult, op1=ALU.add),
                    reads=[("tmp", mi), "s5p"] + CK, writes=[("tmp", gi)])
                gre = TM(gr, T - 1, 1)
                gie = TM(gi, T - 1, 1)
                tA = s5k[:, j, 14, q:q + 1]
                tB = s5k[:, j, 15, q:q + 1]
                ts("dve", tA, gie, Send, None, ALU.mult, None, [("tmp", gi)] + TK, [("s5t", q)])
                ts("dve", tB, gre, Send, None, ALU.mult, None, [("tmp", gr)] + TK, [("s5t", q)])
                stt(cr, gre, Cend, tA, ALU.mult, ALU.subtract, [("tmp", gr), ("s5t", q)] + TK, CK)
                stt(ci, gie, Cend, tB, ALU.mult, ALU.add, [("tmp", gi), ("s5t", q)] + TK, CK)
                hb = q % 2
                tt("dve", hrb[:, 2 * hb, :], TM(gr), Cq, ALU.mult, [("tmp", gr)] + TK, [("hrb", 2 * hb)])
                stt(hrb[:, 2 * hb + 1, :], TM(gi), -1.0, Sq, ALU.mult, ALU.mult, [("tmp", gi)] + TK, [("hrb", 2 * hb + 1)])
                tt(S5_ENG2, hib[:, 2 * hb, :], TM(gr), Sq, ALU.mult, [("tmp", gr)] + TK, [("hib", 2 * hb)])
                tt(S5_ENG2, hib[:, 2 * hb + 1, :], TM(gi), Cq, ALU.mult, [("tmp", gi)] + TK, [("hib", 2 * hb + 1)])
                mm(py, wC[:, q, :], hrb[:, 2 * hb, :], qq == 0, False, [("ring", slot_c), ("hrb", 2 * hb)])
                mm(py, wC[:, q, :], hrb[:, 2 * hb + 1, :], False, False, [("ring", slot_c), ("hrb", 2 * hb + 1)])
                mm(py, wC[:, 32 + q, :], hib[:, 2 * hb, :], False, False, [("ring", slot_c), ("hib", 2 * hb)])
                mm(py, wC[:, 32 + q, :], hib[:, 2 * hb + 1, :], False, qq == 3, [("ring", slot_c), ("hib", 2 * hb + 1)])
                tfree(br_, bi_, t_, mr, mi, v_, gr, gi)
                if qq == 3:
                    yy = talloc()
                    tfree(yy)
                    cp("act", TM(yy), ps[py][:, :], [("ps", py)], [("tmp", yy)])
                    stt(TM(yy), u32[:, c, :], C1("s5_d", j * KC + c), TM(yy), ALU.mult, ALU.add,
                        [("big16", c), ("tmp", yy), "cst"], [("tmp", yy)])
                    actf(act[:, 8 + c, :], TM(yy), AF.Gelu_apprx_tanh, [("tmp", yy)], [("act", 8 + c)])

            load_tab(0)
            load_tab(1)
            if S5_PIPE:
                s5_a(0)
                for q in range(NQ):
                    if q + 2 < NQ:
                        load_tab(q + 2)
                    ra = P.record(lambda: s5_a(q + 1)) if q + 1 < NQ else []
                    rb = P.record(lambda: s5_b(q))
                    P.emit([o for o in ra if o[0] != "dve"])
                    ad = [o for o in ra if o[0] == "dve"]
                    ai = 0
                    for o in rb:
                        P.emit([o])
                        if o[0] == "dve" and ai < len(ad):
                            P.emit([ad[ai]])
                            ai += 1
                    P.emit(ad[ai:])
            else:
                for q in range(NQ):
                    if q + 2 < NQ:
                        load_tab(q + 2)
                    s5_a(q)
                    s5_b(q)
            ps_mod[0] = 8
            ps_i[0] = 0
            tmp_i[0] = 0
            release(g_b)
            release(g_c)
            for i2 in range(2):
                slot_g, g_g = next_slab()
                wg = ring_view(slot_g, KC, 1024)
                for m4 in range(4):
                    m = 4 * i2 + m4
                    pv = psn()
                    for k in range(KC):
                        mm(pv, wg[:, k, 128 * m4:128 * m4 + 128], act[:, 8 + k, :], k == 0, k == KC - 1,
                           [("ring", slot_g), ("act", 8 + k)])
                    pg = psn()
                    for k in range(KC):
                        mm(pg, wg[:, k, 512 + 128 * m4:512 + 128 * m4 + 128], act[:, 8 + k, :], k == 0, k == KC - 1,
                           [("ring", slot_g), ("act", 8 + k)])
                    sg = tslot()
                    actf(TM(sg), ps[pg][:, :], AF.Sigmoid, [("ps", pg)], [("tmp", sg)])
                    tt("dve", act[:, 16 + m, :], ps[pv][:, :], TM(sg), ALU.mult, [("ps", pv), ("tmp", sg)], [("act", 16 + m)])
                release(g_g)
            slot_o, g_o = next_slab()
            wo = ring_view(slot_o, KC, 1024)
            for m in range(KC):
                pi = psn()
                for k in range(KC):
                    mm(pi, wo[:, k, 128 * m:128 * m + 128], act[:, 16 + k, :], k == 0, k == KC - 1,
                       [("ring", slot_o), ("act", 16 + k)])
                tt("dve", h[:, m, :], h[:, m, :], ps[pi][:, :], ALU.add, [("h", m), ("ps", pi)], [("h", m)])
            release(g_o)

        def phase_store(t):
            fs = [tslot() for _ in range(KC)]

            def o(k):
                return TM(fs[k]), ("tmp", fs[k])
            phase_norm("g_fin", 0, out_f32=o)
            for j in range(4):
                for hf in range(2):
                    pi = psn()
                    for kk in range(4):
                        k = 4 * hf + kk
                        tr(pi, slice(128 * kk, 128 * kk + 128), TM(fs[k], 128 * j, 128), [("tmp", fs[k])], last=(kk == 3))
                    cp("act" if hf else "dve", xin[:, j, 512 * hf:512 * hf + 512], ps[pi][:, :], [("ps", pi)],
                       [("big16", 2 * j + hf)])
            dma("sp", out_d[t * T:(t + 1) * T, :].rearrange("(j p) d -> p j d", p=128), xin, "ost",
                [("big16", i) for i in range(8)], [("outd", t)])

        EPS_AP = sb("eps_ap", [128, 1])
        ONE_AP = sb("one_ap", [128, 1])
        memset("dve", EPS_AP[:, :], EPS, ["eps"])
        memset("dve", ONE_AP[:, :], 1.0, ["one"])

        memset("dve", onesf[:, :], 1.0, ["onesf"])
        for gi_ in range(8):
            P.op("dve", lambda e, gi_=gi_: e.reduce_sum(out=masks[:, gi_:gi_ + 1], in_=ident[:, 16 * gi_:16 * gi_ + 16],
                                                        axis=mybir.AxisListType.X), reads=["ident"], writes=["masks"])
        for hf_ in range(2):
            P.op("dve", lambda e, hf_=hf_: e.reduce_sum(out=masks[:, 16 + hf_:17 + hf_], in_=ident[:, 64 * hf_:64 * hf_ + 64],
                                                        axis=mybir.AxisListType.X), reads=["ident"], writes=["masks"])
        ts("dve", masks[:, 8:16], masks[:, 0:8], -1.0, None, ALU.mult, None, ["masks"], ["masks"])
        if "s" in mix:
            for j_ in sorted({i // 2 for i in layers if i % 2 == 1}):
                if s5_stage != 21:
                    s5_prologue(j_)

        for i in layers:
            if i % 2 == 0 and "r" in mix:
                rg_slabs(i // 2)
            if i % 2 == 1 and "s" in mix and s5_stage >= 2:
                s5_slabs(i // 2)
            if "f" in mix:
                ffn_slabs(i)

        if USE_WSC:
            cast_all_slabs()
        P.barrier()
        for t in range(n_tiles):
            cur[0] = t
            phase_load(t)
            for i in layers:
                if i % 2 == 0 and "r" in mix:
                    phase_norm("g_mix", i * KC)
                    phase_rg(i // 2)
                if i % 2 == 1 and "s" in mix and s5_stage >= 2:
                    phase_norm("g_mix", i * KC)
                    phase_s5(i // 2)
                if "f" in mix:
                    phase_norm("g_ffn", i * KC)
                    phase_ffn(i)
            phase_store(t)
        P.final_wait("sp", [("outd", t) for t in range(n_tiles)] + [("dbg", i) for i in range(dbg_n[0])])
        P.dbg_names = dbg_names

        with nc.Block() as block:
            @block.sync
            def _(e):
                P.replay("sp", e, sems)

            @block.gpsimd
            def _(e):
                P.replay("pool", e, sems)

            @block.scalar
            def _(e):
                P.replay("act", e, sems)

            @block.vector
            def _(e):
                P.replay("dve", e, sems)

            @block.tensor
            def _(e):
                P.replay("pe", e, sems)
    return nc, P


def make_consts():
    return {"ident": np.eye(128, dtype=np.float32),
            "tau": np.tile(np.arange(1, T + 1, dtype=np.float32)[None, :], (128, 1))}


def shape_inputs(inputs):
    r = {}
    for k, v in inputs.items():
        if k == "x":
            continue
        v = np.ascontiguousarray(v, dtype=np.float32)
        if k == "norm_final_g":
            v = v.reshape(1, D)
        elif k in ("rg_b_a", "rg_b_x"):
            v = v.reshape(2, D)
        elif k in ("s5_a_re", "s5_a_im"):
            v = v.reshape(2, 4096)
        elif k in ("s5_b_re", "s5_b_im"):
            v = v.reshape(2, 4096, 16)
        elif k in ("s5_c_re", "s5_c_im"):
            v = v.reshape(2, 1024, 64)
        r[k] = v
    return r


def kernel(**inputs):
    x = np.ascontiguousarray(inputs["x"], dtype=np.float32)
    w = shape_inputs(inputs)
    w.update(make_consts())
    nc, _ = build_program()
    in_maps = [dict(w, x=x[b]) for b in range(BATCH)]
    res = run_bass_kernel_spmd(nc, in_maps, core_ids=list(range(BATCH)))
    return np.stack([res.results[b]["out"] for b in range(BATCH)], axis=0)
```

```python
import math
import numpy as np
import concourse.bass as bass
import concourse.mybir as mybir
from concourse.bass_utils import run_bass_kernel_spmd

F32 = mybir.dt.float32
BF16 = mybir.dt.bfloat16
AF = mybir.ActivationFunctionType
ALU = mybir.AluOpType

D = 1024
KC = 8
T = 512
SEQ = 8192
BATCH = 4
DEPTH = 4
DFF = 3072
NJ = 24
EPS = 1e-6
LSUB = 64
NQ = 32
ENGS = ("pe", "act", "dve", "pool", "sp")
SLAB = 8192
import os
S5_PIPE = os.environ.get('K_S5PIPE', '1') == '1'
S5_ENG2 = os.environ.get('K_S5ENG2', 'dve')
FFN_LOOK = int(os.environ.get('K_FFNLOOK', '2'))
RG_PIPE = os.environ.get('K_RGPIPE', '1') == '1'
USE_WSC = True
NSLOT = 3


class Prog:
    def __init__(self):
        self.prog = {e: [] for e in ENGS}
        self.count = {e: 0 for e in ENGS}
        self.dcount = {}
        self.seen = {e: {} for e in ENGS}
        self.last_write = {}
        self.readers = {}
        self.n_ops = 0
        self._rec = None

    def record(self, f):
        assert self._rec is None
        self._rec = []
        try:
            f()
            return self._rec
        finally:
            self._rec = None

    def emit(self, ops):
        for o in ops:
            self.op(*o)

    def _need(self, eng, waits, tok):
        if tok is None:
            return
        sk, v = tok
        if sk == eng and (eng in ("pe", "sp") or v > self.count[eng]):
            return
        if self.seen[eng].get(sk, 0) >= v:
            return
        if waits.get(sk, 0) < v:
            waits[sk] = v

    def op(self, eng, fn, reads=(), writes=(), track=True, dma=None, ninc=1):
        if self._rec is not None:
            self._rec.append((eng, fn, tuple(reads), tuple(writes), track, dma, ninc))
            return
        waits = {}
        for r in reads:
            self._need(eng, waits, self.last_write.get(r))
        for w in writes:
            self._need(eng, waits, self.last_write.get(w))
            for sk, v in self.readers.get(w, {}).items():
                self._need(eng, waits, (sk, v))
        for sk, v in waits.items():
            self.prog[eng].append(("wait", sk, v))
            self.seen[eng][sk] = v
        if dma is not None:
            prev = self.dcount.get(dma, 0)
            if prev > self.seen[eng].get(dma, 0) and (dma.startswith("cld") or dma.startswith("wcs")):
                self.prog[eng].append(("wait", dma, prev))
                self.seen[eng][dma] = prev
            self.dcount[dma] = self.dcount.get(dma, 0) + 16 * ninc
            tok = (dma, self.dcount[dma])
            self.prog[eng].append(("op", fn, dma, 16))
        elif track:
            self.count[eng] += 1
            tok = (eng, self.count[eng])
            self.prog[eng].append(("op", fn, eng, 1))
        else:
            tok = (eng, self.count[eng] + 1)
            self.prog[eng].append(("op", fn, None, 0))
        for w in writes:
            self.last_write[w] = tok
            self.readers[w] = {}
        for r in reads:
            d = self.readers.setdefault(r, {})
            if d.get(tok[0], 0) < tok[1]:
                d[tok[0]] = tok[1]
        self.n_ops += 1

    def barrier(self):
        snap = dict(self.count)
        dsnap = dict(self.dcount)
        for e in ENGS:
            for o, v in list(snap.items()) + list(dsnap.items()):
                if o.startswith("wcs"):
                    continue
                if o != e and v > 0 and self.seen[e].get(o, 0) < v:
                    self.prog[e].append(("wait", o, v))
                    self.seen[e][o] = v

    def final_wait(self, eng, keys):
        waits = {}
        for k in keys:
            self._need(eng, waits, self.last_write.get(k))
        for sk, v in waits.items():
            self.prog[eng].append(("wait", sk, v))
            self.seen[eng][sk] = v

    def replay(self, eng, handle, sems):
        for item in self.prog[eng]:
            if item[0] == "wait":
                handle.wait_ge(sems[item[1]], item[2])
            else:
                _, fn, sk, inc = item
                r = fn(handle)
                if sk is not None:
                    if isinstance(r, (list, tuple)):
                        for ins in r:
                            ins.then_inc(sems[sk], inc)
                    else:
                        r.then_inc(sems[sk], inc)


def build_program(n_tiles=SEQ // T, layers=(0, 1, 2, 3), seq=SEQ, mix="rsf", debug=0, s5_stage=4):
    nc = bass.Bass("TRN2", target_bir_lowering=False)
    P = Prog()

    def dram(name, shape, kind="ExternalInput", dt=F32):
        return nc.dram_tensor(name, list(shape), dt, kind=kind).ap()

    x_d = dram("x", [seq, D])
    out_d = dram("out", [seq, D], kind="ExternalOutput")
    ident_d = dram("ident", [128, 128])
    tau_d = dram("tau", [128, T])
    s5tab_d = [dram(f"s5tab{j}", [NQ, 128, 2 * T], kind="Internal") for j in range(2)]
    W = {}
    for name, shape in [
        ("norm_mix_g", [4, D]), ("norm_ffn_g", [4, D]), ("norm_final_g", [1, D]),
        ("rg_w_in", [2, D, 2 * D]), ("rg_conv_w", [2, 4, D]), ("rg_conv_b", [2, D]),
        ("rg_w_a", [2, 8, 128, 128]), ("rg_b_a", [2, D]), ("rg_w_x", [2, 8, 128, 128]),
        ("rg_b_x", [2, D]), ("rg_lambda", [2, D]), ("rg_w_out", [2, D, D]),
        ("s5_w_in", [2, D, D]), ("s5_a_re", [2, 4096]), ("s5_a_im", [2, 4096]),
        ("s5_log_dt", [2, 64]), ("s5_b_re", [2, 4096, 16]), ("s5_b_im", [2, 4096, 16]),
        ("s5_c_re", [2, 1024, 64]), ("s5_c_im", [2, 1024, 64]), ("s5_d", [2, D]),
        ("s5_w_glu", [2, D, 2 * D]), ("s5_w_out", [2, D, D]),
        ("ffn_w_up", [4, D, 2 * DFF]), ("ffn_conv_w", [4, 3, 2 * DFF]),
        ("ffn_conv_b", [4, 2 * DFF]), ("ffn_w_down", [4, DFF, D]),
    ]:
        W[name] = dram(name, shape)
    s5bc_d = [dram(f"s5bc{j}", [4, 128, NQ * 128], kind="Internal", dt=BF16) for j in range(2)]

    dbg_d = dram("dbg", [max(debug, 1), 128, T], kind="ExternalOutput") if debug else None
    dbg_n = [0]
    cur = [0]
    dbg_names = []

    import contextlib
    es = contextlib.ExitStack()
    with es:
        def sb(name, shape, dt=F32):
            return es.enter_context(nc.sbuf_tensor(name, list(shape), dt))

        big16 = sb("big16", [128, 4096])
        xin = big16[:, :].rearrange("p (j d) -> p j d", j=4)
        u32 = big16[:, :].rearrange("p (k t) -> p k t", k=KC)
        h = sb("h", [128, KC, T])
        sq = sb("sq", [128, KC, T], BF16)
        hn = sb("hn", [128, KC, T], BF16)
        rstd = sb("rstd", [128, T])
        ring = sb("ring", [128, NSLOT, SLAB], BF16)
        act = sb("act", [128, NJ, T], BF16)
        NTMP = 16
        TW = T + 4
        tmp = sb("tmp", [128, NTMP * TW])
        xcb = sb("xcb", [128, 2, T], BF16)
        hrb = sb("hrb", [128, 4, T], BF16)
        hib = sb("hib", [128, 4, T], BF16)
        ident = sb("ident_sb", [128, 128])
        identb = sb("identb", [128, 128], BF16)
        onesb = sb("onesb", [128, 128], BF16)
        tau = sb("tau_sb", [128, T])
        tabr = sb("tabr", [128, 3, 2, T])
        NCONST = 1152
        cst = sb("cst", [128, NCONST])
        stage = tmp[:, 0:1152].rearrange("p (b n) -> p b n", b=9)
        rgc = sb("rgc", [128, 2, 2, KC])
        halo_rg = sb("halo_rg", [128, 2, KC, 3])
        carry_rg = sb("carry_rg", [128, 2, KC])
        halo_ffn = sb("halo_ffn", [128, 4, 2 * NJ, 2], BF16)
        NUPB = 6
        upb = sb("upb", [128, NUPB, T + 2], BF16)
        dgs = sb("dgs", [128, NUPB, 3, 128], BF16)
        upb_i = [0]
        s5k = sb("s5k", [128, 2, 16, NQ])
        s5carry = sb("s5carry", [128, 2, 2, NQ])
        s5nat = tmp[:, 1152:1152 + 2048].rearrange("p (a q c) -> p a q c", a=4, q=NQ)
        s5cn = tmp[:, 3200:3200 + 1024].rearrange("p (a k n) -> p a k n", a=2, k=KC)
        s5full = sb("s5full", [128, 2, 128])
        s5bcs = sb("s5bcs", [128, 4, 128], BF16)
        masks = sb("masks", [128, 20])
        onesf = sb("onesf", [128, 128])
        ldrow = sb("ldrow", [1, 128])

        ps = [es.enter_context(nc.psum_tensor(f"ps{i}", [128, T], F32)) for i in range(8)]
        ps_i = [0]
        ps_mod = [8]

        def psn():
            i = ps_i[0] % ps_mod[0]
            ps_i[0] = (i + 1) % ps_mod[0]
            return i

        NCLD = 8
        sem_names = list(ENGS) + [f"w{s}" for s in range(NSLOT)] + [f"cld{i}" for i in range(NCLD)] + ["xld", "ost", "scr", "tb0", "tb1", "tb2", "tbw0", "tbw1"]
        cld_i = [0]
        NWCS = 8
        wcs_i = [0]
        sem_names += [f"wcs{i_}" for i_ in range(NWCS)]
        sems = {n: es.enter_context(nc.semaphore(n)) for n in sem_names}

        def mm(pi, lhsT, rhs, start, stop, reads, cols=slice(0, T)):
            P.op("pe", lambda e: e.matmul(ps[pi][:, cols], lhsT=lhsT, rhs=rhs, start=start, stop=stop),
                 reads=reads, writes=[("ps", pi)], track=stop)

        def tr(pi, cols, in_, reads, last=True):
            P.op("pe", lambda e: e.transpose(ps[pi][:, cols], in_, ident[:, :]),
                 reads=list(reads) + ["ident"], writes=[("ps", pi)], track=last)

        def actf(out, in_, func, reads, writes, bias=None, scale=None):
            kw = {}
            if bias is not None:
                kw["bias"] = bias
            if scale is not None:
                kw["scale"] = scale
            P.op("act", lambda e: e.activation(out=out, in_=in_, func=func, **kw), reads=reads, writes=writes)

        def tt(eng, out, in0, in1, op, reads, writes):
            P.op(eng, lambda e: e.tensor_tensor(out=out, in0=in0, in1=in1, op=op), reads=reads, writes=writes)

        def ts(eng, out, in0, s1, s2, op0, op1, reads, writes):
            if op1 is None:
                P.op(eng, lambda e: e.tensor_scalar(out=out, in0=in0, scalar1=s1, scalar2=None, op0=op0),
                     reads=reads, writes=writes)
            else:
                P.op(eng, lambda e: e.tensor_scalar(out=out, in0=in0, scalar1=s1, scalar2=s2, op0=op0, op1=op1),
                     reads=reads, writes=writes)

        def stt(out, in0, scalar, in1, op0, op1, reads, writes):
            P.op("dve", lambda e: e.scalar_tensor_tensor(out=out, in0=in0, scalar=scalar, in1=in1, op0=op0, op1=op1),
                 reads=reads, writes=writes)

        def cp(eng, out, in_, reads, writes):
            if eng == "act":
                P.op("act", lambda e: e.copy(out=out, in_=in_), reads=reads, writes=writes)
            else:
                P.op(eng, lambda e: e.tensor_copy(out=out, in_=in_), reads=reads, writes=writes)

        def memset(eng, ap, val, writes):
            P.op(eng, lambda e: e.memset(ap, val), writes=writes)

        def dma(eng, out, in_, sem, reads, writes, slow=False):
            if sem == "cld":
                sem = f"cld{cld_i[0]}"
                cld_i[0] = (cld_i[0] + 1) % NCLD
            if sem == "wcs":
                sem = f"wcs{wcs_i[0]}"
                wcs_i[0] = (wcs_i[0] + 1) % NWCS
            kw = {"allow_slow_non_contiguous": True} if slow else {}
            P.op(eng, lambda e: e.dma_start(out=out, in_=in_, **kw), reads=reads, writes=writes, dma=sem)

        def dbg(name, ap, key):
            if not debug or dbg_n[0] >= debug or cur[0] != n_tiles - 1:
                return
            i = dbg_n[0]
            dbg_n[0] += 1
            dbg_names.append(name)
            dma("sp", dbg_d[i], ap, "scr", [key], [("dbg", i)])

        tmp_i = [0]

        def tslot():
            i = tmp_i[0]
            tmp_i[0] = (i + 1) % NTMP
            return i

        def TM(i, c0=0, n=T):
            return tmp[:, i * TW + c0:i * TW + c0 + n]

        slab_list = []
        slab_state = {"issued": 0, "used": 0}
        PF = NSLOT - 1

        wsc_box = [None]

        def issue_slab(gidx):
            idx = gidx % len(slab_list)
            slot = gidx % NSLOT
            sd = slab_list[idx]
            if sd.direct or not USE_WSC:
                pairs, q = sd(ring[:, slot, :])
                if USE_WSC:
                    q = "sp"

                def fn(e, pairs=pairs):
                    return [e.dma_start(out=o, in_=i) for (o, i) in pairs]
                P.op(q, fn, reads=[r for r in sd.reads], writes=[("ring", slot)],
                     dma=f"w{slot}", ninc=len(pairs))
            else:
                n = sd.size
                src = wsc_box[0][idx, :, 0:n]
                P.op("sp", lambda e, slot=slot, n=n, src=src: e.dma_start(out=ring[:, slot, 0:n], in_=src),
                     reads=[("wsc", idx)], writes=[("ring", slot)], dma=f"w{slot}", ninc=1)

        def cast_all_slabs():
            wsc_box[0] = dram("wsc", [len(slab_list), 128, SLAB], kind="Internal", dt=BF16)
            for idx, sd in enumerate(slab_list):
                if sd.direct:
                    continue
                pairs, _ = sd(wsc_box[0][idx])
                for (o, i) in pairs:
                    dma("pool", o, i, "wcs", [], [("wsc", idx)])

        released = set()

        def pump():
            total = len(slab_list) * n_tiles
            while slab_state["issued"] < min(slab_state["used"] + PF + 1, total):
                n = slab_state["issued"]
                if n >= NSLOT and (n - NSLOT) not in released:
                    break
                issue_slab(n)
                slab_state["issued"] += 1

        def next_slab():
            g = slab_state["used"]
            slab_state["used"] += 1
            pump()
            assert slab_state["issued"] > g, "slab ring deadlock: too many live slabs"
            return g % NSLOT, g

        def release(g):
            released.add(g)
            pump()

        class SlabDef:
            def __init__(self, fn, reads=(), direct=False, size=SLAB):
                self.fn = fn
                self.reads = reads
                self.direct = direct
                self.size = size

            def __call__(self, dst2d):
                return self.fn(dst2d)

        def slab_cols(wname, l, kc, col_sets, q="pool"):
            ntot = sum(n for _, n in col_sets)

            def fn(dst2d):
                dst = dst2d[:, 0:kc * ntot].rearrange("p (k n) -> p k n", k=kc)
                src = W[wname][l].rearrange("(k p) n -> p k n", p=128)
                pairs = []
                o = 0
                for c0, n in col_sets:
                    pairs.append((dst[:, :, o:o + n], src[:, :, c0:c0 + n]))
                    o += n
                return pairs, q
            return SlabDef(fn, size=kc * ntot)

        def ring_view(slot, kc, n):
            return ring[:, slot, 0:kc * n].rearrange("p (k n) -> p k n", k=kc)

        crow = {}
        vec_list = []

        def addvec(name, ap2d):
            crow[name] = sum(v[1] for v in vec_list)
            vec_list.append((name, ap2d.shape[0], ap2d))

        addvec("g_mix", W["norm_mix_g"].rearrange("l (k p) -> (l k) p", p=128))
        addvec("g_ffn", W["norm_ffn_g"].rearrange("l (k p) -> (l k) p", p=128))
        addvec("g_fin", W["norm_final_g"].rearrange("l (k p) -> (l k) p", p=128))
        addvec("rg_cw", W["rg_conv_w"].rearrange("l t (k p) -> (l t k) p", p=128))
        addvec("rg_cb", W["rg_conv_b"].rearrange("l (k p) -> (l k) p", p=128))
        addvec("rg_ba", W["rg_b_a"].rearrange("l (k p) -> (l k) p", p=128))
        addvec("rg_bx", W["rg_b_x"].rearrange("l (k p) -> (l k) p", p=128))
        addvec("rg_lam", W["rg_lambda"].rearrange("l (k p) -> (l k) p", p=128))
        addvec("s5_d", W["s5_d"].rearrange("l (k p) -> (l k) p", p=128))
        addvec("ffn_cw", W["ffn_conv_w"].rearrange("l t (k p) -> (l t k) p", p=128))
        addvec("ffn_cb", W["ffn_conv_b"].rearrange("l (k p) -> (l k) p", p=128))
        addvec("s5_are", W["s5_a_re"].rearrange("l (q p) -> (l q) p", p=128))
        addvec("s5_aim", W["s5_a_im"].rearrange("l (q p) -> (l q) p", p=128))
        nrows = sum(v[1] for v in vec_list)
        assert nrows <= NCONST, nrows

        def C1(name, idx):
            c = crow[name] + idx
            return cst[:, c:c + 1]

        dma("sp", ident[:, :], ident_d[:, :], "cld", [], ["ident"])
        dma("sp", tau[:, :], tau_d[:, :], "cld", [], ["tau"])
        cp("dve", identb[:, :], ident[:, :], ["ident"], ["identb"])
        memset("dve", onesb[:, :], 1.0, ["onesb"])
        memset("dve", halo_rg[:, :, :, :], 0.0, [("halo_rg", l_, c_) for l_ in range(2) for c_ in range(KC)])
        memset("dve", carry_rg[:, :, :], 0.0, [("carry_rg", l_, c_) for l_ in range(2) for c_ in range(KC)])
        memset("dve", halo_ffn[:, :, :, :], 0.0, [("halo_ffn", i_, c_) for i_ in range(4) for c_ in range(2 * NJ)])
        memset("dve", s5carry[:, :, :, :], 0.0, ["s5carry"])
        nblk = (nrows + 127) // 128
        memset("dve", stage[:, :, :], 0.0, [("stage", b) for b in range(nblk)])
        r0 = 0
        for name, n, ap2d in vec_list:
            done = 0
            while done < n:
                blk, off = divmod(r0 + done, 128)
                take = min(n - done, 128 - off)
                dma("sp", stage[off:off + take, blk, :], ap2d[done:done + take, :], "cld", [], [("stage", blk)])
                done += take
            r0 += n
        for blk in range(nblk):
            pi = psn()
            tr(pi, slice(0, 128), stage[:, blk, :], [("stage", blk)])
            cp("dve", cst[:, blk * 128:(blk + 1) * 128], ps[pi][:, 0:128], [("ps", pi)], ["cst"])
        for l in range(2):
            lam = cst[:, crow["rg_lam"] + l * KC: crow["rg_lam"] + (l + 1) * KC]
            actf(rgc[:, l, 0, :], lam, AF.Exp, ["cst"], ["rgc"], scale=-1.0)
            actf(rgc[:, l, 0, :], rgc[:, l, 0, :], AF.Ln, ["rgc"], ["rgc"], bias=1.0)
            ts("dve", rgc[:, l, 1, :], rgc[:, l, 0, :], -16.0, None, ALU.mult, None, ["rgc"], ["rgc"])
            ts("dve", rgc[:, l, 0, :], rgc[:, l, 0, :], -8.0, None, ALU.mult, None, ["rgc"], ["rgc"])

        def phase_load(t):
            dma("sp", xin, x_d[t * T:(t + 1) * T, :].rearrange("(j p) d -> p j d", p=128), "xld",
                [], [("big16", i) for i in range(8)])
            for k in range(KC):
                pi = psn()
                for j in range(4):
                    tr(pi, slice(128 * j, 128 * j + 128), xin[:, j, 128 * k:128 * k + 128],
                       [("big16", 2 * j), ("big16", 2 * j + 1)], last=(j == 3))
                cp("act" if k % 2 else "dve", h[:, k, :], ps[pi][:, :], [("ps", pi)], [("h", k)])

        def phase_norm(gname, gidx0, out_f32=None):
            for k in range(KC):
                if k % 2 == 0:
                    actf(sq[:, k, :], h[:, k, :], AF.Square, [("h", k)], [("sq", k)])
                else:
                    tt("dve", sq[:, k, :], h[:, k, :], h[:, k, :], ALU.mult, [("h", k)], [("sq", k)])
            pi = psn()
            for k in range(KC):
                mm(pi, onesb[:, :], sq[:, k, :], k == 0, k == KC - 1, ["onesb", ("sq", k)])
            actf(rstd[:, :], ps[pi][:, :], AF.Sqrt, [("ps", pi), "eps"], ["rstd"], bias=EPS_AP[:, :], scale=1.0 / D)
            P.op("dve", lambda e: e.reciprocal(out=rstd[:, :], in_=rstd[:, :]), reads=["rstd"], writes=["rstd"])
            for k in range(KC):
                if out_f32 is None:
                    stt(hn[:, k, :], h[:, k, :], C1(gname, gidx0 + k), rstd[:, :], ALU.mult, ALU.mult,
                        [("h", k), "rstd", "cst"], [("hn", k)])
                else:
                    o, key = out_f32(k)
                    stt(o, h[:, k, :], C1(gname, gidx0 + k), rstd[:, :], ALU.mult, ALU.mult,
                        [("h", k), "rstd", "cst"], [key])

        def phase_ffn(i):
            cw0 = crow["ffn_cw"] + i * 3 * 48
            cb0 = crow["ffn_cb"] + i * 48
            slabs = {}
            st = {}

            def get_slab(si):
                if si not in slabs:
                    slabs[si] = next_slab()
                return slabs[si]

            def stage_a(hx):
                jg, half = divmod(hx, 2)
                si, cc = divmod(jg, 4)
                slot, gsl = get_slab(si)
                wv = ring_view(slot, KC, 1024)
                ch = jg + NJ * half
                pi = psn()
                for k in range(KC):
                    mm(pi, wv[:, k, 512 * half + 128 * cc: 512 * half + 128 * cc + 128], hn[:, k, :],
                       k == 0, k == KC - 1, [("ring", slot), ("hn", k)])
                if hx % 8 == 7:
                    release(gsl)
                ub_i = upb_i[0]
                upb_i[0] = (ub_i + 1) % NUPB
                cp("pool", upb[:, ub_i, 0:2], halo_ffn[:, i, ch, :], [("halo_ffn", i, ch)], [("upbh", ub_i)])
                cp("act", upb[:, ub_i, 2:2 + T], ps[pi][:, :], [("ps", pi)], [("upb", ub_i)])
                cp("pool", halo_ffn[:, i, ch, :], upb[:, ub_i, T:T + 2], [("upb", ub_i)], [("halo_ffn", i, ch)])
                for t3 in range(3):
                    actf(dgs[:, ub_i, t3, :], identb[:, :], AF.Copy, ["identb", "cst"], [("dgs", ub_i, t3)],
                         scale=cst[:, cw0 + 48 * t3 + ch:cw0 + 48 * t3 + ch + 1])
                st[hx] = ub_i

            def stage_b(hx):
                jg, half = divmod(hx, 2)
                ub_i = st.pop(hx)
                pc = psn()
                for t3 in range(3):
                    mm(pc, dgs[:, ub_i, t3, :], upb[:, ub_i, t3:t3 + T], t3 == 0, t3 == 2,
                       [("dgs", ub_i, t3), ("upb", ub_i), ("upbh", ub_i)])
                if half == 0:
                    g = tslot()
                    actf(TM(g), ps[pc][:, :], AF.Gelu_apprx_tanh, [("ps", pc), "cst"], [("tmp", g)],
                         bias=cst[:, cb0 + jg:cb0 + jg + 1])
                    st[("g", jg)] = g
                else:
                    g = st.pop(("g", jg))
                    stt(act[:, jg, :], ps[pc][:, :], cst[:, cb0 + NJ + jg:cb0 + NJ + jg + 1], TM(g), ALU.add, ALU.mult,
                        [("ps", pc), ("tmp", g), "cst"], [("act", jg)])

            NH = 2 * NJ
            LOOK = FFN_LOOK
            for hx in range(min(LOOK, NH)):
                stage_a(hx)
            for hx in range(NH):
                if hx + LOOK < NH:
                    stage_a(hx + LOOK)
                stage_b(hx)
            for s in range(4):
                slot, gsl = next_slab()
                wv = ring_view(slot, NJ, 256)
                for mh in range(2):
                    m = 2 * s + mh
                    pi = psn()
                    for j in range(NJ):
                        mm(pi, wv[:, j, 128 * mh:128 * mh + 128], act[:, j, :], j == 0, j == NJ - 1,
                           [("ring", slot), ("act", j)])
                    tt("dve", h[:, m, :], h[:, m, :], ps[pi][:, :], ALU.add, [("h", m), ("ps", pi)], [("h", m)])
                release(gsl)

        def ffn_slabs(i):
            for s in range(6):
                slab_list.append(slab_cols("ffn_w_up", i, KC, [(512 * s, 512), (DFF + 512 * s, 512)]))
            for s in range(4):
                slab_list.append(slab_cols("ffn_w_down", i, NJ, [(256 * s, 256)]))

        def phase_rg(l):
            slot_g, g_g = next_slab()
            wg = ring_view(slot_g, KC, 1024)
            for c in range(KC):
                pg = psn()
                for k in range(KC):
                    mm(pg, wg[:, k, 128 * c:128 * c + 128], hn[:, k, :], k == 0, k == KC - 1, [("ring", slot_g), ("hn", k)])
                actf(act[:, 8 + c, :], ps[pg][:, :], AF.Gelu_apprx_tanh, [("ps", pg)], [("act", 8 + c)])
            release(g_g)
            slot_x, g_x = next_slab()
            wx = ring_view(slot_x, KC, 1024)
            slot_a, g_a = next_slab()
            wa = ring_view(slot_a, 16, 128)
            RG_BATCH = 4

            def rg_a(c):
                pi = psn()
                for k in range(KC):
                    mm(pi, wx[:, k, 128 * c:128 * c + 128], hn[:, k, :], k == 0, k == KC - 1, [("ring", slot_x), ("hn", k)])
                u = 12 + (c % 4)
                cp("pool", TM(u, 0, 3), halo_rg[:, l, c, :], [("halo_rg", l, c)], [("tmp", u)])
                cp("act", TM(u, 3, T), ps[pi][:, :], [("ps", pi)], [("tmp", u)])
                cp("pool", halo_rg[:, l, c, :], TM(u, T, 3), [("tmp", u)], [("halo_rg", l, c)])
                return u

            def rg_conv(c, u):
                xc = u32[:, c, :]
                XK = [("big16", c)]
                ts("dve", xc, TM(u, 0, T), C1("rg_cw", l * 32 + 0 * KC + c), C1("rg_cb", l * KC + c),
                   ALU.mult, ALU.add, [("tmp", u), "cst"], XK)
                for k in range(1, 4):
                    stt(xc, TM(u, k, T), C1("rg_cw", l * 32 + k * KC + c), xc, ALU.mult, ALU.add,
                        [("tmp", u), "cst"] + XK, XK)
                cp("pool", sq[:, c, :], xc, XK, [("sq", c)])

            us = {0: rg_a(0)}
            for c in range(KC):
                if c + 1 < KC:
                    us[c + 1] = rg_a(c + 1)
                rg_conv(c, us.pop(c))
            release(g_x)

            for c0 in range(0, KC, RG_BATCH):
                cs = list(range(c0, c0 + RG_BATCH))
                sl = {c: (c % 4, 4 + c % 4, 8 + c % 4) for c in cs}
                for c in cs:
                    r, ig, a = sl[c]
                    pa = psn()
                    mm(pa, wa[:, c, :], sq[:, c, :], True, True, [("ring", slot_a), ("sq", c)])
                    px = psn()
                    mm(px, wa[:, 8 + c, :], sq[:, c, :], True, True, [("ring", slot_a), ("sq", c)])
                    actf(TM(r), ps[pa][:, :], AF.Sigmoid, [("ps", pa), "cst"], [("tmp", r)], bias=C1("rg_ba", l * KC + c))
                    actf(TM(ig), ps[px][:, :], AF.Sigmoid, [("ps", px), "cst"], [("tmp", ig)], bias=C1("rg_bx", l * KC + c))
                for c in cs:
                    r, ig, a = sl[c]
                    actf(TM(a), TM(r), AF.Exp, [("tmp", r), "rgc"], [("tmp", a)], scale=rgc[:, l, 0, c:c + 1])
                    actf(TM(r), TM(r), AF.Exp, [("tmp", r), "rgc"], [("tmp", r)], scale=rgc[:, l, 1, c:c + 1])
                for c in cs:
                    r, ig, a = sl[c]
                    actf(TM(r), TM(r), AF.Sqrt, [("tmp", r), "one"], [("tmp", r)], bias=ONE_AP[:, :], scale=-1.0)
                for c in cs:
                    r, ig, a = sl[c]
                    tt("dve", TM(ig), TM(ig), u32[:, c, :], ALU.mult, [("tmp", ig), ("big16", c)], [("tmp", ig)])
                    tt("dve", TM(ig), TM(ig), TM(r), ALU.mult, [("tmp", ig), ("tmp", r)], [("tmp", ig)])
                    hs = 12 + (c % 4)
                    P.op("dve", lambda e, hs=hs, a=a, ig=ig, c=c: e.tensor_tensor_scan(
                        out=TM(hs), data0=TM(a), data1=TM(ig), initial=carry_rg[:, l, c:c + 1],
                        op0=ALU.mult, op1=ALU.add),
                        reads=[("tmp", a), ("tmp", ig), ("carry_rg", l, c)], writes=[("tmp", hs)])
                    cp("dve", carry_rg[:, l, c:c + 1], TM(hs, T - 1, 1), [("tmp", hs)], [("carry_rg", l, c)])
                    tt("dve", act[:, c, :], TM(hs), act[:, 8 + c, :], ALU.mult, [("tmp", hs), ("act", 8 + c)], [("act", c)])
            tmp_i[0] = 0
            release(g_a)
            slot_o, g_o = next_slab()
            wo = ring_view(slot_o, KC, 1024)
            for m in range(KC):
                pi = psn()
                for c in range(KC):
                    mm(pi, wo[:, c, 128 * m:128 * m + 128], act[:, c, :], c == 0, c == KC - 1, [("ring", slot_o), ("act", c)])
                tt("dve", h[:, m, :], h[:, m, :], ps[pi][:, :], ALU.add, [("h", m), ("ps", pi)], [("h", m)])
            release(g_o)

        def rg_slabs(l):
            slab_list.append(slab_cols("rg_w_in", l, KC, [(1024, 1024)]))
            slab_list.append(slab_cols("rg_w_in", l, KC, [(0, 1024)]))

            def fn(dst2d, l=l):
                dst = dst2d[:, 0:2048].rearrange("p (k n) -> p k n", k=16)
                return [(dst[:, 0:8, :], W["rg_w_a"][l].rearrange("h i j -> i h j")),
                        (dst[:, 8:16, :], W["rg_w_x"][l].rearrange("h i j -> i h j"))], "pool"
            slab_list.append(SlabDef(fn, size=2048))
            slab_list.append(slab_cols("rg_w_out", l, KC, [(0, 1024)]))


        I32 = mybir.dt.int32
        TWO_PI = 2.0 * math.pi

        def K5(j, idx, q0=0, n=NQ):
            return s5k[:, j, idx, q0:q0 + n]

        def range_reduce(eng_out_ap, z_ap, zi_ap, kf_ap, width_keys):
            C1_ = 6.28125
            C2_ = TWO_PI - C1_
            PI_LO = 3.1415925
            ts("dve", zi_ap, z_ap, 1.0 / TWO_PI, None, ALU.mult, None, width_keys, width_keys)
            cp("dve", kf_ap, zi_ap, width_keys, width_keys)
            stt(eng_out_ap, kf_ap, -C1_, z_ap, ALU.mult, ALU.add, width_keys, width_keys)
            stt(eng_out_ap, kf_ap, -C2_, eng_out_ap, ALU.mult, ALU.add, width_keys, width_keys)
            ts("dve", eng_out_ap, eng_out_ap, -PI_LO, PI_LO, ALU.max, ALU.min, width_keys, width_keys)

        def s5_prologue(j):
            PK = ["s5p"]
            cp("dve", K5(j, 0), cst[:, crow["s5_are"] + NQ * j:crow["s5_are"] + NQ * (j + 1)], ["cst"], PK)
            cp("dve", K5(j, 1), cst[:, crow["s5_aim"] + NQ * j:crow["s5_aim"] + NQ * (j + 1)], ["cst"], PK)
            dma("sp", ldrow[0:1, 0:64], W["s5_log_dt"][j:j + 1, :], "cld", [], PK)
            for q0 in range(0, NQ, 4):
                dma("sp", s5nat[:, 0, q0:q0 + 4, :], W["s5_b_re"][j].rearrange("(q p) c -> p q c", p=128)[:, q0:q0 + 4, :], "cld", [], PK)
                dma("sp", s5nat[:, 1, q0:q0 + 4, :], W["s5_b_im"][j].rearrange("(q p) c -> p q c", p=128)[:, q0:q0 + 4, :], "cld", [], PK)
            dma("sp", s5cn[:, 0, :, :], W["s5_c_re"][j].rearrange("(k r) n -> r k n", r=128), "cld", [], PK)
            dma("sp", s5cn[:, 1, :, :], W["s5_c_im"][j].rearrange("(k r) n -> r k n", r=128), "cld", [], PK)
            pi = psn()
            P.op("pe", lambda e: e.matmul(ps[pi][:, 0:64], lhsT=onesf[0:1, :], rhs=ldrow[0:1, 0:64], start=True, stop=True),
                 reads=PK + ["onesf"], writes=[("ps", pi)])
            pv = ps[pi][:, 0:64].rearrange("p (q g) -> p q g", g=2)
            cp("dve", s5k[0:64, j, 2, :], pv[0:64, :, 0], [("ps", pi)], PK)
            cp("dve", s5k[64:128, j, 2, :], pv[64:128, :, 1], [("ps", pi)], PK)
            actf(K5(j, 2), K5(j, 2), AF.Exp, PK, PK)
            tt("dve", K5(j, 12), K5(j, 0), K5(j, 2), ALU.mult, PK, PK)
            actf(K5(j, 3), K5(j, 12), AF.Exp, PK, PK)
            tt("dve", K5(j, 4), K5(j, 1), K5(j, 2), ALU.mult, PK, PK)
            zi = s5k[:, j, 13, :].bitcast(I32)
            range_reduce(K5(j, 12), K5(j, 4), zi, K5(j, 14), PK)
            actf(K5(j, 6), K5(j, 12), AF.Sin, PK, PK)
            ts("dve", K5(j, 15), K5(j, 4), math.pi / 2, None, ALU.add, None, PK, PK)
            range_reduce(K5(j, 12), K5(j, 15), zi, K5(j, 14), PK)
            actf(K5(j, 5), K5(j, 12), AF.Sin, PK, PK)
            tt("dve", K5(j, 5), K5(j, 5), K5(j, 3), ALU.mult, PK, PK)
            tt("dve", K5(j, 6), K5(j, 6), K5(j, 3), ALU.mult, PK, PK)
            tt("dve", K5(j, 12), K5(j, 0), K5(j, 0), ALU.mult, PK, PK)
            tt("dve", K5(j, 13), K5(j, 1), K5(j, 1), ALU.mult, PK, PK)
            tt("dve", K5(j, 9), K5(j, 12), K5(j, 13), ALU.add, PK, PK)
            P.op("dve", lambda e: e.reciprocal(out=K5(j, 9), in_=K5(j, 9)), reads=PK, writes=PK)
            ts("dve", K5(j, 10), K5(j, 5), -1.0, None, ALU.add, None, PK, PK)
            tt("dve", K5(j, 12), K5(j, 10), K5(j, 0), ALU.mult, PK, PK)
            tt("dve", K5(j, 13), K5(j, 6), K5(j, 1), ALU.mult, PK, PK)
            tt("dve", K5(j, 12), K5(j, 12), K5(j, 13), ALU.add, PK, PK)
            tt("dve", K5(j, 7), K5(j, 12), K5(j, 9), ALU.mult, PK, PK)
            tt("dve", K5(j, 12), K5(j, 6), K5(j, 0), ALU.mult, PK, PK)
            tt("dve", K5(j, 13), K5(j, 10), K5(j, 1), ALU.mult, PK, PK)
            tt("dve", K5(j, 12), K5(j, 12), K5(j, 13), ALU.subtract, PK, PK)
            tt("dve", K5(j, 8), K5(j, 12), K5(j, 9), ALU.mult, PK, PK)
            ts("dve", K5(j, 10), K5(j, 8), -1.0, None, ALU.mult, None, PK, PK)
            hflat = h[:, :, :].rearrange("p k t -> p (k t)")
            za = big16[:, 0:T]
            zb = big16[:, T:2 * T]
            zc = hflat[:, 0:T]
            zd = hflat[:, T:2 * T]
            ZK = ["zq"]
            for q in range(NQ):
                b_ = q % 2
                stg = tmp[:, 4256 + b_ * 2 * T:4256 + (b_ + 1) * 2 * T].rearrange("p (a t) -> p a t", a=2)
                SK = [("tabstg", b_)]
                ts("dve", za, tau[:, :], s5k[:, j, 4, q:q + 1], None, ALU.mult, None, PK + ZK + ["tau"], ZK)
                range_reduce(zd, za, zb.bitcast(I32), zc, ZK)
                actf(stg[:, 1, :], zd, AF.Sin, ZK, SK)
                ts("dve", za, za, math.pi / 2, None, ALU.add, None, ZK, ZK)
                range_reduce(zd, za, zb.bitcast(I32), zc, ZK)
                actf(stg[:, 0, :], zd, AF.Sin, ZK, SK)
                dma("sp", s5tab_d[j][q].rearrange("p (a t) -> p a t", a=2), stg, f"tbw{b_}", SK, [("s5tab", j)])
            if s5_stage < 1:
                return
            ta = tmp[:, 4224:4240]
            tb = tmp[:, 4240:4256]
            for q in range(NQ):
                k, r4 = divmod(q, 4)
                br = s5nat[:, 0, q, :]
                bi = s5nat[:, 1, q, :]
                ts("dve", ta, br, s5k[:, j, 7, q:q + 1], None, ALU.mult, None, PK, PK)
                stt(ta, bi, s5k[:, j, 10, q:q + 1], ta, ALU.mult, ALU.add, PK, PK)
                ts("dve", tb, bi, s5k[:, j, 7, q:q + 1], None, ALU.mult, None, PK, PK)
                stt(tb, br, s5k[:, j, 8, q:q + 1], tb, ALU.mult, ALU.add, PK, PK)
                FK = ["s5full"]
                memset("dve", s5full[:, :, :], 0.0, FK)
                for a_, src in ((0, ta), (1, tb)):
                    for gl in range(2):
                        ts("dve", s5full[:, a_, 32 * r4 + 16 * gl:32 * r4 + 16 * gl + 16], src, masks[:, 16 + gl:17 + gl], None,
                           ALU.mult, None, PK + FK + ["masks"], FK)
                pi = psn()
                tr(pi, slice(0, 128), s5full[:, 0, :], FK, last=False)
                tr(pi, slice(128, 256), s5full[:, 1, :], FK, last=True)
                cp("act", s5bcs[:, 0:2, :], ps[pi][:, 0:256].rearrange("p (a n) -> p a n", a=2), [("ps", pi)], ["s5bcs"])
                for a_ in range(2):
                    for gl in range(2):
                        mcol = (8 if a_ else 0) + 2 * r4 + gl
                        ts("dve", s5full[:, a_, 64 * gl:64 * gl + 64], s5cn[:, a_, k, :], masks[:, mcol:mcol + 1], None,
                           ALU.mult, None, PK + FK + ["masks"], FK)
                pi = psn()
                tr(pi, slice(0, 128), s5full[:, 0, :], FK, last=False)
                tr(pi, slice(128, 256), s5full[:, 1, :], FK, last=True)
                cp("act", s5bcs[:, 2:4, :], ps[pi][:, 0:256].rearrange("p (a n) -> p a n", a=2), [("ps", pi)], ["s5bcs"])
                dma("sp", s5bc_d[j][:, :, q * 128:(q + 1) * 128].rearrange("a p n -> p a n"), s5bcs[:, :, :], "scr",
                    ["s5bcs"], [("s5bc", j)])

        def s5_slabs(j):
            slab_list.append(slab_cols("s5_w_in", j, KC, [(0, 1024)]))
            for half in range(2):
                def fn(dst2d, j=j, half=half):
                    dst = dst2d[:, 0:8192].rearrange("p (k n) -> p k n", k=64)
                    src = s5bc_d[j][2 * half:2 * half + 2, :, :].rearrange("a p (q n) -> p a q n", q=NQ)
                    return [(dst[:, 0:32, :], src[:, 0, :, :]), (dst[:, 32:64, :], src[:, 1, :, :])], "pool"
                slab_list.append(SlabDef(fn, reads=[("s5bc", j)], direct=True))
            for i2 in range(2):
                slab_list.append(slab_cols("s5_w_glu", j, KC, [(512 * i2, 512), (1024 + 512 * i2, 512)]))
            slab_list.append(slab_cols("s5_w_out", j, KC, [(0, 1024)]))

        def phase_s5(j):
            slot_i, g_i = next_slab()
            wi_ = ring_view(slot_i, KC, 1024)
            for c in range(KC):
                pi = psn()
                for k in range(KC):
                    mm(pi, wi_[:, k, 128 * c:128 * c + 128], hn[:, k, :], k == 0, k == KC - 1, [("ring", slot_i), ("hn", k)])
                cp("act", u32[:, c, :], ps[pi][:, :], [("ps", pi)], [("big16", c)])
                cp("dve", sq[:, c, :], u32[:, c, :], [("big16", c)], [("sq", c)])
            release(g_i)
            slot_b, g_b = next_slab()
            wB = ring_view(slot_b, 64, 128)
            slot_c, g_c = next_slab()
            wC = ring_view(slot_c, 64, 128)
            if s5_stage in (2, 20, 21):
                pi = psn()
                mm(pi, wB[:, 0, :], sq[:, 0, :], True, True, [("ring", slot_b), ("sq", 0)])
                pi = psn()
                mm(pi, wC[:, 0, :], sq[:, 0, :], True, True, [("ring", slot_c), ("sq", 0)])
                release(g_b)
                release(g_c)
                for _ in range(3):
                    sl_, g_ = next_slab()
                    pi = psn()
                    mm(pi, ring[:, sl_, 0:128], sq[:, 0, :], True, True, [("ring", sl_), ("sq", 0)])
                    release(g_)
                return
            ps_mod[0] = 6
            ps_i[0] = 0
            qst = {}
            free = list(range(NTMP))

            def talloc():
                assert free, "S5 tmp slots exhausted"
                return free.pop(0)

            def tfree(*ids):
                free.extend(ids)

            def load_tab(q):
                slot = q % 3
                dma("sp", tabr[:, slot, :, :], s5tab_d[j][q].rearrange("p (a t) -> p a t", a=2), f"tb{slot}",
                    [("s5tab", j)], [("tabr", slot)])

            def s5_a(q):
                c = q // 4
                sl = q % 3
                Cq = tabr[:, sl, 0, :]
                Sq = tabr[:, sl, 1, :]
                TK = [("tabr", sl)]
                pbr = psn()
                mm(pbr, wB[:, q, :], sq[:, c, :], True, True, [("ring", slot_b), ("sq", c)])
                pbi = psn()
                mm(pbi, wB[:, 32 + q, :], sq[:, c, :], True, True, [("ring", slot_b), ("sq", c)])
                br_, bi_, t_, mr, u_, v_ = [talloc() for _ in range(6)]
                cp("act", TM(br_), ps[pbr][:, :], [("ps", pbr)], [("tmp", br_)])
                cp("act", TM(bi_), ps[pbi][:, :], [("ps", pbi)], [("tmp", bi_)])
                tt("dve", TM(t_), TM(br_), Cq, ALU.mult, [("tmp", br_)] + TK, [("tmp", t_)])
                tt("dve", TM(mr), TM(bi_), Sq, ALU.mult, [("tmp", bi_)] + TK, [("tmp", mr)])
                tt("dve", TM(mr), TM(mr), TM(t_), ALU.add, [("tmp", mr), ("tmp", t_)], [("tmp", mr)])
                tt(S5_ENG2, TM(u_), TM(bi_), Cq, ALU.mult, [("tmp", bi_)] + TK, [("tmp", u_)])
                tt(S5_ENG2, TM(v_), TM(br_), Sq, ALU.mult, [("tmp", br_)] + TK, [("tmp", v_)])
                tt(S5_ENG2, TM(u_), TM(u_), TM(v_), ALU.subtract, [("tmp", u_), ("tmp", v_)], [("tmp", u_)])
                qst[q] = (br_, bi_, t_, mr, u_, v_)

            def s5_b(q):
                c, qq = divmod(q, 4)
                py = 6 + (c % 2)
                sl = q % 3
                Cq = tabr[:, sl, 0, :]
                Sq = tabr[:, sl, 1, :]
                Cend = tabr[:, sl, 0, T - 1:T]
                Send = tabr[:, sl, 1, T - 1:T]
                TK = [("tabr", sl)]
                br_, bi_, t_, mr, mi, v_ = qst.pop(q)
                rho = s5k[:, j, 3, q:q + 1].to_broadcast([128, T])
                gr, gi = talloc(), talloc()
                cr = s5carry[:, j, 0, q:q + 1]
                ci = s5carry[:, j, 1, q:q + 1]
                CK = [("s5carry", q)]
                P.op("dve", lambda e, gr=gr, mr=mr, cr=cr, rho=rho: e.tensor_tensor_scan(
                    out=TM(gr), data0=rho, data1=TM(mr), initial=cr, op0=ALU.mult, op1=ALU.add),
                    reads=[("tmp", mr), "s5p"] + CK, writes=[("tmp", gr)])
                P.op("dve", lambda e, gi=gi, mi=mi, ci=ci, rho=rho: e.tensor_tensor_scan(
                    out=TM(gi), data0=rho, data1=TM(mi), initial=ci, op0=ALU.mult, op1=ALU.add),
                    reads=[("tmp", mi), "s5p"] + CK, writes=[("tmp", gi)])
                gre = TM(gr, T - 1, 1)
                gie = TM(gi, T - 1, 1)
                tA = s5k[:, j, 14, q:q + 1]
                tB = s5k[:, j, 15, q:q + 1]
                actf(tA, gie, AF.Copy, [("tmp", gi)] + TK, [("s5tA", q)], scale=Send)
                actf(tB, gre, AF.Copy, [("tmp", gr)] + TK, [("s5tB", q)], scale=Send)
                actf(ci, gie, AF.Identity, [("tmp", gi), ("s5tB", q)] + TK, CK, bias=tB, scale=Cend)
                actf(tB, gre, AF.Copy, [("tmp", gr)] + TK, [("s5tB", q)], scale=Cend)
                actf(cr, tA, AF.Identity, [("s5tA", q), ("s5tB", q)], CK, bias=tB, scale=-1.0)
                hb = q % 2
                tt("dve", hrb[:, 2 * hb, :], TM(gr), Cq, ALU.mult, [("tmp", gr)] + TK, [("hrb", 2 * hb)])
                stt(hrb[:, 2 * hb + 1, :], TM(gi), -1.0, Sq, ALU.mult, ALU.mult, [("tmp", gi)] + TK, [("hrb", 2 * hb + 1)])
                tt(S5_ENG2, hib[:, 2 * hb, :], TM(gr), Sq, ALU.mult, [("tmp", gr)] + TK, [("hib", 2 * hb)])
                tt(S5_ENG2, hib[:, 2 * hb + 1, :], TM(gi), Cq, ALU.mult, [("tmp", gi)] + TK, [("hib", 2 * hb + 1)])
                mm(py, wC[:, q, :], hrb[:, 2 * hb, :], qq == 0, False, [("ring", slot_c), ("hrb", 2 * hb)])
                mm(py, wC[:, q, :], hrb[:, 2 * hb + 1, :], False, False, [("ring", slot_c), ("hrb", 2 * hb + 1)])
                mm(py, wC[:, 32 + q, :], hib[:, 2 * hb, :], False, False, [("ring", slot_c), ("hib", 2 * hb)])
                mm(py, wC[:, 32 + q, :], hib[:, 2 * hb + 1, :], False, qq == 3, [("ring", slot_c), ("hib", 2 * hb + 1)])
                tfree(br_, bi_, t_, mr, mi, v_, gr, gi)
                if qq == 3:
                    yy = talloc()
                    tfree(yy)
                    cp("act", TM(yy), ps[py][:, :], [("ps", py)], [("tmp", yy)])
                    stt(TM(yy), u32[:, c, :], C1("s5_d", j * KC + c), TM(yy), ALU.mult, ALU.add,
                        [("big16", c), ("tmp", yy), "cst"], [("tmp", yy)])
                    actf(act[:, 8 + c, :], TM(yy), AF.Gelu_apprx_tanh, [("tmp", yy)], [("act", 8 + c)])

            load_tab(0)
            load_tab(1)
            if S5_PIPE:
                s5_a(0)
                for q in range(NQ):
                    if q + 2 < NQ:
                        load_tab(q + 2)
                    ra = P.record(lambda: s5_a(q + 1)) if q + 1 < NQ else []
                    rb = P.record(lambda: s5_b(q))
                    P.emit([o for o in ra if o[0] != "dve"])
                    ad = [o for o in ra if o[0] == "dve"]
                    ai = 0
                    for o in rb:
                        P.emit([o])
                        if o[0] == "dve" and ai < len(ad):
                            P.emit([ad[ai]])
                            ai += 1
                    P.emit(ad[ai:])
            else:
                for q in range(NQ):
                    if q + 2 < NQ:
                        load_tab(q + 2)
                    s5_a(q)
                    s5_b(q)
            ps_mod[0] = 8
            ps_i[0] = 0
            tmp_i[0] = 0
            release(g_b)
            release(g_c)
            for i2 in range(2):
                slot_g, g_g = next_slab()
                wg = ring_view(slot_g, KC, 1024)
                for m4 in range(4):
                    m = 4 * i2 + m4
                    pv = psn()
                    for k in range(KC):
                        mm(pv, wg[:, k, 128 * m4:128 * m4 + 128], act[:, 8 + k, :], k == 0, k == KC - 1,
                           [("ring", slot_g), ("act", 8 + k)])
                    pg = psn()
                    for k in range(KC):
                        mm(pg, wg[:, k, 512 + 128 * m4:512 + 128 * m4 + 128], act[:, 8 + k, :], k == 0, k == KC - 1,
                           [("ring", slot_g), ("act", 8 + k)])
                    sg = tslot()
                    actf(TM(sg), ps[pg][:, :], AF.Sigmoid, [("ps", pg)], [("tmp", sg)])
                    tt("dve", act[:, 16 + m, :], ps[pv][:, :], TM(sg), ALU.mult, [("ps", pv), ("tmp", sg)], [("act", 16 + m)])
                release(g_g)
            slot_o, g_o = next_slab()
            wo = ring_view(slot_o, KC, 1024)
            for m in range(KC):
                pi = psn()
                for k in range(KC):
                    mm(pi, wo[:, k, 128 * m:128 * m + 128], act[:, 16 + k, :], k == 0, k == KC - 1,
                       [("ring", slot_o), ("act", 16 + k)])
                tt("dve", h[:, m, :], h[:, m, :], ps[pi][:, :], ALU.add, [("h", m), ("ps", pi)], [("h", m)])
            release(g_o)

        def phase_store(t):
            fs = [tslot() for _ in range(KC)]

            def o(k):
                return TM(fs[k]), ("tmp", fs[k])
            phase_norm("g_fin", 0, out_f32=o)
            for j in range(4):
                for hf in range(2):
                    pi = psn()
                    for kk in range(4):
                        k = 4 * hf + kk
                        tr(pi, slice(128 * kk, 128 * kk + 128), TM(fs[k], 128 * j, 128), [("tmp", fs[k])], last=(kk == 3))
                    cp("act" if hf else "dve", xin[:, j, 512 * hf:512 * hf + 512], ps[pi][:, :], [("ps", pi)],
                       [("big16", 2 * j + hf)])
            dma("sp", out_d[t * T:(t + 1) * T, :].rearrange("(j p) d -> p j d", p=128), xin, "ost",
                [("big16", i) for i in range(8)], [("outd", t)])

        EPS_AP = sb("eps_ap", [128, 1])
        ONE_AP = sb("one_ap", [128, 1])
        memset("dve", EPS_AP[:, :], EPS, ["eps"])
        memset("dve", ONE_AP[:, :], 1.0, ["one"])

        memset("dve", onesf[:, :], 1.0, ["onesf"])
        for gi_ in range(8):
            P.op("dve", lambda e, gi_=gi_: e.reduce_sum(out=masks[:, gi_:gi_ + 1], in_=ident[:, 16 * gi_:16 * gi_ + 16],
                                                        axis=mybir.AxisListType.X), reads=["ident"], writes=["masks"])
        for hf_ in range(2):
            P.op("dve", lambda e, hf_=hf_: e.reduce_sum(out=masks[:, 16 + hf_:17 + hf_], in_=ident[:, 64 * hf_:64 * hf_ + 64],
                                                        axis=mybir.AxisListType.X), reads=["ident"], writes=["masks"])
        ts("dve", masks[:, 8:16], masks[:, 0:8], -1.0, None, ALU.mult, None, ["masks"], ["masks"])
        if "s" in mix:
            for j_ in sorted({i // 2 for i in layers if i % 2 == 1}):
                if s5_stage != 21:
                    s5_prologue(j_)

        for i in layers:
            if i % 2 == 0 and "r" in mix:
                rg_slabs(i // 2)
            if i % 2 == 1 and "s" in mix and s5_stage >= 2:
                s5_slabs(i // 2)
            if "f" in mix:
                ffn_slabs(i)

        if USE_WSC:
            cast_all_slabs()
        P.barrier()
        for t in range(n_tiles):
            cur[0] = t
            phase_load(t)
            for i in layers:
                if i % 2 == 0 and "r" in mix:
                    phase_norm("g_mix", i * KC)
                    phase_rg(i // 2)
                if i % 2 == 1 and "s" in mix and s5_stage >= 2:
                    phase_norm("g_mix", i * KC)
                    phase_s5(i // 2)
                if "f" in mix:
                    phase_norm("g_ffn", i * KC)
                    phase_ffn(i)
            phase_store(t)
        P.final_wait("sp", [("outd", t) for t in range(n_tiles)] + [("dbg", i) for i in range(dbg_n[0])])
        P.dbg_names = dbg_names

        with nc.Block() as block:
            @block.sync
            def _(e):
                P.replay("sp", e, sems)

            @block.gpsimd
            def _(e):
                P.replay("pool", e, sems)

            @block.scalar
            def _(e):
                P.replay("act", e, sems)

            @block.vector
            def _(e):
                P.replay("dve", e, sems)

            @block.tensor
            def _(e):
                P.replay("pe", e, sems)
    return nc, P


def make_consts():
    return {"ident": np.eye(128, dtype=np.float32),
            "tau": np.tile(np.arange(1, T + 1, dtype=np.float32)[None, :], (128, 1))}


def shape_inputs(inputs):
    r = {}
    for k, v in inputs.items():
        if k == "x":
            continue
        v = np.ascontiguousarray(v, dtype=np.float32)
        if k == "norm_final_g":
            v = v.reshape(1, D)
        elif k in ("rg_b_a", "rg_b_x"):
            v = v.reshape(2, D)
        elif k in ("s5_a_re", "s5_a_im"):
            v = v.reshape(2, 4096)
        elif k in ("s5_b_re", "s5_b_im"):
            v = v.reshape(2, 4096, 16)
        elif k in ("s5_c_re", "s5_c_im"):
            v = v.reshape(2, 1024, 64)
        r[k] = v
    return r


def kernel(**inputs):
    x = np.ascontiguousarray(inputs["x"], dtype=np.float32)
    w = shape_inputs(inputs)
    w.update(make_consts())
    nc, _ = build_program()
    in_maps = [dict(w, x=x[b]) for b in range(BATCH)]
    res = run_bass_kernel_spmd(nc, in_maps, core_ids=list(range(BATCH)))
    return np.stack([res.results[b]["out"] for b in range(BATCH)], axis=0)
```
